# Optimizing a Trainium2 kernel written in Bass

```python
import jax
import jax.numpy as jnp
from jax import lax
import numpy as np


D_MODEL = 1024
BATCH = 2
SEQ = 8192
DEPTH = 4
DEC_BATCH = 128
DEC_SEQ = 1
PAST_LEN = 8192
PAGE_SIZE = 128

N_MIXERS = 3
N_POOL = (DEPTH + 2) // N_MIXERS
N_ATTN = (DEPTH + 1) // N_MIXERS
N_CONV = DEPTH // N_MIXERS

POOL_SIZES = (2, 4, 8, 16)
POOL_GROUPS = len(POOL_SIZES)
POOL_CH = D_MODEL // POOL_GROUPS
POOL_HIST = max(POOL_SIZES) - 1

N_HEADS = 16
N_KV_HEADS = 4
HEAD_DIM = 64
GQA_GROUP = N_HEADS // N_KV_HEADS
WINDOW = 128
BAND_BLOCK = WINDOW
ROPE_THETA = 10000.0

CONV_WIDTH = 31
FFN_DIM = 2816
FFN_CONV_WIDTH = 3

N_MEM = 256
MEM_HEADS = 4
MEM_HEAD_DIM = D_MODEL // MEM_HEADS

EPS = 1e-6

kernel_name = 'hybrid_pool_swa_conformer_memxattn_step'


def rms(x, g):
    x32 = x.astype(jnp.float32)
    y = x32 * lax.rsqrt(jnp.mean(x32 * x32, -1, keepdims=True) + EPS)
    return (y * g.astype(jnp.float32)).astype(x.dtype)


def rope(x, pos):
    hd = x.shape[-1]
    half = hd // 2
    inv = ROPE_THETA ** (-jnp.arange(half, dtype=jnp.float32) * (2.0 / hd))
    ang = pos.astype(jnp.float32)[:, None] * inv[None, :]
    cos = jnp.cos(ang)[None, :, None, :]
    sin = jnp.sin(ang)[None, :, None, :]
    x32 = x.astype(jnp.float32)
    x1, x2 = x32[..., :half], x32[..., half:]
    return jnp.concatenate([x1 * cos - x2 * sin, x2 * cos + x1 * sin], -1).astype(x.dtype)


def dwconv(x_ext, w, b):
    y = lax.conv_general_dilated(x_ext, w[:, None, :], (1,), 'VALID',
                                 dimension_numbers=('NWC', 'WIO', 'NWC'),
                                 feature_group_count=x_ext.shape[-1])
    return y + b


def pool_mixer(h, prev, pos, w_grp, scale):
    n, t, _ = h.shape
    p = prev.shape[1]
    ext = jnp.concatenate([prev, h], 1)
    cs = jnp.concatenate([jnp.zeros((n, 1, D_MODEL), jnp.float32),
                          jnp.cumsum(ext.astype(jnp.float32), 1)], 1)
    end = cs[:, p + 1:]
    means = []
    for g, w in enumerate(POOL_SIZES):
        sl = slice(g * POOL_CH, (g + 1) * POOL_CH)
        cnt = jnp.minimum(pos + 1, w).astype(jnp.float32)[None, :, None]
        means.append((end[..., sl] - cs[:, p + 1 - w:p + 1 - w + t, sl]) / cnt)
    diff = (jnp.concatenate(means, -1) - h.astype(jnp.float32)).astype(h.dtype)
    diff = diff.reshape(n, t, POOL_GROUPS, POOL_CH)
    out = jnp.einsum('ntgc,gce->ntge', diff, w_grp).reshape(n, t, D_MODEL) * scale
    return out, ext[:, -POOL_HIST:]


def swa_project(h, pos, w_qkv, q_norm, k_norm):
    n, t, _ = h.shape
    qkv = h @ w_qkv
    qw, kw = N_HEADS * HEAD_DIM, N_KV_HEADS * HEAD_DIM
    q = qkv[..., :qw].reshape(n, t, N_HEADS, HEAD_DIM)
    k = qkv[..., qw:qw + kw].reshape(n, t, N_KV_HEADS, HEAD_DIM)
    v = qkv[..., qw + kw:].reshape(n, t, N_KV_HEADS, HEAD_DIM)
    return rope(rms(q, q_norm), pos), rope(rms(k, k_norm), pos), v


def sink_softmax(s, mask, sinks):
    sk = sinks.astype(jnp.float32).reshape(N_KV_HEADS, GQA_GROUP, 1, 1)
    s = jnp.where(mask, s, -jnp.inf)
    m = jnp.maximum(jnp.max(s, -1, keepdims=True), sk)
    pr = jnp.exp(s - m)
    return pr / (jnp.sum(pr, -1, keepdims=True) + jnp.exp(sk - m))


def swa_prompt(q, k, v, sinks):
    n, s, _, _ = q.shape
    nb = s // BAND_BLOCK
    qb = q.reshape(n, nb, BAND_BLOCK, N_KV_HEADS, GQA_GROUP, HEAD_DIM)
    kb = k.reshape(n, nb, BAND_BLOCK, N_KV_HEADS, HEAD_DIM)
    vb = v.reshape(n, nb, BAND_BLOCK, N_KV_HEADS, HEAD_DIM)
    kband = jnp.concatenate([jnp.concatenate([jnp.zeros_like(kb[:, :1]), kb[:, :-1]], 1), kb], 2)
    vband = jnp.concatenate([jnp.concatenate([jnp.zeros_like(vb[:, :1]), vb[:, :-1]], 1), vb], 2)
    sc = jnp.einsum('nbqkgd,nbskd->nbkgqs', qb, kband).astype(jnp.float32) * (HEAD_DIM ** -0.5)
    qi = jnp.arange(BAND_BLOCK)[:, None] + BAND_BLOCK
    si = jnp.arange(2 * BAND_BLOCK)[None, :]
    rel = qi - si
    blk = jnp.arange(nb)[:, None, None]
    mask = (rel >= 0) & (rel < WINDOW) & (blk * BAND_BLOCK - BAND_BLOCK + si >= 0)
    pr = sink_softmax(sc, mask[None, :, None, None], sinks)
    o = jnp.einsum('nbkgqs,nbskd->nbqkgd', pr.astype(v.dtype), vband)
    return o.reshape(n, s, N_HEADS * HEAD_DIM)


def swa_sample(q, k, v, buf_k, buf_v, pos, sinks):
    n, t, _, _ = q.shape
    wc = buf_k.shape[1]
    kall = jnp.concatenate([buf_k, k], 1)
    vall = jnp.concatenate([buf_v, v], 1)
    kpos = jnp.concatenate([pos[0] - wc + jnp.arange(wc, dtype=pos.dtype), pos])
    rel = pos[:, None] - kpos[None, :]
    mask = (rel >= 0) & (rel < WINDOW)
    qg = q.reshape(n, t, N_KV_HEADS, GQA_GROUP, HEAD_DIM)
    sc = jnp.einsum('ntkgd,nskd->nkgts', qg, kall).astype(jnp.float32) * (HEAD_DIM ** -0.5)
    pr = sink_softmax(sc, mask[None, None, None], sinks)
    o = jnp.einsum('nkgts,nskd->ntkgd', pr.astype(v.dtype), vall).reshape(n, t, N_HEADS * HEAD_DIM)
    return o, kall[:, -wc:], vall[:, -wc:]


def conv_module(h, prev, w1, b1, w_dw, b_dw, ln_g, ln_b, w2, b2):
    u = h @ w1 + b1
    glu = u[..., :D_MODEL] * jax.nn.sigmoid(u[..., D_MODEL:])
    ext = jnp.concatenate([prev, glu], 1)
    d = dwconv(ext, w_dw, b_dw).astype(jnp.float32)
    mu = jnp.mean(d, -1, keepdims=True)
    var = jnp.mean(jnp.square(d - mu), -1, keepdims=True)
    y = ((d - mu) * lax.rsqrt(var + EPS) * ln_g.astype(jnp.float32) + ln_b.astype(jnp.float32)).astype(h.dtype)
    return jax.nn.silu(y) @ w2 + b2, ext[:, -(CONV_WIDTH - 1):]


def mem_kv(mem, g_src, w_kv, k_norm):
    n, m, _ = mem.shape
    kv = rms(mem, g_src) @ w_kv
    mw = MEM_HEADS * MEM_HEAD_DIM
    k = rms(kv[..., :mw].reshape(n, m, MEM_HEADS, MEM_HEAD_DIM), k_norm)
    v = kv[..., mw:].reshape(n, m, MEM_HEADS, MEM_HEAD_DIM)
    return k, v


def mem_attend(h, k, v, w_q, q_norm, w_o):
    n, t, _ = h.shape
    q = rms((h @ w_q).reshape(n, t, MEM_HEADS, MEM_HEAD_DIM), q_norm)
    sc = jnp.einsum('ntkd,nmkd->nktm', q, k).astype(jnp.float32) * (MEM_HEAD_DIM ** -0.5)
    pr = jax.nn.softmax(sc, -1)
    o = jnp.einsum('nktm,nmkd->ntkd', pr.astype(v.dtype), v).reshape(n, t, MEM_HEADS * MEM_HEAD_DIM)
    return o @ w_o


def conv_ffn(h, prev, w_up, w_dw, b_dw, w_down):
    u = h @ w_up
    gate_pre, val = u[..., :FFN_DIM], u[..., FFN_DIM:]
    ext = jnp.concatenate([prev, gate_pre], 1)
    gc = dwconv(ext, w_dw, b_dw)
    return (jax.nn.silu(gc) * val) @ w_down, ext[:, -(FFN_CONV_WIDTH - 1):]


def setup_inputs(seed: int = 0) -> dict:
    key = jax.random.key(seed)
    keys = list(jax.random.split(key, 48))

    def nrm(shape, scale):
        return jax.random.normal(keys.pop(), shape, jnp.float32) * scale

    D = D_MODEL
    wc = min(WINDOW, PAST_LEN)
    qkv_w = (N_HEADS + 2 * N_KV_HEADS) * HEAD_DIM
    aw = N_HEADS * HEAD_DIM
    mw = MEM_HEADS * MEM_HEAD_DIM
    return {
        'x_prompt': nrm((BATCH, SEQ, D), 1.0),
        'x_sample': nrm((DEC_BATCH, DEC_SEQ, D), 1.0),
        'state_pool': nrm((N_POOL, DEC_BATCH, POOL_HIST, D), 1.0),
        'cache_win_k': nrm((N_ATTN, DEC_BATCH, wc, N_KV_HEADS, HEAD_DIM), 1.0),
        'cache_win_v': nrm((N_ATTN, DEC_BATCH, wc, N_KV_HEADS, HEAD_DIM), 1.0),
        'state_conv': nrm((N_CONV, DEC_BATCH, CONV_WIDTH - 1, D), 0.5),
        'state_ffn': nrm((DEPTH, DEC_BATCH, FFN_CONV_WIDTH - 1, FFN_DIM), 1.0),
        'cache_mem_k': nrm((DEPTH, DEC_BATCH, N_MEM, MEM_HEADS, MEM_HEAD_DIM), 1.0),
        'cache_mem_v': nrm((DEPTH, DEC_BATCH, N_MEM, MEM_HEADS, MEM_HEAD_DIM), 1.0),
        'mem_prompt': nrm((BATCH, N_MEM, D), 1.0),
        'norm_mix': 1.0 + nrm((DEPTH, D), 0.02),
        'norm_mem': 1.0 + nrm((DEPTH, D), 0.02),
        'norm_src': 1.0 + nrm((DEPTH, D), 0.02),
        'norm_ffn': 1.0 + nrm((DEPTH, D), 0.02),
        'pool_w': nrm((N_POOL, POOL_GROUPS, POOL_CH, POOL_CH), POOL_CH ** -0.5),
        'pool_scale': 1.0 + nrm((N_POOL, D), 0.02),
        'attn_w_qkv': nrm((N_ATTN, D, qkv_w), D ** -0.5),
        'attn_q_norm': 1.0 + nrm((N_ATTN, HEAD_DIM), 0.02),
        'attn_k_norm': 1.0 + nrm((N_ATTN, HEAD_DIM), 0.02),
        'attn_sinks': nrm((N_ATTN, N_HEADS), 0.5),
        'attn_w_o': nrm((N_ATTN, aw, D), aw ** -0.5),
        'conv_w_pw1': nrm((N_CONV, D, 2 * D), D ** -0.5),
        'conv_b_pw1': nrm((N_CONV, 2 * D), 0.02),
        'conv_w_dw': nrm((N_CONV, CONV_WIDTH, D), CONV_WIDTH ** -0.5),
        'conv_b_dw': nrm((N_CONV, D), 0.02),
        'conv_ln_g': 1.0 + nrm((N_CONV, D), 0.02),
        'conv_ln_b': nrm((N_CONV, D), 0.02),
        'conv_w_pw2': nrm((N_CONV, D, D), D ** -0.5),
        'conv_b_pw2': nrm((N_CONV, D), 0.02),
        'mem_w_q': nrm((DEPTH, D, mw), D ** -0.5),
        'mem_w_kv': nrm((DEPTH, D, 2 * mw), D ** -0.5),
        'mem_q_norm': 1.0 + nrm((DEPTH, MEM_HEAD_DIM), 0.02),
        'mem_k_norm': 1.0 + nrm((DEPTH, MEM_HEAD_DIM), 0.02),
        'mem_w_o': nrm((DEPTH, mw, D), mw ** -0.5),
        'ffn_w_up': nrm((DEPTH, D, 2 * FFN_DIM), D ** -0.5),
        'ffn_w_dw': nrm((DEPTH, FFN_CONV_WIDTH, FFN_DIM), FFN_CONV_WIDTH ** -0.5),
        'ffn_b_dw': nrm((DEPTH, FFN_DIM), 0.02),
        'ffn_w_down': nrm((DEPTH, FFN_DIM, D), FFN_DIM ** -0.5),
    }


def reference(x_prompt, x_sample, state_pool, cache_win_k, cache_win_v, state_conv, state_ffn,
              cache_mem_k, cache_mem_v, mem_prompt, norm_mix, norm_mem, norm_src, norm_ffn,
              pool_w, pool_scale, attn_w_qkv, attn_q_norm, attn_k_norm, attn_sinks, attn_w_o,
              conv_w_pw1, conv_b_pw1, conv_w_dw, conv_b_dw, conv_ln_g, conv_ln_b, conv_w_pw2, conv_b_pw2,
              mem_w_q, mem_w_kv, mem_q_norm, mem_k_norm, mem_w_o,
              ffn_w_up, ffn_w_dw, ffn_b_dw, ffn_w_down):
    xp, xs = x_prompt, x_sample
    n_p, s_p = xp.shape[0], xp.shape[1]
    t_s = xs.shape[1]
    pos_p = jnp.arange(s_p, dtype=jnp.int32)
    pos_s = PAST_LEN + jnp.arange(t_s, dtype=jnp.int32)

    pool_p, pool_s = [], []
    wk_p, wv_p, wk_s, wv_s = [], [], [], []
    conv_p, conv_s = [], []
    ffn_p, ffn_s = [], []
    mk_p, mv_p = [], []

    for i in range(DEPTH):
        kind, j = i % N_MIXERS, i // N_MIXERS
        hp = rms(xp, norm_mix[i])
        hs = rms(xs, norm_mix[i])
        if kind == 0:
            zp = jnp.zeros((n_p, POOL_HIST, D_MODEL), hp.dtype)
            mp, sp = pool_mixer(hp, zp, pos_p, pool_w[j], pool_scale[j])
            ms, ss = pool_mixer(hs, state_pool[j], pos_s, pool_w[j], pool_scale[j])
            pool_p.append(sp)
            pool_s.append(ss)
        elif kind == 1:
            q, k, v = swa_project(hp, pos_p, attn_w_qkv[j], attn_q_norm[j], attn_k_norm[j])
            mp = swa_prompt(q, k, v, attn_sinks[j]) @ attn_w_o[j]
            wk_p.append(k[:, -min(WINDOW, s_p):])
            wv_p.append(v[:, -min(WINDOW, s_p):])
            q, k, v = swa_project(hs, pos_s, attn_w_qkv[j], attn_q_norm[j], attn_k_norm[j])
            o, kb, vb = swa_sample(q, k, v, cache_win_k[j], cache_win_v[j], pos_s, attn_sinks[j])
            ms = o @ attn_w_o[j]
            wk_s.append(kb)
            wv_s.append(vb)
        else:
            zp = jnp.zeros((n_p, CONV_WIDTH - 1, D_MODEL), hp.dtype)
            mp, sp = conv_module(hp, zp, conv_w_pw1[j], conv_b_pw1[j], conv_w_dw[j], conv_b_dw[j],
                                 conv_ln_g[j], conv_ln_b[j], conv_w_pw2[j], conv_b_pw2[j])
            ms, ss = conv_module(hs, state_conv[j], conv_w_pw1[j], conv_b_pw1[j], conv_w_dw[j], conv_b_dw[j],
                                 conv_ln_g[j], conv_ln_b[j], conv_w_pw2[j], conv_b_pw2[j])
            conv_p.append(sp)
            conv_s.append(ss)
        xp = xp + mp
        xs = xs + ms

        kp, vp = mem_kv(mem_prompt, norm_src[i], mem_w_kv[i], mem_k_norm[i])
        mk_p.append(kp)
        mv_p.append(vp)
        xp = xp + mem_attend(rms(xp, norm_mem[i]), kp, vp, mem_w_q[i], mem_q_norm[i], mem_w_o[i])
        xs = xs + mem_attend(rms(xs, norm_mem[i]), cache_mem_k[i], cache_mem_v[i],
                             mem_w_q[i], mem_q_norm[i], mem_w_o[i])

        zf = jnp.zeros((n_p, FFN_CONV_WIDTH - 1, FFN_DIM), xp.dtype)
        fp, sfp = conv_ffn(rms(xp, norm_ffn[i]), zf, ffn_w_up[i], ffn_w_dw[i], ffn_b_dw[i], ffn_w_down[i])
        fs, sfs = conv_ffn(rms(xs, norm_ffn[i]), state_ffn[i], ffn_w_up[i], ffn_w_dw[i], ffn_b_dw[i], ffn_w_down[i])
        ffn_p.append(sfp)
        ffn_s.append(sfs)
        xp = xp + fp
        xs = xs + fs

    return (xp, xs,
            jnp.stack(pool_p), jnp.stack(pool_s),
            jnp.stack(wk_p), jnp.stack(wv_p), jnp.stack(wk_s), jnp.stack(wv_s),
            jnp.stack(conv_p), jnp.stack(conv_s),
            jnp.stack(ffn_p), jnp.stack(ffn_s),
            jnp.stack(mk_p), jnp.stack(mv_p))
```

```python
import numpy as np
import ml_dtypes
from contextlib import ExitStack
import concourse.bass as bass
import concourse.mybir as mybir
from concourse.bass_utils import run_bass_kernel_spmd

F32 = mybir.dt.float32
BF16 = mybir.dt.bfloat16
AF = mybir.ActivationFunctionType
ALU = mybir.AluOpType
AX = mybir.AxisListType

D = 1024
NCH = 8
HALO = 256
MAIN = 2048
TP = HALO + MAIN
NS = 16
T = TP + NS
TILES = [(0, 512), (512, 512), (1024, 512), (1536, 512), (2048, 272)]
NT = len(TILES)
FFN = 2816
NHC = 22
EPS = 1e-6
DEPTH = 4
NSLOT = 6
SLOT = 4096


class Buf:
    __slots__ = ("name", "w", "r")

    def __init__(self, name=""):
        self.name = name
        self.w = None
        self.r = {}


class _Eng:
    def __init__(self, name, h, sem, is_pe=False):
        self.name = name
        self.h = h
        self.sem = sem
        self.cnt = 0
        self.waited = {}
        self.is_pe = is_pe
        self.dsems = []
        self.dcnt = []
        self.dnext = 0


class Sched:
    def __init__(self, nc, es, n_dma_sems=8):
        self.nc = nc
        mk = lambda n: es.enter_context(nc.semaphore(n))
        self.E = {
            "pe": _Eng("pe", nc.tensor, mk("c_pe"), is_pe=True),
            "act": _Eng("act", nc.scalar, mk("c_act")),
            "dve": _Eng("dve", nc.vector, mk("c_dve")),
            "pool": _Eng("pool", nc.gpsimd, mk("c_pool")),
            "sp": _Eng("sp", nc.sync, mk("c_sp")),
        }
        for q in ("sp", "pool"):
            e = self.E[q]
            for i in range(n_dma_sems):
                e.dsems.append(mk(f"d_{q}{i}"))
                e.dcnt.append(0)
        self.n_ins = 0

    def _wait(self, e, toks):
        best = {}
        for t in toks:
            if t is None:
                continue
            s, v = t
            k = id(s)
            if k not in best or best[k][1] < v:
                best[k] = (s, v)
        for k, (s, v) in best.items():
            if s is e.sem and e.is_pe:
                continue
            if e.waited.get(k, 0) >= v:
                continue
            e.h.wait_ge(s, v)
            e.waited[k] = v
            self.n_ins += 1

    @staticmethod
    def _deps(reads, writes):
        toks = []
        for b in reads:
            toks.append(b.w)
        for b in writes:
            toks.append(b.w)
            toks.extend(b.r.values())
        return toks

    @staticmethod
    def _mark(tok, reads, writes):
        for b in writes:
            b.w = tok
            b.r = {}
        for b in reads:
            if b in writes:
                continue
            k = id(tok[0])
            if k not in b.r or b.r[k][1] < tok[1]:
                b.r[k] = tok

    def op(self, eng, fn, reads=(), writes=(), inc=True):
        e = self.E[eng]
        self._wait(e, self._deps(reads, writes))
        ins = fn(e.h)
        self.n_ins += 1
        if inc:
            e.cnt += 1
            ins.then_inc(e.sem, 1)
            tok = (e.sem, e.cnt)
        else:
            tok = (e.sem, e.cnt + 1)
        self._mark(tok, reads, writes)
        return ins

    def dma(self, q, out, in_, reads=(), writes=(), **kw):
        e = self.E[q]
        toks = self._deps(reads, writes)
        i = e.dnext
        e.dnext = (e.dnext + 1) % len(e.dsems)
        s = e.dsems[i]
        if e.dcnt[i] > 0:
            toks.append((s, e.dcnt[i]))
        self._wait(e, toks)
        e.dcnt[i] += 16
        ins = e.h.dma_start(out=out, in_=in_, **kw)
        ins.then_inc(s, 16)
        self.n_ins += 1
        tok = (s, e.dcnt[i])
        self._mark(tok, reads, writes)
        return tok

    def all_tokens(self):
        toks = []
        for q in ("sp", "pool"):
            qe = self.E[q]
            for s, c in zip(qe.dsems, qe.dcnt):
                if c > 0:
                    toks.append((s, c))
        for n in ("pe", "act", "dve", "pool"):
            x = self.E[n]
            if x.cnt > 0:
                toks.append((x.sem, x.cnt))
        return toks

    def barrier(self, toks=None):
        toks = [t for t in (self.all_tokens() if toks is None else toks)]
        for n in ("pe", "act", "dve", "pool", "sp"):
            e = self.E[n]
            mine = [t for t in toks if not (t[0] is e.sem)]
            if e.is_pe:
                pass
            own = [t for t in toks if t[0] is e.sem]
            self._wait(e, mine + (own if (not e.is_pe and n != "sp") else []))

    def finish(self):
        self._wait(self.E["sp"], self.all_tokens())


def _fm(v):
    v = np.asarray(v, np.float32)
    lead = int(np.prod(v.shape[:-1])) if v.ndim > 1 else 1
    n = v.shape[-1] // 128
    return v.reshape(lead, n, 128).transpose(2, 0, 1).reshape(128, lead * n)


def _qperm():
    heads = []
    for c in range(8):
        g2, cc = divmod(c, 4)
        heads.append((g2 * 8 + cc, g2 * 8 + 4 + cc))
    return heads


class CMap:
    def __init__(self):
        self.off = {}
        self.n = 0
        self.parts = []

    def add(self, name, arr):
        arr = np.ascontiguousarray(arr, dtype=np.float32)
        assert arr.shape[0] == 128, (name, arr.shape)
        self.off[name] = (self.n, arr.shape[1])
        self.n += arr.shape[1]
        self.parts.append(arr)

    def pack(self):
        return np.ascontiguousarray(np.concatenate(self.parts, axis=1))


def build_consts(inp, core):
    q = core % 4
    first = (q == 0)
    cm = CMap()
    cm.add("ident", np.eye(128, dtype=np.float32))
    cm.add("eps", np.full((128, 1), EPS, np.float32))
    for nm in ("norm_mix", "norm_mem", "norm_src", "norm_ffn"):
        cm.add(nm, _fm(inp[nm]))
    cm.add("pool_scale", _fm(inp["pool_scale"]))
    mq = np.asarray(inp["mem_q_norm"], np.float32)
    cm.add("mem_q_norm", _fm(np.tile(mq, (1, 4))))
    aq = np.asarray(inp["attn_q_norm"], np.float32)[0]
    ak = np.asarray(inp["attn_k_norm"], np.float32)[0]
    cm.add("attn_qn", np.tile(aq, 2)[:, None])
    cm.add("attn_kn", np.tile(ak, 2)[:, None])
    sk = np.asarray(inp["attn_sinks"], np.float32)[0]
    sinks = np.zeros((128, 8), np.float32)
    for c, (ha, hb) in enumerate(_qperm()):
        sinks[:64, c] = sk[ha]
        sinks[64:, c] = sk[hb]
    cm.add("sinks", sinks)
    cm.add("conv_b1", _fm(inp["conv_b_pw1"]))
    wdw = np.asarray(inp["conv_w_dw"], np.float32)[0]
    cm.add("conv_wdw", wdw.reshape(31, 8, 128).transpose(2, 1, 0).reshape(128, 8 * 31))
    cm.add("conv_bdw", _fm(inp["conv_b_dw"]))
    cm.add("conv_lng", _fm(inp["conv_ln_g"]))
    cm.add("conv_lnb", _fm(inp["conv_ln_b"]))
    cm.add("conv_b2", _fm(inp["conv_b_pw2"]))
    cm.add("ffn_wdw", _fm(inp["ffn_w_dw"]))
    cm.add("ffn_bdw", _fm(inp["ffn_b_dw"]))
    hm = np.zeros((128, HALO), np.float32) if first else np.ones((128, HALO), np.float32)
    cm.add("hmask", hm)
    s_ = np.arange(128)[:, None]
    q_ = np.arange(128)[None, :]
    cm.add("mask_cur", (s_ <= q_).astype(np.float32))
    cm.add("mask_prev", (s_ > q_).astype(np.float32))
    cm.add("mask_prev_h", (s_ > q_).astype(np.float32) * (0.0 if first else 1.0))
    rc = np.zeros((128, 4 * 16), np.float32)
    for g, w in enumerate((2, 4, 8, 16)):
        pos = np.arange(16) if first else np.full(16, 10 ** 6)
        rc[:, g * 16:(g + 1) * 16] = (w / np.minimum(pos + 1, w))[None, :]
    cm.add("pool_rc16", rc)
    cst = cm.pack()

    bm = {}
    parts = []
    nb = 0

    def badd(name, arr):
        nonlocal nb
        arr = np.asarray(arr, np.float32)
        bm[name] = (nb, arr.shape[1])
        nb += arr.shape[1]
        parts.append(arr)

    badd("ident", np.eye(128))
    badd("ones1024", np.full((128, 128), 1.0 / 1024))
    badd("ones256", np.full((128, 128), 1.0 / 256))
    badd("ones1", np.ones((128, 128)))
    bd = np.zeros((128, 128))
    bd[:64, :64] = 1.0 / 64
    bd[64:, 64:] = 1.0 / 64
    badd("bd64", bd)
    rot = np.zeros((128, 128))
    for dst in range(128):
        if (dst % 64) < 32:
            rot[dst + 32, dst] = -1.0
        else:
            rot[dst - 32, dst] = 1.0
    badd("rot", rot)
    e16 = np.zeros((128, 16 * 128))
    for s in range(16):
        e16[s, s * 128:(s + 1) * 128] = 1.0
    s_ = np.arange(128)[:, None]
    q_ = np.arange(128)[None, :]
    NEG = -30000.0
    badd("mb_cur", np.where(s_ <= q_, 0.0, NEG))
    badd("mb_prev", np.where(s_ > q_, 0.0, NEG))
    badd("mb_prev_h", np.where(s_ > q_, 0.0, NEG) if not first else np.full((128, 128), NEG))
    badd("e16", e16)
    cbf = np.concatenate(parts, axis=1).astype(ml_dtypes.bfloat16)
    start = q * MAIN - HALO
    pos = np.concatenate([np.arange(start, start + TP), np.full(NS, 8192)]).astype(np.float32)
    inv = (10000.0 ** (-np.arange(32, dtype=np.float32) * np.float32(2.0 / 64))).astype(np.float32)
    ang = (pos[None, :] * inv[:, None]).astype(np.float32)
    cos = np.tile(np.cos(ang).astype(np.float32), (4, 1))
    sin = np.tile(np.sin(ang).astype(np.float32), (4, 1))
    return cm.off, cst, bm, cbf, np.ascontiguousarray(cos), np.ascontiguousarray(sin)


def build_program(coff, ncst, boff, nbf, n_layers=DEPTH, dump=False, stop_after=None):
    nc = bass.Bass("TRN2", target_bir_lowering=False)

    def din(name, shape, dt=F32):
        return nc.dram_tensor(name, list(shape), dt, kind="ExternalInput").ap()

    def dout(name, shape):
        return nc.dram_tensor(name, list(shape), F32, kind="ExternalOutput").ap()

    xp_d = din("xp", [TP, D])
    xs_d = din("xs", [NS, D])
    memp_d = din("memp", [256, D])
    stpool_d = din("st_pool", [2, NS * 15, D])
    stconv_d = din("st_conv", [NS * 30, D])
    stffn_d = din("st_ffn", [DEPTH, NS * 2, FFN])
    cwk_d = din("cwk", [NS, 128, 256])
    cwv_d = din("cwv", [NS, 128, 256])
    cmk_d = din("cmk", [DEPTH, NS, 256, D])
    cmv_d = din("cmv", [DEPTH, NS, 256, D])
    cst_d = din("cst", [128, ncst])
    cbf_d = din("cbf", [128, nbf], BF16)
    cos_d = din("cos", [128, T])
    sin_d = din("sin", [128, T])
    pool_w_d = din("pool_w", [2, 4, 256, 256])
    wqkv_d = din("attn_w_qkv", [D, 1536])
    wo_d = din("attn_w_o", [D, D])
    pw1_d = din("conv_w_pw1", [D, 2 * D])
    pw2_d = din("conv_w_pw2", [D, D])
    mwq_d = din("mem_w_q", [DEPTH, D, D])
    mwkv_d = din("mem_w_kv", [DEPTH, D, 2 * D])
    mwo_d = din("mem_w_o", [DEPTH, D, D])
    mkn_d = din("mem_k_norm_b", [128, DEPTH * 256])
    nsrc_d = din("norm_src_b", [128, DEPTH * D])
    wup_d = din("ffn_w_up", [DEPTH, D, 2 * FFN])
    wdn_d = din("ffn_w_down", [DEPTH, FFN, D])

    y_p = dout("y_p", [MAIN, D])
    y_s = dout("y_s", [NS, D])
    o_pool_p = dout("o_pool_p", [2, 15, D])
    o_pool_s = dout("o_pool_s", [2, NS, 15, D])
    o_wk_p = dout("o_wk_p", [128, 256])
    o_wv_p = dout("o_wv_p", [128, 256])
    o_wk_s = dout("o_wk_s", [NS, 128, 256])
    o_wv_s = dout("o_wv_s", [NS, 128, 256])
    o_conv_p = dout("o_conv_p", [30, D])
    o_conv_s = dout("o_conv_s", [NS, 30, D])
    o_ffn_p = dout("o_ffn_p", [DEPTH, 2, FFN])
    o_ffn_s = dout("o_ffn_s", [DEPTH, NS * 2, FFN])
    o_mk = dout("o_mk", [DEPTH, 256, D])
    o_mv = dout("o_mv", [DEPTH, 256, D])
    NDUMP = 3 * DEPTH
    dbg = dout("dbg", [NDUMP, 128, NCH * T]) if dump else None

    es = ExitStack()
    with es:
        S = Sched(nc, es)
        _uc = [0]

        def uniq(name):
            _uc[0] += 1
            return f"{name}_{_uc[0]}"

        sb = lambda name, shape, dt=F32: es.enter_context(nc.sbuf_tensor(uniq(name), list(shape), dt))
        X = sb("X", [128, NCH, T])
        XB = [Buf(f"X{t}") for t in range(NT)]
        CST = sb("CST", [128, ncst])
        CBF = sb("CBF", [128, nbf], BF16)
        CB = Buf("const")
        WR = [sb(f"wr{i}", [128, SLOT], BF16) for i in range(NSLOT)]
        WB = [Buf(f"wr{i}") for i in range(NSLOT)]
        PS = [es.enter_context(nc.psum_tensor(f"ps{i}", [128, 512], F32)) for i in range(8)]
        PSB = [Buf(f"ps{i}") for i in range(8)]
        st = {"bank": 0, "slot": 0, "dump": 0}
        DR = {}

        def drb(name):
            if name not in DR:
                DR[name] = Buf(name)
            return DR[name]

        def cs(name, j=0, n=1):
            o, w = coff[name]
            return CST[:, o + j:o + j + n]

        def cb(name, rows=128, j=0, n=128):
            o, w = boff[name]
            return CBF[:rows, o + j:o + j + n]

        free_banks = list(range(8))

        def balloc():
            assert free_banks, "out of PSUM banks"
            return free_banks.pop(0)

        def bfree(b):
            free_banks.append(b)

        def bank():
            b = balloc()
            bfree(b)
            return b

        def wload(src, kc, n):
            s = st["slot"]
            st["slot"] = (s + 1) % NSLOT
            view = WR[s][:, :kc * n].rearrange("p (k n) -> p k n", k=kc)
            S.dma("pool", view, src, writes=[WB[s]])
            return view, WB[s]

        def wsrc(W, n0, n, r0=0, kc=8):
            return W[r0 * 128:(r0 + kc) * 128, n0:n0 + n].rearrange("(k p) n -> p k n", p=128)

        PRE = {}

        def loaders(kind, i):
            j = i // 3
            if kind == "pool":
                return {("pool", j): lambda: wload(pool_w_d[j].rearrange("g (k p) n -> p (g k) n", p=128), 8, 256)}
            if kind == "swa":
                d = {("wqkv", u): (lambda u=u: wload(wsrc(wqkv_d, u * 512, 512), 8, 512)) for u in range(3)}
                d.update({("wo", u): (lambda u=u: wload(wsrc(wo_d, u * 512, 512), 8, 512)) for u in range(2)})
                return d
            if kind == "conv":
                d = {("w1", u): (lambda u=u: wload(wsrc(pw1_d, u * 512, 512), 8, 512)) for u in range(4)}
                d.update({("w2", u): (lambda u=u: wload(wsrc(pw2_d, u * 512, 512), 8, 512)) for u in range(2)})
                return d
            if kind == "mem":
                return {("mkv", i, u): (lambda u=u: wload(wsrc(mwkv_d[i], u * 512, 512), 8, 512)) for u in range(4)}
            if kind == "ffn":
                return {("ffng", i, k): f for k, f in enumerate([
                    lambda: wload(wsrc(wup_d[i], 0, 512), 8, 512),
                    lambda: wload(wsrc(wup_d[i], FFN, 512), 8, 512),
                    lambda: wload(wsrc(wdn_d[i], 0, D, r0=0, kc=4), 4, D)])}
            return {}

        def take(key, loader):
            if key in PRE:
                return PRE.pop(key)
            return loader()

        def end_phase(nxt_phase):
            tk = S.all_tokens()
            if nxt_phase is not None:
                for key, ld in loaders(*nxt_phase).items():
                    PRE[key] = ld()
            S.barrier(tk)

        def mm(out, pairs, reads, wbuf, inc_last=True):
            n = len(pairs)
            for i, (l, r) in enumerate(pairs):
                S.op("pe", lambda e, l=l, r=r, i=i: e.matmul(out, lhsT=l, rhs=r, start=(i == 0), stop=(i == n - 1)),
                     reads=reads, writes=[wbuf], inc=(i == n - 1) and inc_last)

        def transpose(out, in_, ident, reads, wbuf, inc=True):
            S.op("pe", lambda e: e.transpose(out, in_, ident), reads=reads, writes=[wbuf], inc=inc)

        def dump_x():
            if dbg is None:
                return
            k = st["dump"]
            st["dump"] += 1
            S.dma("sp", dbg[k], X[:].rearrange("p c t -> p (c t)"), reads=XB, writes=[drb("dbg")])

        def halo_fix():
            o, _ = coff["hmask"]
            hm = CST[:, o:o + HALO].unsqueeze(1).broadcast_to([128, NCH, HALO])
            S.op("dve", lambda e: e.tensor_tensor(out=X[:, :, 0:HALO], in0=X[:, :, 0:HALO], in1=hm, op=ALU.mult),
                 reads=[XB[0], CB], writes=[XB[0]])

        S.dma("sp", CST[:], cst_d, writes=[CB])
        S.dma("sp", CBF[:], cbf_d, writes=[CB])
        IDF = cs("ident", 0, 128)

        with ExitStack() as ph:
            psb = lambda name, shape, dt=F32: ph.enter_context(nc.sbuf_tensor(uniq(name), list(shape), dt))
            STG = [psb(f"stg{i}", [128, D]) for i in range(2)]
            STGB = [Buf() for _ in range(2)]
            nblk = TP // 128
            for blk in range(nblk + 1):
                s = blk % 2
                rows = 128 if blk < nblk else NS
                src = xp_d[blk * 128:(blk + 1) * 128, :] if blk < nblk else xs_d
                col0 = blk * 128
                S.dma("sp", STG[s][:rows, :], src, writes=[STGB[s]])
                for half in range(2):
                    b = bank()
                    for c4 in range(4):
                        c = half * 4 + c4
                        transpose(PS[b][:, c4 * 128:c4 * 128 + rows], STG[s][:rows, c * 128:(c + 1) * 128],
                                  IDF[:rows, :rows], reads=[STGB[s], CB], wbuf=PSB[b], inc=(c4 == 3))
                    t = min(col0 // 512, NT - 1)
                    src_v = PS[b][:].rearrange("p (c r) -> p c r", c=4)[:, :, :rows]
                    eng = "act" if half == 0 else "dve"
                    if eng == "act":
                        S.op("act", lambda e, src_v=src_v, half=half, col0=col0, rows=rows: e.activation(
                            out=X[:, half * 4:half * 4 + 4, col0:col0 + rows], in_=src_v, func=AF.Copy),
                            reads=[PSB[b]], writes=[XB[t]])
                    else:
                        S.op("dve", lambda e, src_v=src_v, half=half, col0=col0, rows=rows: e.tensor_copy(
                            out=X[:, half * 4:half * 4 + 4, col0:col0 + rows], in_=src_v),
                            reads=[PSB[b]], writes=[XB[t]])
            S.barrier()

        def recip_act(out, in_, reads, wbuf):
            S.op("act", lambda e: e.activation(out=out, in_=in_, func=AF.Ln), reads=list(reads), writes=[wbuf])
            S.op("act", lambda e: e.activation(out=out, in_=out, func=AF.Exp, scale=-1.0), reads=[wbuf], writes=[wbuf])

        def rstd_act(out, in_, reads, wbuf, scale=1.0):
            S.op("act", lambda e: e.activation(out=out, in_=in_, func=AF.Ln, bias=cs("eps"), scale=scale),
                 reads=list(reads) + [CB], writes=[wbuf])
            S.op("act", lambda e: e.activation(out=out, in_=out, func=AF.Exp, scale=-0.5), reads=[wbuf], writes=[wbuf])

        def rms_cols(c0, w, xb, gname, gcol, Hout, HB, SQ, SQB, R, RB):
            b = bank()
            for c in range(NCH):
                S.op("act", lambda e, c=c: e.activation(out=SQ[c % 4][:, :w], in_=X[:, c, c0:c0 + w], func=AF.Square),
                     reads=[xb], writes=[SQB[c % 4]])
                S.op("pe", lambda e, c=c: e.matmul(PS[b][:, :w], lhsT=cb("ones1024"), rhs=SQ[c % 4][:, :w], start=(c == 0),
                                                   stop=(c == NCH - 1)), reads=[SQB[c % 4], CB], writes=[PSB[b]], inc=True)
            rstd_act(R[:, :w], PS[b][:, :w], [PSB[b]], RB)
            for c in range(NCH):
                S.op("dve", lambda e, c=c: e.scalar_tensor_tensor(
                    out=Hout[:, c, :w], in0=X[:, c, c0:c0 + w], scalar=cs(gname, gcol + c), in1=R[:, :w],
                    op0=ALU.mult, op1=ALU.mult), reads=[xb, RB, CB], writes=[HB])

        def rms_tile(t, gname, gcol, Hout, HB, SQ, SQB, R, RB):
            rms_cols(TILES[t][0], TILES[t][1], XB[t], gname, gcol, Hout, HB, SQ, SQB, R, RB)

        def resid_add(t, co, b, scale=None, bias=None):
            c0, w = TILES[t]
            if scale is not None:
                S.op("dve", lambda e: e.scalar_tensor_tensor(out=X[:, co, c0:c0 + w], in0=PS[b][:, :w], scalar=scale,
                                                             in1=X[:, co, c0:c0 + w], op0=ALU.mult, op1=ALU.add),
                     reads=[PSB[b], XB[t], CB], writes=[XB[t]])
            elif bias is not None:
                S.op("dve", lambda e: e.scalar_tensor_tensor(out=X[:, co, c0:c0 + w], in0=PS[b][:, :w], scalar=bias,
                                                             in1=X[:, co, c0:c0 + w], op0=ALU.add, op1=ALU.add),
                     reads=[PSB[b], XB[t], CB], writes=[XB[t]])
            else:
                S.op("dve", lambda e: e.tensor_tensor(out=X[:, co, c0:c0 + w], in0=X[:, co, c0:c0 + w],
                                                      in1=PS[b][:, :w], op=ALU.add),
                     reads=[PSB[b], XB[t]], writes=[XB[t]])

        def load_fm(ph_sb, rows_ap, R_, ncols, dst_fn, name, split=None):
            if isinstance(ph_sb, tuple):
                stg, sbuf = ph_sb
            else:
                stg = ph_sb(name, [128, 512])
                sbuf = Buf(name)
            for g0 in range(0, ncols, 512):
                wg = min(512, ncols - g0)
                ng = wg // 128
                S.dma("sp", stg[:R_, :wg], rows_ap[:, g0:g0 + wg], writes=[sbuf])
                b = bank()
                for j in range(ng):
                    transpose(PS[b][:, j * R_:(j + 1) * R_], stg[:R_, j * 128:(j + 1) * 128], IDF[:R_, :R_],
                              reads=[sbuf, CB], wbuf=PSB[b], inc=(j == ng - 1))
                dst, dbuf = dst_fn(g0 // 128, ng)
                if split is None:
                    src = PS[b][:, :ng * R_].rearrange("p (c r) -> p c r", c=ng)
                else:
                    src = PS[b][:, :ng * R_].rearrange("p (c s r) -> p c s r", c=ng, s=split[0], r=split[1])
                S.op("act", lambda e, dst=dst, src=src: e.activation(out=dst, in_=src, func=AF.Copy),
                     reads=[PSB[b]], writes=[dbuf])

        def store_tm(ph_sb, src_fn, R_, ncols, dst_ap, dbufname, name, reads):
            stg = [ph_sb(name + str(i), [128, 512]) for i in range(2)]
            sbufs = [Buf() for _ in range(2)]
            k = 0
            for g0 in range(0, ncols, 512):
                wg = min(512, ncols - g0)
                ng = wg // 128
                b = bank()
                for j in range(ng):
                    transpose(PS[b][:R_, j * 128:(j + 1) * 128], src_fn(g0 // 128 + j), IDF, reads=reads + [CB],
                              wbuf=PSB[b], inc=(j == ng - 1))
                s = k % 2
                k += 1
                S.op("act", lambda e, s=s, b=b, wg=wg: e.activation(out=stg[s][:R_, :wg], in_=PS[b][:R_, :wg],
                                                                  func=AF.Copy), reads=[PSB[b]], writes=[sbufs[s]])
                S.dma("sp", dst_ap[:, g0:g0 + wg], stg[s][:R_, :wg], reads=[sbufs[s]], writes=[drb(dbufname)])

        def pool_phase(i, j, nxt_phase=None):
            with ExitStack() as ph:
                psb = lambda name, shape, dt=F32: ph.enter_context(nc.sbuf_tensor(uniq(name), list(shape), dt))
                SQ = [psb(f"p_sq{k}", [128, 512], BF16) for k in range(4)]; SQB = [Buf() for _ in range(4)]
                RR = psb("p_r", [128, 512]); RRB = Buf()
                DF = [psb(f"p_d{k}", [128, NCH, 512], BF16) for k in range(2)]; DFB = [Buf() for _ in range(2)]
                HC = [psb(f"p_hc{k}", [128, 16 + 512]) for k in range(8)]; HCB = [Buf() for _ in range(8)]
                RES = [psb(f"p_res{k}", [128, 16 + 512]) for k in range(2)] * 4; RESB = [Buf() for _ in range(2)] * 4
                HCHB = [Buf() for _ in range(NCH)]
                ZERO16 = psb("p_z16", [128, 16])
                S.op("pool", lambda e: e.memset(ZERO16[:, :], 0.0), writes=[CB])
                SA = [psb(f"p_sa{k}", [128, 16 + 512]) for k in range(2)]; SAB = [Buf() for _ in range(2)]
                HCAR = psb("p_car", [128, NCH, 16]); HCARB = [Buf() for _ in range(NCH)]
                HS = psb("p_hs", [128, NCH, NS, 16]); HSB = Buf()
                TL = psb("p_tl", [128, NCH, 15 + NS]); TLB = Buf()
                WS = psb("p_ws", [128, 2, NS]); WSB = Buf()
                pw, pwB = take(("pool", j), loaders("pool", i)[("pool", j)])
                pstg = (psb("p_stg", [128, 512]), Buf())
                for half in range(2):
                    load_fm(pstg, stpool_d[j][half * 120:(half + 1) * 120, :], 120, D,
                            lambda c0, ng, half=half: (HS[:, c0:c0 + ng, half * 8:(half + 1) * 8, 0:15], HSB),
                            f"p_stg{half}", split=(8, 15))
                rco, _ = coff["pool_rc16"]
                kk_box = [0]
                def poolA(t):
                    c0, w = TILES[t]
                    last = (t == NT - 1)
                    wp = 256 if last else w
                    b = bank()
                    for c in range(NCH):
                        S.op("act", lambda e, c=c: e.activation(out=SQ[c % 4][:, :w], in_=X[:, c, c0:c0 + w], func=AF.Square),
                             reads=[XB[t]], writes=[SQB[c % 4]])
                        S.op("pe", lambda e, c=c: e.matmul(PS[b][:, :w], lhsT=cb("ones1024"), rhs=SQ[c % 4][:, :w], start=(c == 0),
                                                           stop=(c == NCH - 1)), reads=[SQB[c % 4], CB], writes=[PSB[b]], inc=True)
                    rstd_act(RR[:, :w], PS[b][:, :w], [PSB[b]], RRB)
                    df = DF[t % 2]; dfB = DFB[t % 2]
                    L = 16 + wp
                    for c in range(NCH):
                        hc = HC[c]; hcb = HCB[c]
                        if t == 0:
                            S.op("act", lambda e, hc=hc: e.activation(out=hc[:, 0:16], in_=ZERO16[:, :], func=AF.Copy), reads=[CB], writes=[HCHB[c]])
                        else:
                            S.op("act", lambda e, hc=hc, c=c: e.activation(out=hc[:, 0:16], in_=HCAR[:, c, :], func=AF.Copy),
                                 reads=[HCARB[c]], writes=[HCHB[c]])
                        S.op("dve", lambda e, hc=hc, c=c: e.scalar_tensor_tensor(
                            out=hc[:, 16:16 + w], in0=X[:, c, c0:c0 + w], scalar=cs("norm_mix", i * 8 + c),
                            in1=RR[:, :w], op0=ALU.mult, op1=ALU.mult), reads=[XB[t], RRB, CB], writes=[hcb])
                        if not last:
                            S.op("act", lambda e, hc=hc, c=c: e.activation(out=HCAR[:, c, :], in_=hc[:, w:w + 16], func=AF.Copy),
                                 reads=[hcb], writes=[HCARB[c]])
                        else:
                            S.op("act", lambda e, hc=hc, c=c: e.activation(out=TL[:, c, :], in_=hc[:, 16 + wp - 15:16 + w], func=AF.Copy),
                                 reads=[hcb], writes=[TLB])
                            S.op("act", lambda e, hc=hc, c=c: e.activation(out=HS[:, c, :, 15], in_=hc[:, 16 + wp:16 + w], func=AF.Copy),
                                 reads=[hcb], writes=[HSB])
                    fin = {}

                    def chain(c, eng):
                        wwin = (2, 4, 8, 16)[c // 2]
                        nst = {2: 1, 4: 2, 8: 3, 16: 4}[wwin]
                        cur, curB = HC[c], HCB[c]
                        k = 1
                        first = True
                        for st_ in range(nst):
                            if st_ == nst - 1:
                                nxt, nxtB = RES[c], RESB[c]
                            else:
                                nxt, nxtB = SA[st_ % 2], SAB[st_ % 2]
                            S.op(eng, lambda e, cur=cur, nxt=nxt, k=k: e.tensor_tensor(
                                out=nxt[:, k:L], in0=cur[:, k:L], in1=cur[:, 0:L - k], op=ALU.add),
                                reads=[curB] + ([HCHB[c]] if first else []), writes=[nxtB])
                            first = False
                            cur, curB = nxt, nxtB
                            k *= 2
                        fin[c] = (cur, curB)

                    def diff(c):
                        g = c // 2
                        wwin = (2, 4, 8, 16)[g]
                        cur, curB = fin[c]
                        hc, hcb = HC[c], HCB[c]
                        if t == 0:
                            S.op("dve", lambda e: e.tensor_tensor(
                                out=cur[:, 16 + HALO - c0:16 + HALO - c0 + 16], in0=cur[:, 16 + HALO - c0:16 + HALO - c0 + 16],
                                in1=CST[:, rco + g * 16:rco + (g + 1) * 16], op=ALU.mult), reads=[curB, CB], writes=[curB])
                        S.op("dve", lambda e: e.scalar_tensor_tensor(
                            out=df[:, c, 0:wp], in0=cur[:, 16:16 + wp], scalar=1.0 / wwin, in1=hc[:, 16:16 + wp],
                            op0=ALU.mult, op1=ALU.subtract), reads=[curB, hcb], writes=[dfB])
                        if last:
                            S.op("dve", lambda e: e.tensor_reduce(out=WS[:, c % 2, :], in_=HS[:, c, :, 16 - wwin:16],
                                                                  axis=AX.X, op=ALU.add), reads=[HSB], writes=[WSB])
                            S.op("dve", lambda e: e.scalar_tensor_tensor(
                                out=df[:, c, wp:w], in0=WS[:, c % 2, :], scalar=1.0 / wwin, in1=hc[:, 16 + wp:16 + w],
                                op0=ALU.mult, op1=ALU.subtract), reads=[WSB, hcb], writes=[dfB])
                    for c in range(NCH):
                        chain(c, "dve")
                        diff(c)

                def poolB(t):
                    c0, w = TILES[t]
                    df = DF[t % 2]; dfB = DFB[t % 2]
                    for co in range(NCH):
                        g = co // 2
                        b = bank()
                        mm(PS[b][:, :w], [(pw[:, g * 2 + k, (co % 2) * 128:(co % 2 + 1) * 128], df[:, g * 2 + k, :w])
                                          for k in range(2)], reads=[pwB, dfB], wbuf=PSB[b])
                        resid_add(t, co, b, scale=cs("pool_scale", j * 8 + co))
                poolA(0)
                for t in range(NT):
                    if t + 1 < NT:
                        poolA(t + 1)
                    poolB(t)
                halo_fix()
                stg = [psb(f"p_o{k}", [128, 512]) for k in range(2)]
                sbufs = [Buf() for _ in range(2)]
                for half in range(2):
                    b = bank()
                    for c4 in range(4):
                        transpose(PS[b][:15 + NS, c4 * 128:(c4 + 1) * 128], TL[:, half * 4 + c4, :], IDF,
                                  reads=[TLB, CB], wbuf=PSB[b], inc=(c4 == 3))
                    S.op("act", lambda e, half=half, b=b: e.activation(out=stg[half][:15 + NS, :], in_=PS[b][:15 + NS, :],
                                                                      func=AF.Copy), reads=[PSB[b]], writes=[sbufs[half]])
                    S.dma("sp", o_pool_p[j][:, half * 512:(half + 1) * 512], stg[half][0:15, :], reads=[sbufs[half]],
                          writes=[drb("o_pool_p")])
                    S.dma("sp", o_pool_s[j][:, 14, half * 512:(half + 1) * 512], stg[half][15:15 + NS, :],
                          reads=[sbufs[half]], writes=[drb("o_pool_s")])
                S.dma("sp", o_pool_s[j][:, 0:14, :], stpool_d[j].rearrange("(s r) d -> s r d", r=15)[:, 1:15, :],
                      writes=[drb("o_pool_s")])
                end_phase(nxt_phase)

        def mem_phase(i, nxt_phase=None):
            with ExitStack() as ph:
                psb = lambda name, shape, dt=F32: ph.enter_context(nc.sbuf_tensor(uniq(name), list(shape), dt))
                KT = psb("m_kt", [128, NCH, 256], BF16); KTB = Buf()
                VB_ = psb("m_vb", [128, 2, D], BF16); VBB = Buf()
                ph_kv = ExitStack()
                psbk = lambda name, shape, dt=F32: ph_kv.enter_context(nc.sbuf_tensor(uniq(name), list(shape), dt))
                MT = psbk("m_mt", [128, 2, D]); MTB = Buf()
                MSQ = psbk("m_msq", [128, D]); MSQB = Buf()
                MST = psbk("m_mst", [128, 8]); MSTB = Buf()
                MH = psbk("m_mh", [128, NCH, 256], BF16); MHB = Buf()
                KNB = psbk("m_knb", [128, 2, D], BF16); KNBB = Buf()
                KVO = [psbk(f"m_kvo{k}", [128, 512]) for k in range(2)]; KVOB = [Buf() for _ in range(2)]
                GK = psbk("m_gk", [128, 256]); GKB = Buf()
                GSR = psbk("m_gsr", [128, D]); GSRB = Buf()
                S.dma("sp", MT[:], memp_d.rearrange("(m p) d -> p m d", p=128), writes=[MTB])
                S.dma("sp", GK[:], mkn_d[:, i * 256:(i + 1) * 256], writes=[GKB])
                S.dma("sp", GSR[:], nsrc_d[:, i * D:(i + 1) * D], writes=[GSRB])
                for mc in range(2):
                    S.op("act", lambda e, mc=mc: e.activation(out=MSQ[:], in_=MT[:, mc, :], func=AF.Square,
                                                              accum_out=MST[:, mc:mc + 1]), reads=[MTB], writes=[MSQB, MSTB])
                rstd_act(MST[:, 4:6], MST[:, 0:2], [MSTB], MSTB, scale=1.0 / D)
                for mc in range(2):
                    S.op("dve", lambda e, mc=mc: e.scalar_tensor_tensor(out=MT[:, mc, :], in0=MT[:, mc, :], scalar=MST[:, 4 + mc:5 + mc],
                                                                        in1=GSR[:], op0=ALU.mult, op1=ALU.mult),
                         reads=[MSTB, MTB, GSRB], writes=[MTB])
                for mc in range(2):
                    for half in range(2):
                        b = bank()
                        for c4 in range(4):
                            c = half * 4 + c4
                            transpose(PS[b][:, c4 * 128:(c4 + 1) * 128], MT[:, mc, c * 128:(c + 1) * 128], IDF,
                                      reads=[MTB, CB], wbuf=PSB[b], inc=(c4 == 3))
                        src_v = PS[b][:].rearrange("p (c m) -> p c m", c=4)
                        if half == 0:
                            S.op("act", lambda e, src_v=src_v, half=half, mc=mc: e.activation(
                                out=MH[:, half * 4:half * 4 + 4, mc * 128:(mc + 1) * 128], in_=src_v, func=AF.Copy),
                                reads=[PSB[b]], writes=[MHB])
                        else:
                            S.op("dve", lambda e, src_v=src_v, half=half, mc=mc: e.tensor_copy(
                                out=MH[:, half * 4:half * 4 + 4, mc * 128:(mc + 1) * 128], in_=src_v),
                                reads=[PSB[b]], writes=[MHB])
                for u in range(4):
                    wv_, wvB = take(("mkv", i, u), loaders("mem", i)[("mkv", i, u)])
                    for mc in range(2):
                        b = bank()
                        mm(PS[b][:, :], [(MH[:, k, mc * 128:(mc + 1) * 128], wv_[:, k, :]) for k in range(NCH)],
                           reads=[MHB, wvB], wbuf=PSB[b])
                        o = KVO[(u * 2 + mc) % 2]; ob = KVOB[(u * 2 + mc) % 2]
                        if u < 2:
                            for hh in range(2):
                                S.op("act", lambda e, hh=hh, b=b: e.activation(
                                    out=MSQ[:, hh * 256:(hh + 1) * 256], in_=PS[b][:, hh * 256:(hh + 1) * 256], func=AF.Square,
                                    accum_out=MST[:, hh:hh + 1]), reads=[PSB[b]], writes=[MSQB, MSTB])
                            rstd_act(MST[:, 4:6], MST[:, 0:2], [MSTB], MSTB, scale=1.0 / 256)
                            for hh in range(2):
                                S.op("dve", lambda e, hh=hh, b=b, o=o: e.scalar_tensor_tensor(
                                    out=o[:, hh * 256:(hh + 1) * 256], in0=PS[b][:, hh * 256:(hh + 1) * 256],
                                    scalar=MST[:, 4 + hh:5 + hh], in1=GK[:], op0=ALU.mult, op1=ALU.mult),
                                    reads=[PSB[b], MSTB, GKB], writes=[ob])
                            S.op("act", lambda e, o=o, u=u, mc=mc: e.activation(out=KNB[:, mc, u * 512:(u + 1) * 512], in_=o[:],
                                                                              func=AF.Copy), reads=[ob], writes=[KNBB])
                            S.dma("sp", o_mk[i][mc * 128:(mc + 1) * 128, u * 512:(u + 1) * 512], o[:], reads=[ob],
                                  writes=[drb("o_mk")])
                        else:
                            S.op("act", lambda e, o=o, b=b: e.activation(out=o[:], in_=PS[b][:], func=AF.Copy),
                                 reads=[PSB[b]], writes=[ob])
                            S.op("dve", lambda e, b=b, u=u, mc=mc: e.tensor_copy(
                                out=VB_[:, mc, (u - 2) * 512:(u - 1) * 512], in_=PS[b][:]), reads=[PSB[b]], writes=[VBB])
                            S.dma("sp", o_mv[i][mc * 128:(mc + 1) * 128, (u - 2) * 512:(u - 1) * 512], o[:], reads=[ob],
                                  writes=[drb("o_mv")])
                IDB = cb("ident")
                for mc in range(2):
                    for half in range(2):
                        b = bank()
                        pb = PS[b][:].bitcast(BF16)
                        for c4 in range(4):
                            c = half * 4 + c4
                            transpose(pb[:, c4 * 128:(c4 + 1) * 128], KNB[:, mc, c * 128:(c + 1) * 128], IDB,
                                      reads=[KNBB, CB], wbuf=PSB[b], inc=(c4 == 3))
                        S.op("act", lambda e, pb=pb, half=half, mc=mc: e.activation(
                            out=KT[:, half * 4:half * 4 + 4, mc * 128:(mc + 1) * 128],
                            in_=pb[:, 0:512].rearrange("p (c m) -> p c m", c=4), func=AF.Copy), reads=[PSB[b]], writes=[KTB])
                S.barrier()
                ph_kv.close()
                wq = [wload(wsrc(mwq_d[i], u * 512, 512), 8, 512) for u in range(2)]
                wo = [wload(wsrc(mwo_d[i], u * 512, 512), 8, 512) for u in range(2)]
                HT = [psb(f"m_ht{k}", [128, NCH, 512], BF16) for k in range(3)]; HTB = [Buf() for _ in range(3)]
                SQ = [psb(f"m_sq{k}", [128, 512], BF16) for k in range(2)] * 2; SQB = [Buf() for _ in range(2)] * 2
                R = psb("m_r", [128, 512]); RB = Buf()
                SQH = [psb(f"m_sqh{k}", [128, 2, 512], BF16) for k in range(2)]; SQHB = [Buf() for _ in range(2)]
                QN = psb("m_qn", [128, NCH, 512], BF16); QNB = [Buf() for _ in range(4)]
                EE = [psb(f"m_e{k}", [128, 2, 512], BF16) for k in range(2)]; EEB = [Buf() for _ in range(2)]
                RH = psb("m_rh", [128, 512]); RHB = Buf()
                RD = psb("m_rd", [128, 512]); RDB = Buf()
                QS = psb("m_qs", [128, NCH, NS], BF16); QSB = Buf()
                OS = psb("m_os", [128, NCH, NS], BF16); OSB = Buf()
                order = [NT - 1] + list(range(NT - 1))
                IDB = cb("ident")
                QT = psb("s_qt", [NS, D], BF16); QTB = Buf()
                KS = [psb(f"s_ks{k}", [128, D], BF16) for k in range(2)]; KSB = [Buf() for _ in range(2)]
                VS = [psb(f"s_vs{k}", [128, 2, D], BF16) for k in range(2)]; VSB = [Buf() for _ in range(2)]
                PR = psb("s_pr", [128, 512]); PRB = Buf()
                SC = psb("s_sc", [128, NS, 2, 4]); SCB = [Buf() for _ in range(NS)]
                SE = psb("s_se", [128, NS, 2, 4], BF16); SEB = [Buf() for _ in range(NS)]
                DN = psb("s_dn", [128, NS * 4]); DNB = [Buf() for _ in range(NS)]
                plan = {}

                def at(step, prio, fn):
                    plan.setdefault(step, []).append((prio, len(plan.get(step, [])), fn))

                def mk_unit(ti, t, h):
                    c0, w = TILES[t]
                    wp = 256 if t == NT - 1 else w
                    n = 4 * ti + h
                    Ht = HT[ti % 3]; HtB = HTB[ti % 3]
                    sq = SQH[n % 2]; sqB = SQHB[n % 2]
                    ee = EE[n % 2]; eeB = EEB[n % 2]
                    u = {}

                    def A1():
                        u["qb"] = []
                        for jj in range(2):
                            co = h * 2 + jj
                            uu, cu = divmod(co, 4)
                            b = balloc()
                            u["qb"].append(b)
                            mm(PS[b][:, :w], [(wq[uu][0][:, k, cu * 128:(cu + 1) * 128], Ht[:, k, :w]) for k in range(NCH)],
                               reads=[wq[uu][1], HtB], wbuf=PSB[b])
                            S.op("act", lambda e, jj=jj, b=b: e.activation(out=sq[:, jj, :w], in_=PS[b][:, :w], func=AF.Square),
                                 reads=[PSB[b]], writes=[sqB])

                    def A2():
                        br = balloc()
                        mm(PS[br][:, :w], [(cb("ones256"), sq[:, jj, :w]) for jj in range(2)], reads=[sqB, CB], wbuf=PSB[br])
                        rstd_act(RH[:, :w], PS[br][:, :w], [PSB[br]], RHB)
                        bfree(br)
                        for jj in range(2):
                            qb = u["qb"][jj]
                            S.op("dve", lambda e, jj=jj, qb=qb: e.scalar_tensor_tensor(
                                out=QN[:, h * 2 + jj, :w], in0=PS[qb][:, :w], scalar=cs("mem_q_norm", i * 8 + h * 2 + jj),
                                in1=RH[:, :w], op0=ALU.mult, op1=ALU.mult), reads=[PSB[qb], RHB, CB], writes=[QNB[h]])
                            bfree(qb)
                        if t == NT - 1:
                            S.op("act", lambda e: e.activation(out=QS[:, h * 2:h * 2 + 2, :], in_=QN[:, h * 2:h * 2 + 2, 256:272], func=AF.Copy),
                                 reads=[QNB[h]], writes=[QSB])

                    def B1():
                        for mc in range(2):
                            b = balloc()
                            mm(PS[b][:, :wp], [(KT[:, h * 2 + jj, mc * 128:(mc + 1) * 128], QN[:, h * 2 + jj, :wp]) for jj in range(2)],
                               reads=[KTB, QNB[h]], wbuf=PSB[b])
                            S.op("act", lambda e, mc=mc, b=b: e.activation(out=ee[:, mc, :wp], in_=PS[b][:, :wp], func=AF.Exp,
                                                                          scale=1.0 / 16), reads=[PSB[b]], writes=[eeB])
                            bfree(b)

                    def B2():
                        bd = balloc()
                        mm(PS[bd][:, :wp], [(cb("ones1"), ee[:, mc, :wp]) for mc in range(2)], reads=[eeB, CB], wbuf=PSB[bd])
                        recip_act(RD[:, :wp], PS[bd][:, :wp], [PSB[bd]], RDB)
                        bfree(bd)
                        for jj in range(2):
                            b = balloc()
                            mm(PS[b][:, :wp], [(VB_[:, mc, h * 256 + jj * 128:h * 256 + (jj + 1) * 128], ee[:, mc, :wp])
                                               for mc in range(2)], reads=[VBB, eeB], wbuf=PSB[b])
                            S.op("dve", lambda e, jj=jj, b=b: e.tensor_tensor(out=Ht[:, h * 2 + jj, :wp], in0=PS[b][:, :wp],
                                                                            in1=RD[:, :wp], op=ALU.mult),
                                 reads=[PSB[b], RDB], writes=[HtB])
                            bfree(b)
                    at(n, 0, A1); at(n + 1, 1, A2); at(n + 2, 2, B1); at(n + 3, 3, B2)

                def mk_tile(ti, t):
                    c0, w = TILES[t]
                    wp = 256 if t == NT - 1 else w
                    Ht = HT[ti % 3]; HtB = HTB[ti % 3]

                    def RMS():
                        rms_tile(t, "norm_mem", i * 8, Ht, HtB, SQ, SQB, R, RB)

                    def WO():
                        for co in range(NCH):
                            uu, cu = divmod(co, 4)
                            b = balloc()
                            mm(PS[b][:, :wp], [(wo[uu][0][:, k, cu * 128:(cu + 1) * 128], Ht[:, k, :wp]) for k in range(NCH)],
                               reads=[wo[uu][1], HtB], wbuf=PSB[b])
                            S.op("dve", lambda e, co=co, b=b: e.tensor_tensor(out=X[:, co, c0:c0 + wp], in0=X[:, co, c0:c0 + wp],
                                                                            in1=PS[b][:, :wp], op=ALU.add),
                                 reads=[PSB[b], XB[t]], writes=[XB[t]])
                            bfree(b)
                    at(max(4 * ti - 4, -1), 5, RMS)
                    at(4 * ti + 7, 4, WO)

                def SETUP():
                    for half in range(2):
                        b = balloc()
                        pb = PS[b][:].bitcast(BF16)
                        for c4 in range(4):
                            c = half * 4 + c4
                            transpose(pb[:NS, c4 * 128:(c4 + 1) * 128], QS[:, c, :], IDB, reads=[QSB, CB], wbuf=PSB[b], inc=(c4 == 3))
                        S.op("act", lambda e, pb=pb, half=half: e.activation(out=QT[:, half * 512:(half + 1) * 512], in_=pb[:NS, 0:512],
                                                                            func=AF.Copy), reads=[PSB[b]], writes=[QTB])
                        bfree(b)

                def mk_sample(s):
                    vs = VS[s % 2]; vsB = VSB[s % 2]

                    def S1():
                        for mc in range(2):
                            S.dma("pool", KS[mc][:], cmk_d[i, s, mc * 128:(mc + 1) * 128, :], writes=[KSB[mc]])
                        S.dma("pool", vs[:], cmv_d[i, s].rearrange("(m p) d -> p m d", p=128), writes=[vsB])
                        qbb = []
                        for half in range(2):
                            b = balloc()
                            qbb.append(b)
                            mm(PS[b][:, :], [(cb("e16", rows=NS, j=s * 128, n=128), QT[:, half * 512:(half + 1) * 512])],
                               reads=[QTB, CB], wbuf=PSB[b])
                        for mc in range(2):
                            for h in range(4):
                                half, hh = divmod(h, 2)
                                S.op("dve", lambda e, mc=mc, h=h, half=half, hh=hh: e.scalar_tensor_tensor(
                                    out=PR[:, hh * 256:(hh + 1) * 256], in0=KS[mc][:, h * 256:(h + 1) * 256], scalar=1.0,
                                    in1=PS[qbb[half]][:, hh * 256:(hh + 1) * 256], op0=ALU.mult, op1=ALU.mult,
                                    accum_out=SC[:, s, mc, h:h + 1]), reads=[KSB[mc], PSB[qbb[half]]], writes=[PRB, SCB[s]])
                        for b in qbb:
                            bfree(b)

                    def S2a():
                        S.op("act", lambda e: e.activation(out=SE[:, s, :, :], in_=SC[:, s, :, :], func=AF.Exp, scale=1.0 / 16),
                             reads=[SCB[s]], writes=[SEB[s]])

                    def S2b():
                        bo = balloc()
                        for h in range(4):
                            for jj in range(2):
                                mm(PS[bo][:, (h * 2 + jj):(h * 2 + jj) + 1],
                                   [(vs[:, mc, h * 256 + jj * 128:h * 256 + (jj + 1) * 128], SE[:, s, mc, h:h + 1]) for mc in range(2)],
                                   reads=[vsB, SEB[s]], wbuf=PSB[bo], inc_last=(h == 3 and jj == 1))
                        bd = balloc()
                        mm(PS[bd][:, 0:4], [(cb("ones1"), SE[:, s, mc, :]) for mc in range(2)], reads=[SEB[s], CB], wbuf=PSB[bd])
                        S.op("dve", lambda e: e.reciprocal(out=DN[:, s * 4:(s + 1) * 4], in_=PS[bd][:, 0:4]),
                             reads=[PSB[bd]], writes=[DNB[s]])
                        S.op("dve", lambda e: e.tensor_tensor(
                            out=OS[:, :, s].rearrange("p (h j) -> p h j", h=4), in0=PS[bo][:, 0:8].rearrange("p (h j) -> p h j", h=4),
                            in1=DN[:, s * 4:(s + 1) * 4].unsqueeze(2).broadcast_to([128, 4, 2]), op=ALU.mult),
                            reads=[PSB[bo], DNB[s]], writes=[OSB])
                        bfree(bo); bfree(bd)
                    at(6 + s, 8, S1); at(7 + s, 6, S2a); at(8 + s, 7, S2b)

                for ti, t in enumerate(order):
                    mk_tile(ti, t)
                    for h in range(4):
                        mk_unit(ti, t, h)
                at(5, 9, SETUP)
                for s_ in range(NS):
                    mk_sample(s_)
                for step in sorted(plan):
                    for _, _, fn in sorted(plan[step], key=lambda x: (x[0], x[1])):
                        fn()
                for co in range(NCH):
                    u, cu = divmod(co, 4)
                    b = bank()
                    mm(PS[b][:, :NS], [(wo[u][0][:, k, cu * 128:(cu + 1) * 128], OS[:, k, :]) for k in range(NCH)],
                       reads=[wo[u][1], OSB], wbuf=PSB[b])
                    S.op("dve", lambda e, co=co, b=b: e.tensor_tensor(out=X[:, co, TP:T], in0=X[:, co, TP:T],
                                                                    in1=PS[b][:, :NS], op=ALU.add),
                         reads=[PSB[b], XB[NT - 1]], writes=[XB[NT - 1]])
                halo_fix()
                end_phase(nxt_phase)

        def ffn_phase(i, nxt_phase=None):
            with ExitStack() as ph:
                psb = lambda name, shape, dt=F32: ph.enter_context(nc.sbuf_tensor(uniq(name), list(shape), dt))
                H = psb("f_h", [128, NCH, T], BF16); HB = [Buf() for _ in range(NT)]
                GS = psb("f_gs", [128, NHC, NS, 3]); GSB = [Buf() for _ in range(NHC)]
                gsall = Buf()

                def gs_dst(c0, ng):
                    return GS[:, c0:c0 + ng, :, 0:2], gsall
                with ExitStack() as ph2:
                    psb2 = lambda name, shape, dt=F32: ph2.enter_context(nc.sbuf_tensor(uniq(name), list(shape), dt))
                    SQ = [psb2(f"f_sq{k}", [128, 512], BF16) for k in range(4)]; SQB = [Buf() for _ in range(4)]
                    R = psb2("f_r", [128, 512]); RB = Buf()
                    for t in range(NT):
                        rms_tile(t, "norm_ffn", i * 8, H[:, :, TILES[t][0]:TILES[t][0] + TILES[t][1]], HB[t], SQ, SQB, R, RB)
                    load_fm(psb2, stffn_d[i], 2 * NS, FFN, gs_dst, "f_stg", split=(NS, 2))
                    S.barrier()
                for b_ in GSB:
                    b_.w = gsall.w
                G = [psb(f"f_g{k}", [128, 516]) for k in range(4)]; GB = [Buf() for _ in range(4)]
                A = [psb(f"f_a{k}", [128, 512]) for k in range(2)]; AB = [Buf() for _ in range(2)]
                SG = [psb(f"f_sg{k}", [128, 512]) for k in range(2)]; SGB = [Buf() for _ in range(2)]
                HID = [psb(f"f_hid{k}", [128, 4, 512], BF16) for k in range(2)]; HIDB = [Buf() for _ in range(2)]
                HAL = psb("f_hal", [128, NHC, 2]); HALB = [Buf() for _ in range(NHC)]
                GST = psb("f_gst", [128, NHC, 2 * NS + 2]); GSTB = [Buf() for _ in range(NHC)]
                groups = [(0, 4), (4, 4), (8, 4), (12, 4), (16, 4), (20, 2)]

                def load_group(h0, n):
                    if h0 == 0:
                        lds = loaders("ffn", i)
                        return tuple(take(("ffng", i, k), lds[("ffng", i, k)]) for k in range(3))
                    g = wload(wsrc(wup_d[i], h0 * 128, n * 128), 8, n * 128)
                    v = wload(wsrc(wup_d[i], FFN + h0 * 128, n * 128), 8, n * 128)
                    d = wload(wsrc(wdn_d[i], 0, D, r0=h0, kc=n), n, D)
                    return g, v, d
                wdw = lambda k, hc: cs("ffn_wdw", (i * 3 + k) * NHC + hc)
                bdw = lambda hc: cs("ffn_bdw", i * NHC + hc)
                loaded = {0: load_group(*groups[0])}
                cnt_box = [0]

                def ffn_up(gi, t):
                    nonlocal_cnt = cnt_box
                    h0, n = groups[gi]
                    (gw, gwB), (vw, vwB), (dw, dwB) = loaded[gi]
                    c0, w = TILES[t]
                    last = (t == NT - 1)
                    wp = 256 if last else w
                    hid = HID[(gi * NT + t) % 2]; hidB = HIDB[(gi * NT + t) % 2]
                    cnt = cnt_box[0]
                    for j in range(n):
                        hc = h0 + j
                        Gr = G[cnt % 4]; GrB = GB[cnt % 4]
                        A_ = A[cnt % 2]; A_B = AB[cnt % 2]
                        SG_ = SG[cnt % 2]; SG_B = SGB[cnt % 2]
                        cnt += 1; cnt_box[0] = cnt
                        bg = bank()
                        mm(PS[bg][:, :w], [(gw[:, k, j * 128:(j + 1) * 128], H[:, k, c0:c0 + w]) for k in range(NCH)],
                           reads=[gwB, HB[t]], wbuf=PSB[bg])
                        if t == 0:
                            S.op("pool", lambda e, Gr=Gr: e.memset(Gr[:, 0:2], 0.0), writes=[GrB])
                        else:
                            S.op("pool", lambda e, Gr=Gr, hc=hc: e.tensor_copy(out=Gr[:, 0:2], in_=HAL[:, hc, :]),
                                 reads=[HALB[hc]], writes=[GrB])
                        S.op("act", lambda e, Gr=Gr, bg=bg: e.activation(out=Gr[:, 2:2 + wp], in_=PS[bg][:, :wp], func=AF.Copy),
                             reads=[PSB[bg]], writes=[GrB])
                        if not last:
                            S.op("pool", lambda e, Gr=Gr, hc=hc: e.tensor_copy(out=HAL[:, hc, :], in_=Gr[:, wp:wp + 2]),
                                 reads=[GrB], writes=[HALB[hc]])
                        else:
                            S.op("pool", lambda e, Gr=Gr, hc=hc: e.tensor_copy(out=GST[:, hc, 2 * NS:2 * NS + 2], in_=Gr[:, wp:wp + 2]),
                                 reads=[GrB], writes=[GSTB[hc]])
                            S.op("act", lambda e, hc=hc, bg=bg: e.activation(out=GS[:, hc, :, 2], in_=PS[bg][:, 256:272], func=AF.Copy),
                                 reads=[PSB[bg]], writes=[GSB[hc]])
                        bv = bank()
                        mm(PS[bv][:, :w], [(vw[:, k, j * 128:(j + 1) * 128], H[:, k, c0:c0 + w]) for k in range(NCH)],
                           reads=[vwB, HB[t]], wbuf=PSB[bv])
                        S.op("act", lambda e, A_=A_, hc=hc, bg=bg: e.activation(
                            out=A_[:, :wp], in_=PS[bg][:, :wp], func=AF.Identity, bias=bdw(hc), scale=wdw(2, hc)),
                            reads=[PSB[bg], CB], writes=[A_B])
                        S.op("dve", lambda e, Gr=Gr, A_=A_, hc=hc: e.scalar_tensor_tensor(
                            out=A_[:, :wp], in0=Gr[:, 1:1 + wp], scalar=wdw(1, hc), in1=A_[:, :wp], op0=ALU.mult, op1=ALU.add),
                            reads=[GrB, CB, A_B], writes=[A_B])
                        S.op("dve", lambda e, Gr=Gr, A_=A_, hc=hc: e.scalar_tensor_tensor(
                            out=A_[:, :wp], in0=Gr[:, 0:wp], scalar=wdw(0, hc), in1=A_[:, :wp], op0=ALU.mult, op1=ALU.add),
                            reads=[GrB, CB, A_B], writes=[A_B])
                        if last:
                            S.op("dve", lambda e, A_=A_, hc=hc: e.tensor_scalar(
                                out=A_[:, 256:272], in0=GS[:, hc, :, 0], scalar1=wdw(0, hc), scalar2=bdw(hc), op0=ALU.mult, op1=ALU.add),
                                reads=[GSB[hc], CB], writes=[A_B])
                            for k in (1, 2):
                                S.op("dve", lambda e, A_=A_, hc=hc, k=k: e.scalar_tensor_tensor(
                                    out=A_[:, 256:272], in0=GS[:, hc, :, k], scalar=wdw(k, hc), in1=A_[:, 256:272], op0=ALU.mult, op1=ALU.add),
                                    reads=[GSB[hc], CB, A_B], writes=[A_B])
                        S.op("act", lambda e, A_=A_, SG_=SG_: e.activation(out=SG_[:, :w], in_=A_[:, :w], func=AF.Silu),
                             reads=[A_B], writes=[SG_B])
                        S.op("dve", lambda e, SG_=SG_, bv=bv, j=j, hid=hid: e.tensor_tensor(
                            out=hid[:, j, :w], in0=SG_[:, :w], in1=PS[bv][:, :w], op=ALU.mult), reads=[SG_B, PSB[bv]], writes=[hidB])

                def ffn_down(gi, t):
                    h0, n = groups[gi]
                    (gw, gwB), (vw, vwB), (dw, dwB) = loaded[gi]
                    c0, w = TILES[t]
                    hid = HID[(gi * NT + t) % 2]; hidB = HIDB[(gi * NT + t) % 2]
                    for co in range(NCH):
                        bd = bank()
                        mm(PS[bd][:, :w], [(dw[:, j, co * 128:(co + 1) * 128], hid[:, j, :w]) for j in range(n)],
                           reads=[dwB, hidB], wbuf=PSB[bd])
                        resid_add(t, co, bd)

                steps = [(gi, t) for gi in range(len(groups)) for t in range(NT)]
                for k, (gi, t) in enumerate(steps):
                    ffn_up(gi, t)
                    if k > 0:
                        ffn_down(*steps[k - 1])
                    if t == 0 and gi + 1 < len(groups):
                        loaded[gi + 1] = load_group(*groups[gi + 1])
                ffn_down(*steps[-1])
                halo_fix()
                for hc in range(NHC):
                    S.op("pool", lambda e, hc=hc: e.tensor_copy(out=GST[:, hc, 0:2 * NS].rearrange("p (s r) -> p s r", r=2),
                                                                in_=GS[:, hc, :, 1:3]), reads=[GSB[hc]], writes=[GSTB[hc]])
                stg = [psb(f"f_o{k}", [128, 512]) for k in range(2)]
                sbufs = [Buf() for _ in range(2)]
                R_ = 2 * NS + 2
                for gi2, g0 in enumerate(range(0, NHC, 4)):
                    ng = min(4, NHC - g0)
                    b = bank()
                    for jx in range(ng):
                        transpose(PS[b][:R_, jx * 128:(jx + 1) * 128], GST[:, g0 + jx, :], IDF, reads=[GSTB[g0 + jx], CB],
                                  wbuf=PSB[b], inc=(jx == ng - 1))
                    s = gi2 % 2
                    S.op("act", lambda e, s=s, b=b, ng=ng: e.activation(out=stg[s][:R_, :ng * 128], in_=PS[b][:R_, :ng * 128],
                                                                      func=AF.Copy), reads=[PSB[b]], writes=[sbufs[s]])
                    S.dma("sp", o_ffn_s[i][:, g0 * 128:(g0 + ng) * 128], stg[s][0:2 * NS, :ng * 128], reads=[sbufs[s]],
                          writes=[drb("o_ffn_s")])
                    S.dma("sp", o_ffn_p[i][:, g0 * 128:(g0 + ng) * 128], stg[s][2 * NS:R_, :ng * 128], reads=[sbufs[s]],
                          writes=[drb("o_ffn_p")])
                end_phase(nxt_phase)

        def swa_phase(i, nxt_phase=None):
            with ExitStack() as ph:
                psb = lambda name, shape, dt=F32: ph.enter_context(nc.sbuf_tensor(uniq(name), list(shape), dt))
                lds = loaders("swa", i)
                wqkv = [take(("wqkv", u), lds[("wqkv", u)]) for u in range(3)]
                wo = [take(("wo", u), lds[("wo", u)]) for u in range(2)]
                HT = [psb(f"a_ht{k}", [128, NCH, 512], BF16) for k in range(2)]; HTB = [Buf() for _ in range(2)]
                KT = psb("a_kt", [128, 2, T], BF16); KTB = [Buf() for _ in range(NT)]
                VT = psb("a_vt", [128, TP // 128, 256], BF16); VTB = [Buf() for _ in range(NT)]
                QR = psb("a_qr", [128, NCH, 512], BF16); QRB = Buf()
                SINKE = psb("a_sink", [128, 8]); SINKB = Buf()
                VNEW = psb("a_vnew", [NS, 256]); VNEWB = Buf()
                ph1 = ExitStack()
                psb1 = lambda name, shape, dt=F32: ph1.enter_context(nc.sbuf_tensor(uniq(name), list(shape), dt))
                SQ = [psb1(f"a_sq{k}", [128, 512], BF16) for k in range(4)]; SQB = [Buf() for _ in range(4)]
                R = psb1("a_r", [128, 512]); RB = Buf()
                SQC = [psb1(f"a_sqc{k}", [128, 512], BF16) for k in range(2)]; SQCB = [Buf() for _ in range(2)]
                RC = [psb1(f"a_rc{k}", [128, 512]) for k in range(1)]; RCB = [Buf() for _ in range(1)]
                QNC = [psb1(f"a_qnc{k}", [128, 512], BF16) for k in range(2)]; QNCB = [Buf() for _ in range(2)]
                T1 = [psb1(f"a_t1{k}", [128, 512]) for k in range(1)]; T1B = [Buf() for _ in range(1)]
                T2 = [psb1(f"a_t2{k}", [128, 512]) for k in range(1)] * 2; T2B = [Buf()] * 2
                CS_ = [psb1(f"a_cos{k}", [128, 512]) for k in range(1)] * 2; CSB = [Buf()] * 2
                SN_ = [psb1(f"a_sin{k}", [128, 512]) for k in range(1)] * 2; SNB = [Buf()] * 2
                EE = [psb1(f"a_e{k}", [128, 512], BF16) for k in range(8)]; EEB = [Buf() for _ in range(8)]
                DN = [psb1(f"a_dn{k}", [128, 512]) for k in range(1)]; DNB = [Buf() for _ in range(1)]
                S.op("act", lambda e: e.activation(out=SINKE[:], in_=cs("sinks", 0, 8), func=AF.Exp), reads=[CB], writes=[SINKB])
                plan = {}

                def at(step, prio, fn):
                    plan.setdefault(step, []).append((prio, len(plan.get(step, [])), fn))
                ecnt = [0]

                def mk_attn(t, bl, g2, Ht, HtB):
                    bq = TILES[t][0] // 128 + bl
                    kbs = ([bq - 1] if bq > 0 else []) + [bq]
                    u = {"ee": []}

                    def S1():
                        for half in range(2):
                            kb0 = half * 64
                            for ki, kb in enumerate(kbs):
                                tk = min(kb * 128 // 512, NT - 1)
                                bs = balloc()
                                if kb == bq:
                                    mname = "mb_cur"
                                else:
                                    mname = "mb_prev_h" if bq == HALO // 128 else "mb_prev"
                                out3 = PS[bs][:, :].rearrange("p (c q) -> p c q", c=4)
                                S.op("pe", lambda e, out3=out3, kb0=kb0, kb=kb: e.matmul(
                                    out3, lhsT=KT[kb0:kb0 + 64, g2, kb * 128:(kb + 1) * 128],
                                    rhs=QR[kb0:kb0 + 64, 4 * g2:4 * g2 + 4, bl * 128:(bl + 1) * 128], start=True, stop=False),
                                    reads=[KTB[tk], QRB], writes=[PSB[bs]], inc=False)
                                S.op("pe", lambda e, out3=out3, mname=mname: e.matmul(
                                    out3, lhsT=cb("ident"), rhs=cb(mname).unsqueeze(1).broadcast_to([128, 4, 128]), start=False, stop=True),
                                    reads=[CB], writes=[PSB[bs]], inc=True)
                                ee = EE[ecnt[0] % 8]; eeB = EEB[ecnt[0] % 8]
                                ecnt[0] += 1
                                S.op("act", lambda e, ee=ee, bs=bs: e.activation(out=ee[:, :], in_=PS[bs][:, :], func=AF.Exp, scale=0.125),
                                     reads=[PSB[bs]], writes=[eeB])
                                bfree(bs)
                                u["ee"].append((half, ki, kb, tk, ee, eeB))

                    def S2():
                        bo = balloc(); bdn = balloc()
                        nee = len(u["ee"])
                        for idx, (half, ki, kb, tk, ee, eeB) in enumerate(u["ee"]):
                            kh = 2 * g2 + half
                            kb0 = half * 64
                            lastmm = (idx == nee - 1)
                            S.op("pe", lambda e, ee=ee, kb=kb, kh=kh, kb0=kb0, ki=ki: e.matmul(
                                PS[bo][kb0:kb0 + 64, :], lhsT=VT[:, kb, kh * 64:(kh + 1) * 64], rhs=ee[:, :],
                                start=(ki == 0), stop=(ki == len(kbs) - 1)), reads=[VTB[tk], eeB], writes=[PSB[bo]], inc=lastmm)
                            S.op("pe", lambda e, ee=ee, kb0=kb0, ki=ki: e.matmul(
                                PS[bdn][kb0:kb0 + 64, :], lhsT=cb("ones1", j=0, n=64), rhs=ee[:, :],
                                start=(ki == 0), stop=(ki == len(kbs) - 1)), reads=[CB, eeB], writes=[PSB[bdn]], inc=lastmm)
                        dn = DN[0]; dnB = DNB[0]
                        S.op("dve", lambda e: e.tensor_tensor(
                            out=dn[:, :].rearrange("p (c q) -> p c q", c=4), in0=PS[bdn][:, :].rearrange("p (c q) -> p c q", c=4),
                            in1=SINKE[:, 4 * g2:4 * g2 + 4].unsqueeze(2).broadcast_to([128, 4, 128]), op=ALU.add),
                            reads=[PSB[bdn], SINKB], writes=[dnB])
                        recip_act(dn[:, :], dn[:, :], [dnB], dnB)
                        S.op("dve", lambda e: e.tensor_tensor(
                            out=Ht[:, 4 * g2:4 * g2 + 4, bl * 128:(bl + 1) * 128], in0=PS[bo][:, :].rearrange("p (c q) -> p c q", c=4),
                            in1=dn[:, :].rearrange("p (c q) -> p c q", c=4), op=ALU.mult), reads=[PSB[bo], dnB], writes=[HtB])
                        bfree(bo); bfree(bdn)
                    return S1, S2

                def mk_proj(t, qc, Ht, HtB):
                    c0, w = TILES[t]
                    if qc < 8:
                        uu, cu = divmod(qc, 4)
                        gname = "attn_qn"
                    else:
                        uu, cu = 2, qc - 8
                        gname = "attn_kn"
                    k_ = qc % 2
                    u = {}

                    def P1():
                        b = balloc()
                        u["b"] = b
                        mm(PS[b][:, :w], [(wqkv[uu][0][:, k, cu * 128:(cu + 1) * 128], Ht[:, k, :w]) for k in range(NCH)],
                           reads=[wqkv[uu][1], HtB], wbuf=PSB[b])
                        S.op("act", lambda e: e.activation(out=SQC[k_][:, :w], in_=PS[b][:, :w], func=AF.Square),
                             reads=[PSB[b]], writes=[SQCB[k_]])

                    def P2():
                        b = u["b"]
                        br = balloc()
                        mm(PS[br][:, :w], [(cb("bd64"), SQC[k_][:, :w])], reads=[SQCB[k_], CB], wbuf=PSB[br])
                        rstd_act(RC[0][:, :w], PS[br][:, :w], [PSB[br]], RCB[0])
                        bfree(br)
                        S.op("dve", lambda e: e.scalar_tensor_tensor(
                            out=QNC[k_][:, :w], in0=PS[b][:, :w], scalar=cs(gname), in1=RC[0][:, :w], op0=ALU.mult, op1=ALU.mult),
                            reads=[PSB[b], RCB[0], CB], writes=[QNCB[k_]])
                        bfree(b)

                    def P3():
                        bt = balloc()
                        mm(PS[bt][:, :w], [(cb("rot"), QNC[k_][:, :w])], reads=[QNCB[k_], CB], wbuf=PSB[bt])
                        S.op("dve", lambda e: e.tensor_tensor(out=T1[0][:, :w], in0=QNC[k_][:, :w], in1=CS_[0][:, :w], op=ALU.mult),
                             reads=[QNCB[k_], CSB[0]], writes=[T1B[0]])
                        S.op("dve", lambda e: e.tensor_tensor(out=T2[0][:, :w], in0=PS[bt][:, :w], in1=SN_[0][:, :w], op=ALU.mult),
                             reads=[PSB[bt], SNB[0]], writes=[T2B[0]])
                        bfree(bt)
                        if qc < 8:
                            S.op("dve", lambda e: e.tensor_tensor(out=QR[:, qc, :w], in0=T1[0][:, :w], in1=T2[0][:, :w], op=ALU.add),
                                 reads=[T1B[0], T2B[0]], writes=[QRB])
                        else:
                            S.op("dve", lambda e: e.tensor_tensor(out=KT[:, qc - 8, c0:c0 + w], in0=T1[0][:, :w], in1=T2[0][:, :w], op=ALU.add),
                                 reads=[T1B[0], T2B[0]], writes=[KTB[t]])
                    return P1, P2, P3

                def mk_tile(t):
                    c0, w = TILES[t]
                    last = (t == NT - 1)
                    wp = 256 if last else w
                    Ht = HT[t % 2]; HtB = HTB[t % 2]
                    base = 24 * t

                    def RMS():
                        rms_tile(t, "norm_mix", i * 8, Ht, HtB, SQ, SQB, R, RB)

                    def TAB():
                        S.dma("sp", CS_[0][:, :w], cos_d[:, c0:c0 + w], writes=[CSB[0]])
                        S.dma("sp", SN_[0][:, :w], sin_d[:, c0:c0 + w], writes=[SNB[0]])

                    def VP():
                        for bl in range(wp // 128):
                            b = balloc()
                            mm(PS[b][:, 0:256], [(Ht[:, k, bl * 128:(bl + 1) * 128], wqkv[2][0][:, k, 256:512]) for k in range(NCH)],
                               reads=[HtB, wqkv[2][1]], wbuf=PSB[b])
                            S.op("act", lambda e, b=b, bl=bl: e.activation(out=VT[:, c0 // 128 + bl, :], in_=PS[b][:, 0:256], func=AF.Copy),
                                 reads=[PSB[b]], writes=[VTB[t]])
                            bfree(b)
                        if last:
                            b = balloc()
                            mm(PS[b][:NS, 0:256], [(Ht[:, k, 256:272], wqkv[2][0][:, k, 256:512]) for k in range(NCH)],
                               reads=[HtB, wqkv[2][1]], wbuf=PSB[b])
                            S.op("act", lambda e, b=b: e.activation(out=VNEW[:, :], in_=PS[b][:NS, 0:256], func=AF.Copy),
                                 reads=[PSB[b]], writes=[VNEWB])
                            bfree(b)

                    def WO():
                        for co in range(NCH):
                            uu, cu = divmod(co, 4)
                            b = balloc()
                            mm(PS[b][:, :wp], [(wo[uu][0][:, k, cu * 128:(cu + 1) * 128], Ht[:, k, :wp]) for k in range(NCH)],
                               reads=[wo[uu][1], HtB], wbuf=PSB[b])
                            S.op("dve", lambda e, co=co, b=b: e.tensor_tensor(out=X[:, co, c0:c0 + wp], in0=X[:, co, c0:c0 + wp],
                                                                            in1=PS[b][:, :wp], op=ALU.add),
                                 reads=[PSB[b], XB[t]], writes=[XB[t]])
                            bfree(b)
                    at(base - 12 if t > 0 else -2, 9, RMS)
                    at(base - 1, 8, TAB)
                    for qc in range(10):
                        P1, P2, P3 = mk_proj(t, qc, Ht, HtB)
                        at(base + qc, 0, P1); at(base + qc + 1, 1, P2); at(base + qc + 2, 2, P3)
                    at(base + 10, 3, VP)
                    units = [(bl, g2) for bl in range(wp // 128) for g2 in range(2)]
                    for n, (bl, g2) in enumerate(units):
                        S1, S2 = mk_attn(t, bl, g2, Ht, HtB)
                        at(base + 12 + n, 4, S1); at(base + 13 + n, 5, S2)
                    at(base + 14 + len(units), 6, WO)

                for t in range(NT):
                    mk_tile(t)
                for step in sorted(plan):
                    for _, _, fn in sorted(plan[step], key=lambda x: (x[0], x[1])):
                        fn()
                halo_fix()
                S.barrier()
                ph1.close()
                Ht = HT[(NT - 1) % 2]; HtB = HTB[(NT - 1) % 2]
                swa_sample(psb, Ht, HtB, QR, QRB, KT, KTB, VT, VTB, VNEW, VNEWB, SINKE, SINKB)
                for co in range(NCH):
                    u, cu = divmod(co, 4)
                    b = bank()
                    mm(PS[b][:, :NS], [(wo[u][0][:, k, cu * 128:(cu + 1) * 128], Ht[:, k, 256:272]) for k in range(NCH)],
                       reads=[wo[u][1], HtB], wbuf=PSB[b])
                    S.op("dve", lambda e, co=co, b=b: e.tensor_tensor(out=X[:, co, TP:T], in0=X[:, co, TP:T],
                                                                    in1=PS[b][:, :NS], op=ALU.add),
                         reads=[PSB[b], XB[NT - 1]], writes=[XB[NT - 1]])
                end_phase(nxt_phase)

        def swa_sample(psb, Ht, HtB, QR, QRB, KT, KTB, VT, VTB, VNEW, VNEWB, SINKE, SINKB):
            IDB = cb("ident")
            KNEW = psb("as_knew", [NS, 256]); KNEWB = Buf()
            OKP = psb("as_okp", [128, 256]); OKPB = Buf()
            OVP = psb("as_ovp", [128, 256]); OVPB = Buf()
            QT = psb("as_qt", [NS, D], BF16); QTB = Buf()
            KC = psb("as_kc", [128, NS, 256], BF16); KCB = Buf()
            VC = psb("as_vc", [128, NS, 256], BF16); VCB = Buf()
            PR = psb("as_pr", [128, D]); PRB = Buf()
            SC = psb("as_sc", [128, NS, 16]); SCB = Buf()
            SE = psb("as_se", [128, NS, 16], BF16); SEB = Buf()
            DNs = psb("as_dn", [128, 8]); DNsB = Buf()
            b = bank()
            pb = PS[b][:].bitcast(BF16)
            for kc in range(2):
                transpose(pb[:NS, kc * 128:(kc + 1) * 128], KT[:, kc, TP:T], IDB, reads=[KTB[NT - 1], CB], wbuf=PSB[b], inc=(kc == 1))
            S.op("act", lambda e: e.activation(out=KNEW[:, :], in_=pb[:NS, 0:256], func=AF.Copy), reads=[PSB[b]], writes=[KNEWB])
            b = bank()
            pb2 = PS[b][:].bitcast(BF16)
            for kc in range(2):
                transpose(pb2[:, kc * 128:(kc + 1) * 128], KT[:, kc, TP - 128:TP], IDB, reads=[KTB[NT - 1], CB], wbuf=PSB[b], inc=(kc == 1))
            S.op("act", lambda e: e.activation(out=OKP[:, :], in_=pb2[:, 0:256], func=AF.Copy), reads=[PSB[b]], writes=[OKPB])
            S.op("act", lambda e: e.activation(out=OVP[:, :], in_=VT[:, TP // 128 - 1, :], func=AF.Copy), reads=[VTB[NT - 1]], writes=[OVPB])
            S.dma("sp", o_wk_p, OKP[:, :], reads=[OKPB], writes=[drb("o_wk_p")])
            S.dma("sp", o_wv_p, OVP[:, :], reads=[OVPB], writes=[drb("o_wv_p")])
            S.dma("sp", o_wk_s[:, 0:127, :], cwk_d[:, 1:128, :], writes=[drb("o_wk_s")])
            S.dma("sp", o_wv_s[:, 0:127, :], cwv_d[:, 1:128, :], writes=[drb("o_wv_s")])
            S.dma("sp", o_wk_s[:, 127, :], KNEW[:, :], reads=[KNEWB], writes=[drb("o_wk_s2")])
            S.dma("sp", o_wv_s[:, 127, :], VNEW[:, :], reads=[VNEWB], writes=[drb("o_wv_s2")])
            S.dma("pool", KC[:], o_wk_s.rearrange("s k d -> k s d"), reads=[drb("o_wk_s"), drb("o_wk_s2")], writes=[KCB])
            S.dma("pool", VC[:], o_wv_s.rearrange("s k d -> k s d"), reads=[drb("o_wv_s"), drb("o_wv_s2")], writes=[VCB])
            for half in range(2):
                b = bank()
                pb = PS[b][:].bitcast(BF16)
                for c4 in range(4):
                    c = half * 4 + c4
                    transpose(pb[:NS, c4 * 128:(c4 + 1) * 128], QR[:, c, 256:272], IDB, reads=[QRB, CB], wbuf=PSB[b], inc=(c4 == 3))
                S.op("act", lambda e, pb=pb, half=half: e.activation(out=QT[:, half * 512:(half + 1) * 512], in_=pb[:NS, 0:512],
                                                                    func=AF.Copy), reads=[PSB[b]], writes=[QTB])
            for s in range(NS):
                for g2 in range(2):
                    b = bank()
                    mm(PS[b][:, :], [(cb("e16", rows=NS, j=s * 128, n=128), QT[:, g2 * 512:(g2 + 1) * 512])], reads=[QTB, CB], wbuf=PSB[b])
                    S.op("dve", lambda e, s=s, g2=g2, b=b: e.tensor_tensor(
                        out=PR[:, g2 * 512:(g2 + 1) * 512].rearrange("p (c x) -> p c x", c=4),
                        in0=KC[:, s, g2 * 128:(g2 + 1) * 128].unsqueeze(1).broadcast_to([128, 4, 128]),
                        in1=PS[b][:, :].rearrange("p (c x) -> p c x", c=4), op=ALU.mult), reads=[KCB, PSB[b]], writes=[PRB])
                S.op("dve", lambda e, s=s: e.tensor_reduce(out=SC[:, s, :], in_=PR[:, :].rearrange("p (h d) -> p h d", d=64),
                                                          axis=AX.X, op=ALU.add), reads=[PRB], writes=[SCB])
            S.op("act", lambda e: e.activation(out=SE[:], in_=SC[:], func=AF.Exp, scale=0.125), reads=[SCB], writes=[SEB])
            for s in range(NS):
                bo = bank(); bdn = bank()
                for g2 in range(2):
                    for half in range(2):
                        kh = 2 * g2 + half
                        kb0 = half * 64
                        rhs = SE[:, s, :].rearrange("p (c h) -> p c h", h=2)[:, 4 * g2:4 * g2 + 4, half]
                        lastmm = (g2 == 1 and half == 1)
                        S.op("pe", lambda e, s=s, kh=kh, kb0=kb0, g2=g2, rhs=rhs: e.matmul(
                            PS[bo][kb0:kb0 + 64, 4 * g2:4 * g2 + 4], lhsT=VC[:, s, kh * 64:(kh + 1) * 64], rhs=rhs, start=True, stop=True),
                            reads=[VCB, SEB], writes=[PSB[bo]], inc=lastmm)
                        S.op("pe", lambda e, kb0=kb0, g2=g2, rhs=rhs: e.matmul(
                            PS[bdn][kb0:kb0 + 64, 4 * g2:4 * g2 + 4], lhsT=cb("ones1", j=0, n=64), rhs=rhs, start=True, stop=True),
                            reads=[CB, SEB], writes=[PSB[bdn]], inc=lastmm)
                S.op("dve", lambda e, bdn=bdn: e.tensor_tensor(out=DNs[:, :], in0=PS[bdn][:, 0:8], in1=SINKE[:, :], op=ALU.add),
                     reads=[PSB[bdn], SINKB], writes=[DNsB])
                S.op("dve", lambda e: e.reciprocal(out=DNs[:, :], in_=DNs[:, :]), reads=[DNsB], writes=[DNsB])
                S.op("dve", lambda e, s=s, bo=bo: e.tensor_tensor(out=Ht[:, :, 256 + s], in0=PS[bo][:, 0:8], in1=DNs[:, :], op=ALU.mult),
                     reads=[PSB[bo], DNsB], writes=[HtB])

        def conv_phase(i, nxt_phase=None):
            with ExitStack() as ph:
                psb = lambda name, shape, dt=F32: ph.enter_context(nc.sbuf_tensor(uniq(name), list(shape), dt))
                lds = loaders("conv", i)
                w1 = [take(("w1", u), lds[("w1", u)]) for u in range(4)]
                w2 = [take(("w2", u), lds[("w2", u)]) for u in range(2)]
                s0 = TILES[0][0]
                CT = [(k * 256, 256) for k in range(8)] + [(2048, 272)]
                CT[0] = (s0, 256 - s0)
                NCT = len(CT)
                W = 272
                GLUA = psb("c_glua", [128, NCH, 30 + T], BF16)
                GB = [[Buf() for _ in range(NCT)] for _ in range(NCH)]
                G0B = Buf()
                TL = psb("c_tl", [128, NCH, 30 + NS]); TLB = Buf()
                phG = ExitStack()
                GLS = phG.enter_context(nc.sbuf_tensor(uniq("c_gls"), [128, NCH, NS, 31], BF16)); GLSB = Buf()
                wo_, _ = coff["conv_wdw"]
                ho, _ = coff["hmask"]
                S.op("pool", lambda e: e.memset(GLUA[:, :, 0:30 + s0], 0.0), writes=[G0B])
                with ExitStack() as phA:
                    psa = lambda name, shape, dt=F32: phA.enter_context(nc.sbuf_tensor(uniq(name), list(shape), dt))
                    HT = [psa(f"c_ht{k}", [128, NCH, W], BF16) for k in range(2)]; HTB = [Buf() for _ in range(2)]
                    SQ = [psa(f"c_sq{k}", [128, W], BF16) for k in range(4)]; SQB = [Buf() for _ in range(4)]
                    R = psa("c_r", [128, W]); RB = Buf()
                    SGm = [psa(f"c_sg{k}", [128, W]) for k in range(2)]; SGmB = [Buf() for _ in range(2)]
                    cstg = (psa("c_stg", [128, 512]), Buf())
                    for q4 in range(4):
                        load_fm(cstg, stconv_d[q4 * 120:(q4 + 1) * 120, :], 120, D,
                                lambda c0, ng, q4=q4: (GLS[:, c0:c0 + ng, q4 * 4:(q4 + 1) * 4, 0:30], GLSB), f"c_stg{q4}", split=(4, 30))

                    def rmsA(ti):
                        c0, w = CT[ti]
                        rms_cols(c0, w, XB[min(c0 // 512, NT - 1)], "norm_mix", i * 8, HT[ti % 2], HTB[ti % 2], SQ, SQB, R, RB)
                    rmsA(0)
                    for ti, (c0, w) in enumerate(CT):
                        last = (ti == NCT - 1)
                        wp = 256 if last else w
                        Ht = HT[ti % 2]; HtB = HTB[ti % 2]
                        if ti + 1 < NCT:
                            rmsA(ti + 1)
                        for c in range(NCH):
                            u, cu = divmod(c, 4)
                            ba = balloc()
                            mm(PS[ba][:, :w], [(w1[u][0][:, k, cu * 128:(cu + 1) * 128], Ht[:, k, :w]) for k in range(NCH)],
                               reads=[w1[u][1], HtB], wbuf=PSB[ba])
                            bb = balloc()
                            mm(PS[bb][:, :w], [(w1[2 + u][0][:, k, cu * 128:(cu + 1) * 128], Ht[:, k, :w]) for k in range(NCH)],
                               reads=[w1[2 + u][1], HtB], wbuf=PSB[bb])
                            sg = SGm[c % 2]; sgB = SGmB[c % 2]
                            S.op("act", lambda e, sg=sg, bb=bb, c=c: e.activation(out=sg[:, :w], in_=PS[bb][:, :w], func=AF.Sigmoid,
                                                                                bias=cs("conv_b1", 8 + c)), reads=[PSB[bb], CB], writes=[sgB])
                            bfree(bb)
                            S.op("dve", lambda e, sg=sg, ba=ba, c=c: e.scalar_tensor_tensor(
                                out=GLUA[:, c, 30 + c0:30 + c0 + wp], in0=PS[ba][:, :wp], scalar=cs("conv_b1", c), in1=sg[:, :wp],
                                op0=ALU.add, op1=ALU.mult), reads=[PSB[ba], sgB, CB], writes=[GB[c][ti]])
                            if last:
                                S.op("dve", lambda e, sg=sg, ba=ba, c=c: e.scalar_tensor_tensor(
                                    out=TL[:, c, 30:30 + NS], in0=PS[ba][:, 256:272], scalar=cs("conv_b1", c), in1=sg[:, 256:272],
                                    op0=ALU.add, op1=ALU.mult), reads=[PSB[ba], sgB, CB], writes=[TLB])
                                S.op("dve", lambda e, c=c: e.tensor_copy(out=GLS[:, c, :, 30], in_=TL[:, c, 30:30 + NS]),
                                     reads=[TLB], writes=[GLSB])
                                S.op("dve", lambda e, c=c: e.tensor_copy(out=TL[:, c, 0:30], in_=GLUA[:, c, TP:TP + 30]),
                                     reads=[GB[c][ti]], writes=[TLB])
                            bfree(ba)
                        if ti == 0:
                            hm = CST[:, ho:ho + HALO].unsqueeze(1).broadcast_to([128, NCH, HALO])
                            S.op("dve", lambda e, hm=hm: e.tensor_tensor(out=GLUA[:, :, 30:30 + HALO], in0=GLUA[:, :, 30:30 + HALO],
                                                                        in1=hm, op=ALU.mult), reads=[GB[c][0] for c in range(NCH)] + [CB, G0B],
                                 writes=[GB[c][0] for c in range(NCH)] + [G0B])
                    S.barrier()
                with ExitStack() as phB:
                    psq = lambda name, shape, dt=F32: phB.enter_context(nc.sbuf_tensor(uniq(name), list(shape), dt))
                    DG = [psq(f"c_dg{k}", [128, 31, 128], BF16) for k in range(2)]; DGB = [Buf() for _ in range(2)]
                    PRs = psq("c_prs", [128, NS, 31]); PRsB = Buf()
                    DS = psq("c_ds", [128, NS]); DSB = Buf()

                    def build(c):
                        wv = CST[:, wo_ + c * 31:wo_ + (c + 1) * 31]
                        S.op("pool", lambda e, wv=wv, c=c: e.tensor_tensor(
                            out=DG[c % 2][:, :, :], in0=cb("ident").unsqueeze(1).broadcast_to([128, 31, 128]),
                            in1=wv.unsqueeze(2).broadcast_to([128, 31, 128]), op=ALU.mult), reads=[CB], writes=[DGB[c % 2]])
                    build(0)
                    for c in range(NCH):
                        if c + 1 < NCH:
                            build(c + 1)
                        wv = CST[:, wo_ + c * 31:wo_ + (c + 1) * 31]
                        S.op("dve", lambda e, c=c, wv=wv: e.tensor_tensor(
                            out=PRs[:, :, :], in0=GLS[:, c, :, :], in1=wv.unsqueeze(1).broadcast_to([128, NS, 31]), op=ALU.mult),
                            reads=[GLSB, CB], writes=[PRsB])
                        S.op("dve", lambda e: e.tensor_reduce(out=DS[:, :], in_=PRs[:, :, :], axis=AX.X, op=ALU.add),
                             reads=[PRsB], writes=[DSB])
                        for tj in range(NCT - 1, -1, -1):
                            c0 = CT[tj][0]
                            bd = balloc()
                            rb = [GB[c][tj]] + ([GB[c][tj - 1]] if tj > 0 else [G0B])
                            wj = 256 if tj == NCT - 1 else CT[tj][1]
                            mm(PS[bd][:, :wj], [(DG[c % 2][:, k, :], GLUA[:, c, c0 + k:c0 + k + wj]) for k in range(31)],
                               reads=[DGB[c % 2]] + rb, wbuf=PSB[bd])
                            S.op("act", lambda e, bd=bd, c=c, c0=c0, wj=wj: e.activation(out=GLUA[:, c, 30 + c0:30 + c0 + wj], in_=PS[bd][:, :wj],
                                                                                func=AF.Identity, bias=cs("conv_bdw", c)),
                                 reads=[PSB[bd], CB], writes=[GB[c][tj]])
                            bfree(bd)
                        S.op("dve", lambda e, c=c: e.tensor_scalar(out=GLUA[:, c, 30 + TP:30 + T], in0=DS[:, :], scalar1=cs("conv_bdw", c),
                                                                   scalar2=None, op0=ALU.add), reads=[DSB, CB], writes=[GB[c][NCT - 1]])
                    S.barrier()
                phG.close()
                with ExitStack() as phC:
                    psc = lambda name, shape, dt=F32: phC.enter_context(nc.sbuf_tensor(uniq(name), list(shape), dt))
                    HT = [psc(f"c_yt{k}", [128, NCH, 512], BF16) for k in range(2)]; HTB = [Buf() for _ in range(2)]
                    SQ2 = psc("c_sq2", [128, NCH, 512], BF16); SQ2B = Buf()
                    MU = psc("c_mu", [128, 512]); MUB = Buf()
                    RS = psc("c_rs", [128, 512]); RSB = Buf()
                    TA = [psc(f"c_ta{k}", [128, 512]) for k in range(2)]; TAB_ = [Buf() for _ in range(2)]
                    gall = [GB[c][tj] for c in range(NCH) for tj in range(NCT)]

                    def lnorm(t):
                        c0, w = TILES[t]
                        Ht = HT[t % 2]; HtB = HTB[t % 2]
                        Dv = GLUA[:, :, 30 + c0:30 + c0 + w]
                        S.op("act", lambda e: e.activation(out=SQ2[:, :, :w], in_=Dv, func=AF.Square), reads=gall, writes=[SQ2B])
                        bmu = balloc()
                        mm(PS[bmu][:, :w], [(cb("ones1024"), GLUA[:, c, 30 + c0:30 + c0 + w]) for c in range(NCH)], reads=gall + [CB], wbuf=PSB[bmu])
                        bms = balloc()
                        mm(PS[bms][:, :w], [(cb("ones1024"), SQ2[:, c, :w]) for c in range(NCH)], reads=[SQ2B, CB], wbuf=PSB[bms])
                        S.op("act", lambda e: e.activation(out=MU[:, :w], in_=PS[bmu][:, :w], func=AF.Copy), reads=[PSB[bmu]], writes=[MUB])
                        bfree(bmu)
                        S.op("dve", lambda e: e.tensor_tensor(out=RS[:, :w], in0=MU[:, :w], in1=MU[:, :w], op=ALU.mult), reads=[MUB], writes=[RSB])
                        S.op("dve", lambda e: e.tensor_tensor(out=RS[:, :w], in0=PS[bms][:, :w], in1=RS[:, :w], op=ALU.subtract),
                             reads=[PSB[bms], RSB], writes=[RSB])
                        bfree(bms)
                        S.op("dve", lambda e: e.tensor_scalar(out=RS[:, :w], in0=RS[:, :w], scalar1=0.0, scalar2=None, op0=ALU.max),
                             reads=[RSB], writes=[RSB])
                        rstd_act(RS[:, :w], RS[:, :w], [RSB], RSB)
                        for c in range(NCH):
                            ta = TA[c % 2]; taB = TAB_[c % 2]
                            S.op("dve", lambda e, ta=ta, c=c: e.tensor_tensor(out=ta[:, :w], in0=GLUA[:, c, 30 + c0:30 + c0 + w], in1=MU[:, :w],
                                                                            op=ALU.subtract), reads=gall + [MUB], writes=[taB])
                            S.op("dve", lambda e, ta=ta: e.tensor_tensor(out=ta[:, :w], in0=ta[:, :w], in1=RS[:, :w], op=ALU.mult),
                                 reads=[taB, RSB], writes=[taB])
                            S.op("act", lambda e, ta=ta, c=c: e.activation(out=Ht[:, c, :w], in_=ta[:, :w], func=AF.Silu,
                                                                          bias=cs("conv_lnb", c), scale=cs("conv_lng", c)),
                                 reads=[taB, CB], writes=[HtB])

                    def pw2(t):
                        c0, w = TILES[t]
                        Ht = HT[t % 2]; HtB = HTB[t % 2]
                        for co in range(NCH):
                            u, cu = divmod(co, 4)
                            b = balloc()
                            mm(PS[b][:, :w], [(w2[u][0][:, k, cu * 128:(cu + 1) * 128], Ht[:, k, :w]) for k in range(NCH)],
                               reads=[w2[u][1], HtB], wbuf=PSB[b])
                            S.op("dve", lambda e, co=co, b=b: e.scalar_tensor_tensor(
                                out=X[:, co, c0:c0 + w], in0=PS[b][:, :w], scalar=cs("conv_b2", co), in1=X[:, co, c0:c0 + w],
                                op0=ALU.add, op1=ALU.add), reads=[PSB[b], XB[t], CB], writes=[XB[t]])
                            bfree(b)
                    lnorm(0)
                    for t in range(NT):
                        if t + 1 < NT:
                            lnorm(t + 1)
                        pw2(t)
                    halo_fix()
                    stg = [psc(f"c_o{k}", [128, 512]) for k in range(2)]
                    sbufs = [Buf() for _ in range(2)]
                    for half in range(2):
                        b = bank()
                        for c4 in range(4):
                            transpose(PS[b][:30 + NS, c4 * 128:(c4 + 1) * 128], TL[:, half * 4 + c4, :], IDF, reads=[TLB, CB], wbuf=PSB[b], inc=(c4 == 3))
                        S.op("act", lambda e, half=half, b=b: e.activation(out=stg[half][:30 + NS, :], in_=PS[b][:30 + NS, :], func=AF.Copy),
                             reads=[PSB[b]], writes=[sbufs[half]])
                        S.dma("sp", o_conv_p[:, half * 512:(half + 1) * 512], stg[half][0:30, :], reads=[sbufs[half]], writes=[drb("o_conv_p")])
                        S.dma("sp", o_conv_s[:, 29, half * 512:(half + 1) * 512], stg[half][30:30 + NS, :], reads=[sbufs[half]],
                              writes=[drb("o_conv_s")])
                    S.dma("sp", o_conv_s[:, 0:29, :], stconv_d.rearrange("(s r) d -> s r d", r=30)[:, 1:30, :], writes=[drb("o_conv_s")])
                    end_phase(nxt_phase)

        S0 = [56, 0, 192, 224]
        S1 = [56, 184, 224, 248]
        for i in range(n_layers):
            kind, j = i % 3, i // 3
            TILES[0] = (S0[i], 512 - S0[i])
            if kind == 0:
                pool_phase(i, j, nxt_phase=("mem", i))
            elif kind == 1:
                swa_phase(i, nxt_phase=("mem", i))
            else:
                conv_phase(i, nxt_phase=("mem", i))
            dump_x()
            if stop_after == (i, 0):
                break
            TILES[0] = (S1[i], 512 - S1[i])
            mem_phase(i, nxt_phase=("ffn", i))
            dump_x()
            if stop_after == (i, 1):
                break
            ffn_phase(i, nxt_phase=(({0: "pool", 1: "swa", 2: "conv"}[(i + 1) % 3], i + 1) if i + 1 < n_layers else None))
            dump_x()
            if stop_after == (i, 2):
                break

        TILES[0] = (0, 512)
        with ExitStack() as ph:
            psb = lambda name, shape, dt=F32: ph.enter_context(nc.sbuf_tensor(uniq(name), list(shape), dt))
            OST = [psb(f"o_st{k}", [128, D]) for k in range(2)]; OSTB = [Buf() for _ in range(2)]
            nblk = MAIN // 128
            for blk in range(nblk + 1):
                s = blk % 2
                rows = 128 if blk < nblk else NS
                col0 = HALO + blk * 128
                t = min(col0 // 512, NT - 1)
                for half in range(2):
                    b = bank()
                    for c4 in range(4):
                        c = half * 4 + c4
                        transpose(PS[b][:rows, c4 * 128:(c4 + 1) * 128], X[:, c, col0:col0 + rows], IDF,
                                  reads=[XB[t], CB], wbuf=PSB[b], inc=(c4 == 3))
                    if half == 0:
                        S.op("act", lambda e, s=s, b=b, rows=rows: e.activation(out=OST[s][:rows, 0:512], in_=PS[b][:rows, :],
                                                                              func=AF.Copy), reads=[PSB[b]], writes=[OSTB[s]])
                    else:
                        S.op("dve", lambda e, s=s, b=b, rows=rows: e.tensor_copy(out=OST[s][:rows, 512:1024], in_=PS[b][:rows, :]),
                             reads=[PSB[b]], writes=[OSTB[s]])
                dst = y_p[blk * 128:(blk + 1) * 128, :] if blk < nblk else y_s
                S.dma("sp", dst, OST[s][:rows, :], reads=[OSTB[s]], writes=[drb("y")])
        S.finish()
        nc._n_ins = S.n_ins
        nc._cnts = {k: (v.cnt, list(v.dcnt)) for k, v in S.E.items()}
    return nc


def make_in_maps(inp, cores=range(8)):
    f32 = lambda a: np.ascontiguousarray(np.asarray(a, dtype=np.float32))
    maps = []
    meta = None
    perm = []
    for (ha, hb) in _qperm():
        perm += list(range(ha * 64, ha * 64 + 64)) + list(range(hb * 64, hb * 64 + 64))
    perm = np.array(perm)
    wqkv = f32(inp["attn_w_qkv"])[0]
    wqkv_p = np.ascontiguousarray(np.concatenate([wqkv[:, :1024][:, perm], wqkv[:, 1024:]], axis=1))
    wo_p = np.ascontiguousarray(f32(inp["attn_w_o"])[0][perm, :])
    shared = {
        "pool_w": f32(inp["pool_w"]),
        "attn_w_qkv": wqkv_p,
        "attn_w_o": wo_p,
        "conv_w_pw1": f32(inp["conv_w_pw1"])[0],
        "conv_w_pw2": f32(inp["conv_w_pw2"])[0],
        "mem_w_q": f32(inp["mem_w_q"]),
        "mem_w_kv": f32(inp["mem_w_kv"]),
        "mem_w_o": f32(inp["mem_w_o"]),
        "mem_k_norm_b": np.ascontiguousarray(np.tile(f32(inp["mem_k_norm"]).reshape(1, -1), (128, 1))),
        "norm_src_b": np.ascontiguousarray(np.tile(f32(inp["norm_src"]).reshape(1, -1), (128, 1))),
        "ffn_w_up": f32(inp["ffn_w_up"]),
        "ffn_w_down": f32(inp["ffn_w_down"]),
    }
    xp_all = f32(inp["x_prompt"])
    for core in cores:
        n, q = divmod(core, 4)
        coff, cst, boff, cbf, cos, sin = build_consts(inp, core)
        meta = (coff, cst.shape[1], boff, cbf.shape[1])
        xp = np.zeros((TP, D), np.float32)
        s0 = q * MAIN - HALO
        if q == 0:
            xp[HALO:] = xp_all[n, 0:MAIN]
        else:
            xp[:] = xp_all[n, s0:s0 + TP]
        sl = slice(core * NS, (core + 1) * NS)
        m = dict(shared)
        m.update({
            "xp": xp,
            "xs": f32(inp["x_sample"])[sl, 0, :],
            "memp": f32(inp["mem_prompt"])[n],
            "st_pool": np.ascontiguousarray(f32(inp["state_pool"])[:, sl].reshape(2, NS * 15, D)),
            "st_conv": np.ascontiguousarray(f32(inp["state_conv"])[0, sl].reshape(NS * 30, D)),
            "st_ffn": np.ascontiguousarray(f32(inp["state_ffn"])[:, sl].reshape(DEPTH, NS * 2, FFN)),
            "cwk": np.ascontiguousarray(f32(inp["cache_win_k"])[0, sl].reshape(NS, 128, 256)),
            "cwv": np.ascontiguousarray(f32(inp["cache_win_v"])[0, sl].reshape(NS, 128, 256)),
            "cmk": np.ascontiguousarray(f32(inp["cache_mem_k"])[:, sl].reshape(DEPTH, NS, 256, D)),
            "cmv": np.ascontiguousarray(f32(inp["cache_mem_v"])[:, sl].reshape(DEPTH, NS, 256, D)),
            "cst": cst, "cbf": cbf, "cos": cos, "sin": sin,
        })
        maps.append(m)
    return maps, meta


def assemble(results):
    R = results
    y_p = np.stack([np.concatenate([R[n * 4 + q]["y_p"] for q in range(4)], 0) for n in range(2)])
    y_s = np.concatenate([R[c]["y_s"] for c in range(8)], 0)[:, None, :]
    pool_p = np.stack([R[n * 4 + 3]["o_pool_p"] for n in range(2)], 1)
    pool_s = np.concatenate([R[c]["o_pool_s"] for c in range(8)], 1)
    wk_p = np.stack([R[n * 4 + 3]["o_wk_p"].reshape(128, 4, 64) for n in range(2)])[None]
    wv_p = np.stack([R[n * 4 + 3]["o_wv_p"].reshape(128, 4, 64) for n in range(2)])[None]
    wk_s = np.concatenate([R[c]["o_wk_s"].reshape(NS, 128, 4, 64) for c in range(8)], 0)[None]
    wv_s = np.concatenate([R[c]["o_wv_s"].reshape(NS, 128, 4, 64) for c in range(8)], 0)[None]
    conv_p = np.stack([R[n * 4 + 3]["o_conv_p"] for n in range(2)])[None]
    conv_s = np.concatenate([R[c]["o_conv_s"] for c in range(8)], 0)[None]
    ffn_p = np.stack([R[n * 4 + 3]["o_ffn_p"] for n in range(2)], 1)
    ffn_s = np.concatenate([R[c]["o_ffn_s"].reshape(DEPTH, NS, 2, FFN) for c in range(8)], 1)
    mk = np.stack([R[n * 4]["o_mk"].reshape(DEPTH, 256, 4, 256) for n in range(2)], 1)
    mv = np.stack([R[n * 4]["o_mv"].reshape(DEPTH, 256, 4, 256) for n in range(2)], 1)
    outs = (y_p, y_s, pool_p, pool_s, wk_p, wv_p, wk_s, wv_s, conv_p, conv_s, ffn_p, ffn_s, mk, mv)
    return tuple(np.ascontiguousarray(o, dtype=np.float32) for o in outs)


def kernel(**inputs):
    maps, (coff, ncst, boff, nbf) = make_in_maps(inputs)
    nc = build_program(coff, ncst, boff, nbf)
    res = run_bass_kernel_spmd(nc, maps, core_ids=list(range(8)))
    return assemble(res.results)
```

```python
import numpy as np
import ml_dtypes
from contextlib import ExitStack
import concourse.bass as bass
import concourse.mybir as mybir
from concourse.bass_utils import run_bass_kernel_spmd

F32 = mybir.dt.float32
BF16 = mybir.dt.bfloat16
AF = mybir.ActivationFunctionType
ALU = mybir.AluOpType
AX = mybir.AxisListType

D = 1024
NCH = 8
HALO = 256
MAIN = 2048
TP = HALO + MAIN
NS = 16
T = TP + NS
TILES = [(0, 512), (512, 512), (1024, 512), (1536, 512), (2048, 272)]
NT = len(TILES)
FFN = 2816
NHC = 22
EPS = 1e-6
DEPTH = 4
NSLOT = 6
SLOT = 4096


class Buf:
    __slots__ = ("name", "w", "r")

    def __init__(self, name=""):
        self.name = name
        self.w = None
        self.r = {}


class _Eng:
    def __init__(self, name, h, sem, is_pe=False):
        self.name = name
        self.h = h
        self.sem = sem
        self.cnt = 0
        self.waited = {}
        self.is_pe = is_pe
        self.dsems = []
        self.dcnt = []
        self.dnext = 0


class Sched:
    def __init__(self, nc, es, n_dma_sems=8):
        self.nc = nc
        mk = lambda n: es.enter_context(nc.semaphore(n))
        self.E = {
            "pe": _Eng("pe", nc.tensor, mk("c_pe"), is_pe=True),
            "act": _Eng("act", nc.scalar, mk("c_act")),
            "dve": _Eng("dve", nc.vector, mk("c_dve")),
            "pool": _Eng("pool", nc.gpsimd, mk("c_pool")),
            "sp": _Eng("sp", nc.sync, mk("c_sp")),
        }
        for q in ("sp", "pool"):
            e = self.E[q]
            for i in range(n_dma_sems):
                e.dsems.append(mk(f"d_{q}{i}"))
                e.dcnt.append(0)
        self.n_ins = 0

    def _wait(self, e, toks):
        best = {}
        for t in toks:
            if t is None:
                continue
            s, v = t
            k = id(s)
            if k not in best or best[k][1] < v:
                best[k] = (s, v)
        for k, (s, v) in best.items():
            if s is e.sem and e.is_pe:
                continue
            if e.waited.get(k, 0) >= v:
                continue
            e.h.wait_ge(s, v)
            e.waited[k] = v
            self.n_ins += 1

    @staticmethod
    def _deps(reads, writes):
        toks = []
        for b in reads:
            toks.append(b.w)
        for b in writes:
            toks.append(b.w)
            toks.extend(b.r.values())
        return toks

    @staticmethod
    def _mark(tok, reads, writes):
        for b in writes:
            b.w = tok
            b.r = {}
        for b in reads:
            if b in writes:
                continue
            k = id(tok[0])
            if k not in b.r or b.r[k][1] < tok[1]:
                b.r[k] = tok

    def op(self, eng, fn, reads=(), writes=(), inc=True):
        e = self.E[eng]
        self._wait(e, self._deps(reads, writes))
        ins = fn(e.h)
        self.n_ins += 1
        if inc:
            e.cnt += 1
            ins.then_inc(e.sem, 1)
            tok = (e.sem, e.cnt)
        else:
            tok = (e.sem, e.cnt + 1)
        self._mark(tok, reads, writes)
        return ins

    def dma(self, q, out, in_, reads=(), writes=(), **kw):
        e = self.E[q]
        toks = self._deps(reads, writes)
        i = e.dnext
        e.dnext = (e.dnext + 1) % len(e.dsems)
        s = e.dsems[i]
        if e.dcnt[i] > 0:
            toks.append((s, e.dcnt[i]))
        self._wait(e, toks)
        e.dcnt[i] += 16
        ins = e.h.dma_start(out=out, in_=in_, **kw)
        ins.then_inc(s, 16)
        self.n_ins += 1
        tok = (s, e.dcnt[i])
        self._mark(tok, reads, writes)
        return tok

    def all_tokens(self):
        toks = []
        for q in ("sp", "pool"):
            qe = self.E[q]
            for s, c in zip(qe.dsems, qe.dcnt):
                if c > 0:
                    toks.append((s, c))
        for n in ("pe", "act", "dve", "pool"):
            x = self.E[n]
            if x.cnt > 0:
                toks.append((x.sem, x.cnt))
        return toks

    def barrier(self, toks=None):
        toks = [t for t in (self.all_tokens() if toks is None else toks)]
        for n in ("pe", "act", "dve", "pool", "sp"):
            e = self.E[n]
            mine = [t for t in toks if not (t[0] is e.sem)]
            if e.is_pe:
                pass
            own = [t for t in toks if t[0] is e.sem]
            self._wait(e, mine + (own if (not e.is_pe and n != "sp") else []))

    def finish(self):
        self._wait(self.E["sp"], self.all_tokens())


def _fm(v):
    v = np.asarray(v, np.float32)
    lead = int(np.prod(v.shape[:-1])) if v.ndim > 1 else 1
    n = v.shape[-1] // 128
    return v.reshape(lead, n, 128).transpose(2, 0, 1).reshape(128, lead * n)


def _qperm():
    heads = []
    for c in range(8):
        g2, cc = divmod(c, 4)
        heads.append((g2 * 8 + cc, g2 * 8 + 4 + cc))
    return heads


class CMap:
    def __init__(self):
        self.off = {}
        self.n = 0
        self.parts = []

    def add(self, name, arr):
        arr = np.ascontiguousarray(arr, dtype=np.float32)
        assert arr.shape[0] == 128, (name, arr.shape)
        self.off[name] = (self.n, arr.shape[1])
        self.n += arr.shape[1]
        self.parts.append(arr)

    def pack(self):
        return np.ascontiguousarray(np.concatenate(self.parts, axis=1))


def build_consts(inp, core):
    q = core % 4
    first = (q == 0)
    cm = CMap()
    cm.add("ident", np.eye(128, dtype=np.float32))
    cm.add("eps", np.full((128, 1), EPS, np.float32))
    for nm in ("norm_mix", "norm_mem", "norm_src", "norm_ffn"):
        cm.add(nm, _fm(inp[nm]))
    cm.add("pool_scale", _fm(inp["pool_scale"]))
    mq = np.asarray(inp["mem_q_norm"], np.float32)
    cm.add("mem_q_norm", _fm(np.tile(mq, (1, 4))))
    aq = np.asarray(inp["attn_q_norm"], np.float32)[0]
    ak = np.asarray(inp["attn_k_norm"], np.float32)[0]
    cm.add("attn_qn", np.tile(aq, 2)[:, None])
    cm.add("attn_kn", np.tile(ak, 2)[:, None])
    sk = np.asarray(inp["attn_sinks"], np.float32)[0]
    sinks = np.zeros((128, 8), np.float32)
    for c, (ha, hb) in enumerate(_qperm()):
        sinks[:64, c] = sk[ha]
        sinks[64:, c] = sk[hb]
    cm.add("sinks", sinks)
    cm.add("conv_b1", _fm(inp["conv_b_pw1"]))
    wdw = np.asarray(inp["conv_w_dw"], np.float32)[0]
    cm.add("conv_wdw", wdw.reshape(31, 8, 128).transpose(2, 1, 0).reshape(128, 8 * 31))
    cm.add("conv_bdw", _fm(inp["conv_b_dw"]))
    cm.add("conv_lng", _fm(inp["conv_ln_g"]))
    cm.add("conv_lnb", _fm(inp["conv_ln_b"]))
    cm.add("conv_b2", _fm(inp["conv_b_pw2"]))
    cm.add("ffn_wdw", _fm(inp["ffn_w_dw"]))
    cm.add("ffn_bdw", _fm(inp["ffn_b_dw"]))
    hm = np.zeros((128, HALO), np.float32) if first else np.ones((128, HALO), np.float32)
    cm.add("hmask", hm)
    s_ = np.arange(128)[:, None]
    q_ = np.arange(128)[None, :]
    cm.add("mask_cur", (s_ <= q_).astype(np.float32))
    cm.add("mask_prev", (s_ > q_).astype(np.float32))
    cm.add("mask_prev_h", (s_ > q_).astype(np.float32) * (0.0 if first else 1.0))
    rc = np.zeros((128, 4 * 16), np.float32)
    for g, w in enumerate((2, 4, 8, 16)):
        pos = np.arange(16) if first else np.full(16, 10 ** 6)
        rc[:, g * 16:(g + 1) * 16] = (w / np.minimum(pos + 1, w))[None, :]
    cm.add("pool_rc16", rc)
    cst = cm.pack()

    bm = {}
    parts = []
    nb = 0

    def badd(name, arr):
        nonlocal nb
        arr = np.asarray(arr, np.float32)
        bm[name] = (nb, arr.shape[1])
        nb += arr.shape[1]
        parts.append(arr)

    badd("ident", np.eye(128))
    badd("ones1024", np.full((128, 128), 1.0 / 1024))
    badd("ones256", np.full((128, 128), 1.0 / 256))
    badd("ones1", np.ones((128, 128)))
    bd = np.zeros((128, 128))
    bd[:64, :64] = 1.0 / 64
    bd[64:, 64:] = 1.0 / 64
    badd("bd64", bd)
    rot = np.zeros((128, 128))
    for dst in range(128):
        if (dst % 64) < 32:
            rot[dst + 32, dst] = -1.0
        else:
            rot[dst - 32, dst] = 1.0
    badd("rot", rot)
    e16 = np.zeros((128, 16 * 128))
    for s in range(16):
        e16[s, s * 128:(s + 1) * 128] = 1.0
    s_ = np.arange(128)[:, None]
    q_ = np.arange(128)[None, :]
    NEG = -30000.0
    badd("mb_cur", np.where(s_ <= q_, 0.0, NEG))
    badd("mb_prev", np.where(s_ > q_, 0.0, NEG))
    badd("mb_prev_h", np.where(s_ > q_, 0.0, NEG) if not first else np.full((128, 128), NEG))
    badd("e16", e16)
    cbf = np.concatenate(parts, axis=1).astype(ml_dtypes.bfloat16)
    start = q * MAIN - HALO
    pos = np.concatenate([np.arange(start, start + TP), np.full(NS, 8192)]).astype(np.float32)
    inv = (10000.0 ** (-np.arange(32, dtype=np.float32) * np.float32(2.0 / 64))).astype(np.float32)
    ang = (pos[None, :] * inv[:, None]).astype(np.float32)
    cos = np.tile(np.cos(ang).astype(np.float32), (4, 1))
    sin = np.tile(np.sin(ang).astype(np.float32), (4, 1))
    return cm.off, cst, bm, cbf, np.ascontiguousarray(cos), np.ascontiguousarray(sin)


def build_program(coff, ncst, boff, nbf, n_layers=DEPTH, dump=False, stop_after=None):
    nc = bass.Bass("TRN2", target_bir_lowering=False)

    def din(name, shape, dt=F32):
        return nc.dram_tensor(name, list(shape), dt, kind="ExternalInput").ap()

    def dout(name, shape):
        return nc.dram_tensor(name, list(shape), F32, kind="ExternalOutput").ap()

    xp_d = din("xp", [TP, D])
    xs_d = din("xs", [NS, D])
    memp_d = din("memp", [256, D])
    stpool_d = din("st_pool", [2, NS * 15, D])
    stconv_d = din("st_conv", [NS * 30, D])
    stffn_d = din("st_ffn", [DEPTH, NS * 2, FFN])
    cwk_d = din("cwk", [NS, 128, 256])
    cwv_d = din("cwv", [NS, 128, 256])
    cmk_d = din("cmk", [DEPTH, NS, 256, D])
    cmv_d = din("cmv", [DEPTH, NS, 256, D])
    cst_d = din("cst", [128, ncst])
    cbf_d = din("cbf", [128, nbf], BF16)
    cos_d = din("cos", [128, T])
    sin_d = din("sin", [128, T])
    pool_w_d = din("pool_w", [2, 4, 256, 256])
    wqkv_d = din("attn_w_qkv", [D, 1536])
    wo_d = din("attn_w_o", [D, D])
    pw1_d = din("conv_w_pw1", [D, 2 * D])
    pw2_d = din("conv_w_pw2", [D, D])
    mwq_d = din("mem_w_q", [DEPTH, D, D])
    mwkv_d = din("mem_w_kv", [DEPTH, D, 2 * D])
    mwo_d = din("mem_w_o", [DEPTH, D, D])
    mkn_d = din("mem_k_norm_b", [128, DEPTH * 256])
    nsrc_d = din("norm_src_b", [128, DEPTH * D])
    wup_d = din("ffn_w_up", [DEPTH, D, 2 * FFN])
    wdn_d = din("ffn_w_down", [DEPTH, FFN, D])

    y_p = dout("y_p", [MAIN, D])
    y_s = dout("y_s", [NS, D])
    o_pool_p = dout("o_pool_p", [2, 15, D])
    o_pool_s = dout("o_pool_s", [2, NS, 15, D])
    o_wk_p = dout("o_wk_p", [128, 256])
    o_wv_p = dout("o_wv_p", [128, 256])
    o_wk_s = dout("o_wk_s", [NS, 128, 256])
    o_wv_s = dout("o_wv_s", [NS, 128, 256])
    o_conv_p = dout("o_conv_p", [30, D])
    o_conv_s = dout("o_conv_s", [NS, 30, D])
    o_ffn_p = dout("o_ffn_p", [DEPTH, 2, FFN])
    o_ffn_s = dout("o_ffn_s", [DEPTH, NS * 2, FFN])
    o_mk = dout("o_mk", [DEPTH, 256, D])
    o_mv = dout("o_mv", [DEPTH, 256, D])
    NDUMP = 3 * DEPTH
    dbg = dout("dbg", [NDUMP, 128, NCH * T]) if dump else None

    es = ExitStack()
    with es:
        S = Sched(nc, es)
        _uc = [0]

        def uniq(name):
            _uc[0] += 1
            return f"{name}_{_uc[0]}"

        sb = lambda name, shape, dt=F32: es.enter_context(nc.sbuf_tensor(uniq(name), list(shape), dt))
        X = sb("X", [128, NCH, T])
        XB = [Buf(f"X{t}") for t in range(NT)]
        CST = sb("CST", [128, ncst])
        CBF = sb("CBF", [128, nbf], BF16)
        CB = Buf("const")
        WR = [sb(f"wr{i}", [128, SLOT], BF16) for i in range(NSLOT)]
        WB = [Buf(f"wr{i}") for i in range(NSLOT)]
        PS = [es.enter_context(nc.psum_tensor(f"ps{i}", [128, 512], F32)) for i in range(8)]
        PSB = [Buf(f"ps{i}") for i in range(8)]
        st = {"bank": 0, "slot": 0, "dump": 0}
        DR = {}

        def drb(name):
            if name not in DR:
                DR[name] = Buf(name)
            return DR[name]

        def cs(name, j=0, n=1):
            o, w = coff[name]
            return CST[:, o + j:o + j + n]

        def cb(name, rows=128, j=0, n=128):
            o, w = boff[name]
            return CBF[:rows, o + j:o + j + n]

        free_banks = list(range(8))

        def balloc():
            assert free_banks, "out of PSUM banks"
            return free_banks.pop(0)

        def bfree(b):
            free_banks.append(b)

        def bank():
            b = balloc()
            bfree(b)
            return b

        def wload(src, kc, n):
            s = st["slot"]
            st["slot"] = (s + 1) % NSLOT
            view = WR[s][:, :kc * n].rearrange("p (k n) -> p k n", k=kc)
            S.dma("pool", view, src, writes=[WB[s]])
            return view, WB[s]

        def wsrc(W, n0, n, r0=0, kc=8):
            return W[r0 * 128:(r0 + kc) * 128, n0:n0 + n].rearrange("(k p) n -> p k n", p=128)

        PRE = {}

        def loaders(kind, i):
            j = i // 3
            if kind == "pool":
                return {("pool", j): lambda: wload(pool_w_d[j].rearrange("g (k p) n -> p (g k) n", p=128), 8, 256)}
            if kind == "swa":
                d = {("wqkv", u): (lambda u=u: wload(wsrc(wqkv_d, u * 512, 512), 8, 512)) for u in range(3)}
                d.update({("wo", u): (lambda u=u: wload(wsrc(wo_d, u * 512, 512), 8, 512)) for u in range(2)})
                return d
            if kind == "conv":
                d = {("w1", u): (lambda u=u: wload(wsrc(pw1_d, u * 512, 512), 8, 512)) for u in range(4)}
                d.update({("w2", u): (lambda u=u: wload(wsrc(pw2_d, u * 512, 512), 8, 512)) for u in range(2)})
                return d
            if kind == "mem":
                return {("mkv", i, u): (lambda u=u: wload(wsrc(mwkv_d[i], u * 512, 512), 8, 512)) for u in range(4)}
            if kind == "ffn":
                return {("ffng", i, k): f for k, f in enumerate([
                    lambda: wload(wsrc(wup_d[i], 0, 512), 8, 512),
                    lambda: wload(wsrc(wup_d[i], FFN, 512), 8, 512),
                    lambda: wload(wsrc(wdn_d[i], 0, D, r0=0, kc=4), 4, D)])}
            return {}

        def take(key, loader):
            if key in PRE:
                return PRE.pop(key)
            return loader()

        def end_phase(nxt_phase):
            tk = S.all_tokens()
            if nxt_phase is not None:
                for key, ld in loaders(*nxt_phase).items():
                    PRE[key] = ld()
            S.barrier(tk)

        def mm(out, pairs, reads, wbuf, inc_last=True):
            n = len(pairs)
            for i, (l, r) in enumerate(pairs):
                S.op("pe", lambda e, l=l, r=r, i=i: e.matmul(out, lhsT=l, rhs=r, start=(i == 0), stop=(i == n - 1)),
                     reads=reads, writes=[wbuf], inc=(i == n - 1) and inc_last)

        def transpose(out, in_, ident, reads, wbuf, inc=True):
            S.op("pe", lambda e: e.transpose(out, in_, ident), reads=reads, writes=[wbuf], inc=inc)

        def dump_x():
            if dbg is None:
                return
            k = st["dump"]
            st["dump"] += 1
            S.dma("sp", dbg[k], X[:].rearrange("p c t -> p (c t)"), reads=XB, writes=[drb("dbg")])

        def halo_fix():
            o, _ = coff["hmask"]
            hm = CST[:, o:o + HALO].unsqueeze(1).broadcast_to([128, NCH, HALO])
            S.op("dve", lambda e: e.tensor_tensor(out=X[:, :, 0:HALO], in0=X[:, :, 0:HALO], in1=hm, op=ALU.mult),
                 reads=[XB[0], CB], writes=[XB[0]])

        S.dma("sp", CST[:], cst_d, writes=[CB])
        S.dma("sp", CBF[:], cbf_d, writes=[CB])
        IDF = cs("ident", 0, 128)

        with ExitStack() as ph:
            psb = lambda name, shape, dt=F32: ph.enter_context(nc.sbuf_tensor(uniq(name), list(shape), dt))
            STG = [psb(f"stg{i}", [128, D]) for i in range(2)]
            STGB = [Buf() for _ in range(2)]
            nblk = TP // 128
            for blk in range(nblk + 1):
                s = blk % 2
                rows = 128 if blk < nblk else NS
                src = xp_d[blk * 128:(blk + 1) * 128, :] if blk < nblk else xs_d
                col0 = blk * 128
                S.dma("sp", STG[s][:rows, :], src, writes=[STGB[s]])
                for half in range(2):
                    b = bank()
                    for c4 in range(4):
                        c = half * 4 + c4
                        transpose(PS[b][:, c4 * 128:c4 * 128 + rows], STG[s][:rows, c * 128:(c + 1) * 128],
                                  IDF[:rows, :rows], reads=[STGB[s], CB], wbuf=PSB[b], inc=(c4 == 3))
                    t = min(col0 // 512, NT - 1)
                    src_v = PS[b][:].rearrange("p (c r) -> p c r", c=4)[:, :, :rows]
                    eng = "act" if half == 0 else "dve"
                    if eng == "act":
                        S.op("act", lambda e, src_v=src_v, half=half, col0=col0, rows=rows: e.activation(
                            out=X[:, half * 4:half * 4 + 4, col0:col0 + rows], in_=src_v, func=AF.Copy),
                            reads=[PSB[b]], writes=[XB[t]])
                    else:
                        S.op("dve", lambda e, src_v=src_v, half=half, col0=col0, rows=rows: e.tensor_copy(
                            out=X[:, half * 4:half * 4 + 4, col0:col0 + rows], in_=src_v),
                            reads=[PSB[b]], writes=[XB[t]])
            S.barrier()

        def recip_act(out, in_, reads, wbuf):
            S.op("act", lambda e: e.activation(out=out, in_=in_, func=AF.Ln), reads=list(reads), writes=[wbuf])
            S.op("act", lambda e: e.activation(out=out, in_=out, func=AF.Exp, scale=-1.0), reads=[wbuf], writes=[wbuf])

        def rstd_act(out, in_, reads, wbuf, scale=1.0):
            S.op("act", lambda e: e.activation(out=out, in_=in_, func=AF.Ln, bias=cs("eps"), scale=scale),
                 reads=list(reads) + [CB], writes=[wbuf])
            S.op("act", lambda e: e.activation(out=out, in_=out, func=AF.Exp, scale=-0.5), reads=[wbuf], writes=[wbuf])

        def rms_cols(c0, w, xb, gname, gcol, Hout, HB, SQ, SQB, R, RB):
            b = bank()
            for c in range(NCH):
                S.op("act", lambda e, c=c: e.activation(out=SQ[c % 4][:, :w], in_=X[:, c, c0:c0 + w], func=AF.Square),
                     reads=[xb], writes=[SQB[c % 4]])
                S.op("pe", lambda e, c=c: e.matmul(PS[b][:, :w], lhsT=cb("ones1024"), rhs=SQ[c % 4][:, :w], start=(c == 0),
                                                   stop=(c == NCH - 1)), reads=[SQB[c % 4], CB], writes=[PSB[b]], inc=True)
            rstd_act(R[:, :w], PS[b][:, :w], [PSB[b]], RB)
            for c in range(NCH):
                S.op("dve", lambda e, c=c: e.scalar_tensor_tensor(
                    out=Hout[:, c, :w], in0=X[:, c, c0:c0 + w], scalar=cs(gname, gcol + c), in1=R[:, :w],
                    op0=ALU.mult, op1=ALU.mult), reads=[xb, RB, CB], writes=[HB])

        def rms_tile(t, gname, gcol, Hout, HB, SQ, SQB, R, RB):
            rms_cols(TILES[t][0], TILES[t][1], XB[t], gname, gcol, Hout, HB, SQ, SQB, R, RB)

        def resid_add(t, co, b, scale=None, bias=None):
            c0, w = TILES[t]
            if scale is not None:
                S.op("dve", lambda e: e.scalar_tensor_tensor(out=X[:, co, c0:c0 + w], in0=PS[b][:, :w], scalar=scale,
                                                             in1=X[:, co, c0:c0 + w], op0=ALU.mult, op1=ALU.add),
                     reads=[PSB[b], XB[t], CB], writes=[XB[t]])
            elif bias is not None:
                S.op("dve", lambda e: e.scalar_tensor_tensor(out=X[:, co, c0:c0 + w], in0=PS[b][:, :w], scalar=bias,
                                                             in1=X[:, co, c0:c0 + w], op0=ALU.add, op1=ALU.add),
                     reads=[PSB[b], XB[t], CB], writes=[XB[t]])
            else:
                S.op("dve", lambda e: e.tensor_tensor(out=X[:, co, c0:c0 + w], in0=X[:, co, c0:c0 + w],
                                                      in1=PS[b][:, :w], op=ALU.add),
                     reads=[PSB[b], XB[t]], writes=[XB[t]])

        def load_fm(ph_sb, rows_ap, R_, ncols, dst_fn, name, split=None):
            if isinstance(ph_sb, tuple):
                stg, sbuf = ph_sb
            else:
                stg = ph_sb(name, [128, 512])
                sbuf = Buf(name)
            for g0 in range(0, ncols, 512):
                wg = min(512, ncols - g0)
                ng = wg // 128
                S.dma("sp", stg[:R_, :wg], rows_ap[:, g0:g0 + wg], writes=[sbuf])
                b = bank()
                for j in range(ng):
                    transpose(PS[b][:, j * R_:(j + 1) * R_], stg[:R_, j * 128:(j + 1) * 128], IDF[:R_, :R_],
                              reads=[sbuf, CB], wbuf=PSB[b], inc=(j == ng - 1))
                dst, dbuf = dst_fn(g0 // 128, ng)
                if split is None:
                    src = PS[b][:, :ng * R_].rearrange("p (c r) -> p c r", c=ng)
                else:
                    src = PS[b][:, :ng * R_].rearrange("p (c s r) -> p c s r", c=ng, s=split[0], r=split[1])
                S.op("act", lambda e, dst=dst, src=src: e.activation(out=dst, in_=src, func=AF.Copy),
                     reads=[PSB[b]], writes=[dbuf])

        def store_tm(ph_sb, src_fn, R_, ncols, dst_ap, dbufname, name, reads):
            stg = [ph_sb(name + str(i), [128, 512]) for i in range(2)]
            sbufs = [Buf() for _ in range(2)]
            k = 0
            for g0 in range(0, ncols, 512):
                wg = min(512, ncols - g0)
                ng = wg // 128
                b = bank()
                for j in range(ng):
                    transpose(PS[b][:R_, j * 128:(j + 1) * 128], src_fn(g0 // 128 + j), IDF, reads=reads + [CB],
                              wbuf=PSB[b], inc=(j == ng - 1))
                s = k % 2
                k += 1
                S.op("act", lambda e, s=s, b=b, wg=wg: e.activation(out=stg[s][:R_, :wg], in_=PS[b][:R_, :wg],
                                                                  func=AF.Copy), reads=[PSB[b]], writes=[sbufs[s]])
                S.dma("sp", dst_ap[:, g0:g0 + wg], stg[s][:R_, :wg], reads=[sbufs[s]], writes=[drb(dbufname)])

        def pool_phase(i, j, nxt_phase=None):
            with ExitStack() as ph:
                psb = lambda name, shape, dt=F32: ph.enter_context(nc.sbuf_tensor(uniq(name), list(shape), dt))
                SQ = [psb(f"p_sq{k}", [128, 512], BF16) for k in range(4)]; SQB = [Buf() for _ in range(4)]
                RR = psb("p_r", [128, 512]); RRB = Buf()
                DF = [psb(f"p_d{k}", [128, NCH, 512], BF16) for k in range(2)]; DFB = [Buf() for _ in range(2)]
                HC = [psb(f"p_hc{k}", [128, 16 + 512]) for k in range(8)]; HCB = [Buf() for _ in range(8)]
                RES = [psb(f"p_res{k}", [128, 16 + 512]) for k in range(2)] * 4; RESB = [Buf() for _ in range(2)] * 4
                HCHB = [Buf() for _ in range(NCH)]
                ZERO16 = psb("p_z16", [128, 16])
                S.op("pool", lambda e: e.memset(ZERO16[:, :], 0.0), writes=[CB])
                SA = [psb(f"p_sa{k}", [128, 16 + 512]) for k in range(2)]; SAB = [Buf() for _ in range(2)]
                HCAR = psb("p_car", [128, NCH, 16]); HCARB = [Buf() for _ in range(NCH)]
                HS = psb("p_hs", [128, NCH, NS, 16]); HSB = Buf()
                TL = psb("p_tl", [128, NCH, 15 + NS]); TLB = Buf()
                WS = psb("p_ws", [128, 2, NS]); WSB = Buf()
                pw, pwB = take(("pool", j), loaders("pool", i)[("pool", j)])
                pstg = (psb("p_stg", [128, 512]), Buf())
                for half in range(2):
                    load_fm(pstg, stpool_d[j][half * 120:(half + 1) * 120, :], 120, D,
                            lambda c0, ng, half=half: (HS[:, c0:c0 + ng, half * 8:(half + 1) * 8, 0:15], HSB),
                            f"p_stg{half}", split=(8, 15))
                rco, _ = coff["pool_rc16"]
                kk_box = [0]
                def poolA(t):
                    c0, w = TILES[t]
                    last = (t == NT - 1)
                    wp = 256 if last else w
                    b = bank()
                    for c in range(NCH):
                        S.op("act", lambda e, c=c: e.activation(out=SQ[c % 4][:, :w], in_=X[:, c, c0:c0 + w], func=AF.Square),
                             reads=[XB[t]], writes=[SQB[c % 4]])
                        S.op("pe", lambda e, c=c: e.matmul(PS[b][:, :w], lhsT=cb("ones1024"), rhs=SQ[c % 4][:, :w], start=(c == 0),
                                                           stop=(c == NCH - 1)), reads=[SQB[c % 4], CB], writes=[PSB[b]], inc=True)
                    rstd_act(RR[:, :w], PS[b][:, :w], [PSB[b]], RRB)
                    df = DF[t % 2]; dfB = DFB[t % 2]
                    L = 16 + wp
                    for c in range(NCH):
                        hc = HC[c]; hcb = HCB[c]
                        if t == 0:
                            S.op("act", lambda e, hc=hc: e.activation(out=hc[:, 0:16], in_=ZERO16[:, :], func=AF.Copy), reads=[CB], writes=[HCHB[c]])
                        else:
                            S.op("act", lambda e, hc=hc, c=c: e.activation(out=hc[:, 0:16], in_=HCAR[:, c, :], func=AF.Copy),
                                 reads=[HCARB[c]], writes=[HCHB[c]])
                        S.op("dve", lambda e, hc=hc, c=c: e.scalar_tensor_tensor(
                            out=hc[:, 16:16 + w], in0=X[:, c, c0:c0 + w], scalar=cs("norm_mix", i * 8 + c),
                            in1=RR[:, :w], op0=ALU.mult, op1=ALU.mult), reads=[XB[t], RRB, CB], writes=[hcb])
                        if not last:
                            S.op("act", lambda e, hc=hc, c=c: e.activation(out=HCAR[:, c, :], in_=hc[:, w:w + 16], func=AF.Copy),
                                 reads=[hcb], writes=[HCARB[c]])
                        else:
                            S.op("act", lambda e, hc=hc, c=c: e.activation(out=TL[:, c, :], in_=hc[:, 16 + wp - 15:16 + w], func=AF.Copy),
                                 reads=[hcb], writes=[TLB])
                            S.op("act", lambda e, hc=hc, c=c: e.activation(out=HS[:, c, :, 15], in_=hc[:, 16 + wp:16 + w], func=AF.Copy),
                                 reads=[hcb], writes=[HSB])
                    fin = {}

                    def chain(c, eng):
                        wwin = (2, 4, 8, 16)[c // 2]
                        nst = {2: 1, 4: 2, 8: 3, 16: 4}[wwin]
                        cur, curB = HC[c], HCB[c]
                        k = 1
                        first = True
                        for st_ in range(nst):
                            if st_ == nst - 1:
                                nxt, nxtB = RES[c], RESB[c]
                            else:
                                nxt, nxtB = SA[st_ % 2], SAB[st_ % 2]
                            S.op(eng, lambda e, cur=cur, nxt=nxt, k=k: e.tensor_tensor(
                                out=nxt[:, k:L], in0=cur[:, k:L], in1=cur[:, 0:L - k], op=ALU.add),
                                reads=[curB] + ([HCHB[c]] if first else []), writes=[nxtB])
                            first = False
                            cur, curB = nxt, nxtB
                            k *= 2
                        fin[c] = (cur, curB)

                    def diff(c):
                        g = c // 2
                        wwin = (2, 4, 8, 16)[g]
                        cur, curB = fin[c]
                        hc, hcb = HC[c], HCB[c]
                        if t == 0:
                            S.op("dve", lambda e: e.tensor_tensor(
                                out=cur[:, 16 + HALO - c0:16 + HALO - c0 + 16], in0=cur[:, 16 + HALO - c0:16 + HALO - c0 + 16],
                                in1=CST[:, rco + g * 16:rco + (g + 1) * 16], op=ALU.mult), reads=[curB, CB], writes=[curB])
                        S.op("dve", lambda e: e.scalar_tensor_tensor(
                            out=df[:, c, 0:wp], in0=cur[:, 16:16 + wp], scalar=1.0 / wwin, in1=hc[:, 16:16 + wp],
                            op0=ALU.mult, op1=ALU.subtract), reads=[curB, hcb], writes=[dfB])
                        if last:
                            S.op("dve", lambda e: e.tensor_reduce(out=WS[:, c % 2, :], in_=HS[:, c, :, 16 - wwin:16],
                                                                  axis=AX.X, op=ALU.add), reads=[HSB], writes=[WSB])
                            S.op("dve", lambda e: e.scalar_tensor_tensor(
                                out=df[:, c, wp:w], in0=WS[:, c % 2, :], scalar=1.0 / wwin, in1=hc[:, 16 + wp:16 + w],
                                op0=ALU.mult, op1=ALU.subtract), reads=[WSB, hcb], writes=[dfB])
                    for c in range(NCH):
                        chain(c, "dve")
                        diff(c)

                def poolB(t):
                    c0, w = TILES[t]
                    df = DF[t % 2]; dfB = DFB[t % 2]
                    for co in range(NCH):
                        g = co // 2
                        b = bank()
                        mm(PS[b][:, :w], [(pw[:, g * 2 + k, (co % 2) * 128:(co % 2 + 1) * 128], df[:, g * 2 + k, :w])
                                          for k in range(2)], reads=[pwB, dfB], wbuf=PSB[b])
                        resid_add(t, co, b, scale=cs("pool_scale", j * 8 + co))
                poolA(0)
                for t in range(NT):
                    if t + 1 < NT:
                        poolA(t + 1)
                    poolB(t)
                halo_fix()
                stg = [psb(f"p_o{k}", [128, 512]) for k in range(2)]
                sbufs = [Buf() for _ in range(2)]
                for half in range(2):
                    b = bank()
                    for c4 in range(4):
                        transpose(PS[b][:15 + NS, c4 * 128:(c4 + 1) * 128], TL[:, half * 4 + c4, :], IDF,
                                  reads=[TLB, CB], wbuf=PSB[b], inc=(c4 == 3))
                    S.op("act", lambda e, half=half, b=b: e.activation(out=stg[half][:15 + NS, :], in_=PS[b][:15 + NS, :],
                                                                      func=AF.Copy), reads=[PSB[b]], writes=[sbufs[half]])
                    S.dma("sp", o_pool_p[j][:, half * 512:(half + 1) * 512], stg[half][0:15, :], reads=[sbufs[half]],
                          writes=[drb("o_pool_p")])
                    S.dma("sp", o_pool_s[j][:, 14, half * 512:(half + 1) * 512], stg[half][15:15 + NS, :],
                          reads=[sbufs[half]], writes=[drb("o_pool_s")])
                S.dma("sp", o_pool_s[j][:, 0:14, :], stpool_d[j].rearrange("(s r) d -> s r d", r=15)[:, 1:15, :],
                      writes=[drb("o_pool_s")])
                end_phase(nxt_phase)

        def mem_phase(i, nxt_phase=None):
            with ExitStack() as ph:
                psb = lambda name, shape, dt=F32: ph.enter_context(nc.sbuf_tensor(uniq(name), list(shape), dt))
                KT = psb("m_kt", [128, NCH, 256], BF16); KTB = Buf()
                VB_ = psb("m_vb", [128, 2, D], BF16); VBB = Buf()
                ph_kv = ExitStack()
                psbk = lambda name, shape, dt=F32: ph_kv.enter_context(nc.sbuf_tensor(uniq(name), list(shape), dt))
                MT = psbk("m_mt", [128, 2, D]); MTB = Buf()
                MSQ = psbk("m_msq", [128, D]); MSQB = Buf()
                MST = psbk("m_mst", [128, 8]); MSTB = Buf()
                MH = psbk("m_mh", [128, NCH, 256], BF16); MHB = Buf()
                KNB = psbk("m_knb", [128, 2, D], BF16); KNBB = Buf()
                KVO = [psbk(f"m_kvo{k}", [128, 512]) for k in range(2)]; KVOB = [Buf() for _ in range(2)]
                GK = psbk("m_gk", [128, 256]); GKB = Buf()
                GSR = psbk("m_gsr", [128, D]); GSRB = Buf()
                S.dma("sp", MT[:], memp_d.rearrange("(m p) d -> p m d", p=128), writes=[MTB])
                S.dma("sp", GK[:], mkn_d[:, i * 256:(i + 1) * 256], writes=[GKB])
                S.dma("sp", GSR[:], nsrc_d[:, i * D:(i + 1) * D], writes=[GSRB])
                for mc in range(2):
                    S.op("act", lambda e, mc=mc: e.activation(out=MSQ[:], in_=MT[:, mc, :], func=AF.Square,
                                                              accum_out=MST[:, mc:mc + 1]), reads=[MTB], writes=[MSQB, MSTB])
                rstd_act(MST[:, 4:6], MST[:, 0:2], [MSTB], MSTB, scale=1.0 / D)
                for mc in range(2):
                    S.op("dve", lambda e, mc=mc: e.scalar_tensor_tensor(out=MT[:, mc, :], in0=MT[:, mc, :], scalar=MST[:, 4 + mc:5 + mc],
                                                                        in1=GSR[:], op0=ALU.mult, op1=ALU.mult),
                         reads=[MSTB, MTB, GSRB], writes=[MTB])
                for mc in range(2):
                    for half in range(2):
                        b = bank()
                        for c4 in range(4):
                            c = half * 4 + c4
                            transpose(PS[b][:, c4 * 128:(c4 + 1) * 128], MT[:, mc, c * 128:(c + 1) * 128], IDF,
                                      reads=[MTB, CB], wbuf=PSB[b], inc=(c4 == 3))
                        src_v = PS[b][:].rearrange("p (c m) -> p c m", c=4)
                        if half == 0:
                            S.op("act", lambda e, src_v=src_v, half=half, mc=mc: e.activation(
                                out=MH[:, half * 4:half * 4 + 4, mc * 128:(mc + 1) * 128], in_=src_v, func=AF.Copy),
                                reads=[PSB[b]], writes=[MHB])
                        else:
                            S.op("dve", lambda e, src_v=src_v, half=half, mc=mc: e.tensor_copy(
                                out=MH[:, half * 4:half * 4 + 4, mc * 128:(mc + 1) * 128], in_=src_v),
                                reads=[PSB[b]], writes=[MHB])
                for u in range(4):
                    wv_, wvB = take(("mkv", i, u), loaders("mem", i)[("mkv", i, u)])
                    for mc in range(2):
                        b = bank()
                        mm(PS[b][:, :], [(MH[:, k, mc * 128:(mc + 1) * 128], wv_[:, k, :]) for k in range(NCH)],
                           reads=[MHB, wvB], wbuf=PSB[b])
                        o = KVO[(u * 2 + mc) % 2]; ob = KVOB[(u * 2 + mc) % 2]
                        if u < 2:
                            for hh in range(2):
                                S.op("act", lambda e, hh=hh, b=b: e.activation(
                                    out=MSQ[:, hh * 256:(hh + 1) * 256], in_=PS[b][:, hh * 256:(hh + 1) * 256], func=AF.Square,
                                    accum_out=MST[:, hh:hh + 1]), reads=[PSB[b]], writes=[MSQB, MSTB])
                            rstd_act(MST[:, 4:6], MST[:, 0:2], [MSTB], MSTB, scale=1.0 / 256)
                            for hh in range(2):
                                S.op("dve", lambda e, hh=hh, b=b, o=o: e.scalar_tensor_tensor(
                                    out=o[:, hh * 256:(hh + 1) * 256], in0=PS[b][:, hh * 256:(hh + 1) * 256],
                                    scalar=MST[:, 4 + hh:5 + hh], in1=GK[:], op0=ALU.mult, op1=ALU.mult),
                                    reads=[PSB[b], MSTB, GKB], writes=[ob])
                            S.op("act", lambda e, o=o, u=u, mc=mc: e.activation(out=KNB[:, mc, u * 512:(u + 1) * 512], in_=o[:],
                                                                              func=AF.Copy), reads=[ob], writes=[KNBB])
                            S.dma("sp", o_mk[i][mc * 128:(mc + 1) * 128, u * 512:(u + 1) * 512], o[:], reads=[ob],
                                  writes=[drb("o_mk")])
                        else:
                            S.op("act", lambda e, o=o, b=b: e.activation(out=o[:], in_=PS[b][:], func=AF.Copy),
                                 reads=[PSB[b]], writes=[ob])
                            S.op("dve", lambda e, b=b, u=u, mc=mc: e.tensor_copy(
                                out=VB_[:, mc, (u - 2) * 512:(u - 1) * 512], in_=PS[b][:]), reads=[PSB[b]], writes=[VBB])
                            S.dma("sp", o_mv[i][mc * 128:(mc + 1) * 128, (u - 2) * 512:(u - 1) * 512], o[:], reads=[ob],
                                  writes=[drb("o_mv")])
                IDB = cb("ident")
                for mc in range(2):
                    for half in range(2):
                        b = bank()
                        pb = PS[b][:].bitcast(BF16)
                        for c4 in range(4):
                            c = half * 4 + c4
                            transpose(pb[:, c4 * 128:(c4 + 1) * 128], KNB[:, mc, c * 128:(c + 1) * 128], IDB,
                                      reads=[KNBB, CB], wbuf=PSB[b], inc=(c4 == 3))
                        S.op("act", lambda e, pb=pb, half=half, mc=mc: e.activation(
                            out=KT[:, half * 4:half * 4 + 4, mc * 128:(mc + 1) * 128],
                            in_=pb[:, 0:512].rearrange("p (c m) -> p c m", c=4), func=AF.Copy), reads=[PSB[b]], writes=[KTB])
                tk_kv = S.all_tokens()
                wq = [wload(wsrc(mwq_d[i], u * 512, 512), 8, 512) for u in range(2)]
                wo = [wload(wsrc(mwo_d[i], u * 512, 512), 8, 512) for u in range(2)]
                S.barrier(tk_kv)
                ph_kv.close()
                HT = [psb(f"m_ht{k}", [128, NCH, 512], BF16) for k in range(3)]; HTB = [Buf() for _ in range(3)]
                SQ = [psb(f"m_sq{k}", [128, 512], BF16) for k in range(2)] * 2; SQB = [Buf() for _ in range(2)] * 2
                R = psb("m_r", [128, 512]); RB = Buf()
                SQH = [psb(f"m_sqh{k}", [128, 2, 512], BF16) for k in range(2)]; SQHB = [Buf() for _ in range(2)]
                QN = psb("m_qn", [128, NCH, 512], BF16); QNB = [Buf() for _ in range(4)]
                EE = [psb(f"m_e{k}", [128, 2, 512], BF16) for k in range(2)]; EEB = [Buf() for _ in range(2)]
                RH = psb("m_rh", [128, 512]); RHB = Buf()
                RD = psb("m_rd", [128, 512]); RDB = Buf()
                QS = psb("m_qs", [128, NCH, NS], BF16); QSB = Buf()
                OS = psb("m_os", [128, NCH, NS], BF16); OSB = Buf()
                order = [NT - 1] + list(range(NT - 1))
                IDB = cb("ident")
                QT = psb("s_qt", [NS, D], BF16); QTB = Buf()
                KS = [psb(f"s_ks{k}", [128, D], BF16) for k in range(2)]; KSB = [Buf() for _ in range(2)]
                VS = [psb(f"s_vs{k}", [128, 2, D], BF16) for k in range(2)]; VSB = [Buf() for _ in range(2)]
                PR = psb("s_pr", [128, 512]); PRB = Buf()
                SC = psb("s_sc", [128, NS, 2, 4]); SCB = [Buf() for _ in range(NS)]
                SE = psb("s_se", [128, NS, 2, 4], BF16); SEB = [Buf() for _ in range(NS)]
                DN = psb("s_dn", [128, NS * 4]); DNB = [Buf() for _ in range(NS)]
                plan = {}

                def at(step, prio, fn):
                    plan.setdefault(step, []).append((prio, len(plan.get(step, [])), fn))

                def mk_unit(ti, t, h):
                    c0, w = TILES[t]
                    wp = 256 if t == NT - 1 else w
                    n = 4 * ti + h
                    Ht = HT[ti % 3]; HtB = HTB[ti % 3]
                    sq = SQH[n % 2]; sqB = SQHB[n % 2]
                    ee = EE[n % 2]; eeB = EEB[n % 2]
                    u = {}

                    def A1():
                        u["qb"] = []
                        for jj in range(2):
                            co = h * 2 + jj
                            uu, cu = divmod(co, 4)
                            b = balloc()
                            u["qb"].append(b)
                            mm(PS[b][:, :w], [(wq[uu][0][:, k, cu * 128:(cu + 1) * 128], Ht[:, k, :w]) for k in range(NCH)],
                               reads=[wq[uu][1], HtB], wbuf=PSB[b])
                            S.op("act", lambda e, jj=jj, b=b: e.activation(out=sq[:, jj, :w], in_=PS[b][:, :w], func=AF.Square),
                                 reads=[PSB[b]], writes=[sqB])

                    def A2():
                        br = balloc()
                        mm(PS[br][:, :w], [(cb("ones256"), sq[:, jj, :w]) for jj in range(2)], reads=[sqB, CB], wbuf=PSB[br])
                        rstd_act(RH[:, :w], PS[br][:, :w], [PSB[br]], RHB)
                        bfree(br)
                        for jj in range(2):
                            qb = u["qb"][jj]
                            S.op("dve", lambda e, jj=jj, qb=qb: e.scalar_tensor_tensor(
                                out=QN[:, h * 2 + jj, :w], in0=PS[qb][:, :w], scalar=cs("mem_q_norm", i * 8 + h * 2 + jj),
                                in1=RH[:, :w], op0=ALU.mult, op1=ALU.mult), reads=[PSB[qb], RHB, CB], writes=[QNB[h]])
                            bfree(qb)
                        if t == NT - 1:
                            S.op("act", lambda e: e.activation(out=QS[:, h * 2:h * 2 + 2, :], in_=QN[:, h * 2:h * 2 + 2, 256:272], func=AF.Copy),
                                 reads=[QNB[h]], writes=[QSB])

                    def B1():
                        for mc in range(2):
                            b = balloc()
                            mm(PS[b][:, :wp], [(KT[:, h * 2 + jj, mc * 128:(mc + 1) * 128], QN[:, h * 2 + jj, :wp]) for jj in range(2)],
                               reads=[KTB, QNB[h]], wbuf=PSB[b])
                            S.op("act", lambda e, mc=mc, b=b: e.activation(out=ee[:, mc, :wp], in_=PS[b][:, :wp], func=AF.Exp,
                                                                          scale=1.0 / 16), reads=[PSB[b]], writes=[eeB])
                            bfree(b)

                    def B2():
                        bd = balloc()
                        mm(PS[bd][:, :wp], [(cb("ones1"), ee[:, mc, :wp]) for mc in range(2)], reads=[eeB, CB], wbuf=PSB[bd])
                        recip_act(RD[:, :wp], PS[bd][:, :wp], [PSB[bd]], RDB)
                        bfree(bd)
                        for jj in range(2):
                            b = balloc()
                            mm(PS[b][:, :wp], [(VB_[:, mc, h * 256 + jj * 128:h * 256 + (jj + 1) * 128], ee[:, mc, :wp])
                                               for mc in range(2)], reads=[VBB, eeB], wbuf=PSB[b])
                            S.op("dve", lambda e, jj=jj, b=b: e.tensor_tensor(out=Ht[:, h * 2 + jj, :wp], in0=PS[b][:, :wp],
                                                                            in1=RD[:, :wp], op=ALU.mult),
                                 reads=[PSB[b], RDB], writes=[HtB])
                            bfree(b)
                    at(n, 0, A1); at(n + 1, 1, A2); at(n + 2, 2, B1); at(n + 3, 3, B2)

                def mk_tile(ti, t):
                    c0, w = TILES[t]
                    wp = 256 if t == NT - 1 else w
                    Ht = HT[ti % 3]; HtB = HTB[ti % 3]

                    def RMS():
                        rms_tile(t, "norm_mem", i * 8, Ht, HtB, SQ, SQB, R, RB)

                    def WO():
                        for co in range(NCH):
                            uu, cu = divmod(co, 4)
                            b = balloc()
                            mm(PS[b][:, :wp], [(wo[uu][0][:, k, cu * 128:(cu + 1) * 128], Ht[:, k, :wp]) for k in range(NCH)],
                               reads=[wo[uu][1], HtB], wbuf=PSB[b])
                            S.op("dve", lambda e, co=co, b=b: e.tensor_tensor(out=X[:, co, c0:c0 + wp], in0=X[:, co, c0:c0 + wp],
                                                                            in1=PS[b][:, :wp], op=ALU.add),
                                 reads=[PSB[b], XB[t]], writes=[XB[t]])
                            bfree(b)
                    at(max(4 * ti - 4, -1), 5, RMS)
                    at(4 * ti + 7, 4, WO)

                def SETUP():
                    for half in range(2):
                        b = balloc()
                        pb = PS[b][:].bitcast(BF16)
                        for c4 in range(4):
                            c = half * 4 + c4
                            transpose(pb[:NS, c4 * 128:(c4 + 1) * 128], QS[:, c, :], IDB, reads=[QSB, CB], wbuf=PSB[b], inc=(c4 == 3))
                        S.op("act", lambda e, pb=pb, half=half: e.activation(out=QT[:, half * 512:(half + 1) * 512], in_=pb[:NS, 0:512],
                                                                            func=AF.Copy), reads=[PSB[b]], writes=[QTB])
                        bfree(b)

                def mk_sample(s):
                    vs = VS[s % 2]; vsB = VSB[s % 2]

                    def S1():
                        for mc in range(2):
                            S.dma("pool", KS[mc][:], cmk_d[i, s, mc * 128:(mc + 1) * 128, :], writes=[KSB[mc]])
                        S.dma("pool", vs[:], cmv_d[i, s].rearrange("(m p) d -> p m d", p=128), writes=[vsB])
                        qbb = []
                        for half in range(2):
                            b = balloc()
                            qbb.append(b)
                            mm(PS[b][:, :], [(cb("e16", rows=NS, j=s * 128, n=128), QT[:, half * 512:(half + 1) * 512])],
                               reads=[QTB, CB], wbuf=PSB[b])
                        for mc in range(2):
                            for h in range(4):
                                half, hh = divmod(h, 2)
                                S.op("dve", lambda e, mc=mc, h=h, half=half, hh=hh: e.scalar_tensor_tensor(
                                    out=PR[:, hh * 256:(hh + 1) * 256], in0=KS[mc][:, h * 256:(h + 1) * 256], scalar=1.0,
                                    in1=PS[qbb[half]][:, hh * 256:(hh + 1) * 256], op0=ALU.mult, op1=ALU.mult,
                                    accum_out=SC[:, s, mc, h:h + 1]), reads=[KSB[mc], PSB[qbb[half]]], writes=[PRB, SCB[s]])
                        for b in qbb:
                            bfree(b)

                    def S2a():
                        S.op("act", lambda e: e.activation(out=SE[:, s, :, :], in_=SC[:, s, :, :], func=AF.Exp, scale=1.0 / 16),
                             reads=[SCB[s]], writes=[SEB[s]])

                    def S2b():
                        bo = balloc()
                        for h in range(4):
                            for jj in range(2):
                                mm(PS[bo][:, (h * 2 + jj):(h * 2 + jj) + 1],
                                   [(vs[:, mc, h * 256 + jj * 128:h * 256 + (jj + 1) * 128], SE[:, s, mc, h:h + 1]) for mc in range(2)],
                                   reads=[vsB, SEB[s]], wbuf=PSB[bo], inc_last=(h == 3 and jj == 1))
                        bd = balloc()
                        mm(PS[bd][:, 0:4], [(cb("ones1"), SE[:, s, mc, :]) for mc in range(2)], reads=[SEB[s], CB], wbuf=PSB[bd])
                        S.op("dve", lambda e: e.reciprocal(out=DN[:, s * 4:(s + 1) * 4], in_=PS[bd][:, 0:4]),
                             reads=[PSB[bd]], writes=[DNB[s]])
                        S.op("dve", lambda e: e.tensor_tensor(
                            out=OS[:, :, s].rearrange("p (h j) -> p h j", h=4), in0=PS[bo][:, 0:8].rearrange("p (h j) -> p h j", h=4),
                            in1=DN[:, s * 4:(s + 1) * 4].unsqueeze(2).broadcast_to([128, 4, 2]), op=ALU.mult),
                            reads=[PSB[bo], DNB[s]], writes=[OSB])
                        bfree(bo); bfree(bd)
                    at(6 + s, 8, S1); at(7 + s, 6, S2a); at(8 + s, 7, S2b)

                for ti, t in enumerate(order):
                    mk_tile(ti, t)
                    for h in range(4):
                        mk_unit(ti, t, h)
                at(5, 9, SETUP)
                for s_ in range(NS):
                    mk_sample(s_)
                for step in sorted(plan):
                    for _, _, fn in sorted(plan[step], key=lambda x: (x[0], x[1])):
                        fn()
                for co in range(NCH):
                    u, cu = divmod(co, 4)
                    b = bank()
                    mm(PS[b][:, :NS], [(wo[u][0][:, k, cu * 128:(cu + 1) * 128], OS[:, k, :]) for k in range(NCH)],
                       reads=[wo[u][1], OSB], wbuf=PSB[b])
                    S.op("dve", lambda e, co=co, b=b: e.tensor_tensor(out=X[:, co, TP:T], in0=X[:, co, TP:T],
                                                                    in1=PS[b][:, :NS], op=ALU.add),
                         reads=[PSB[b], XB[NT - 1]], writes=[XB[NT - 1]])
                halo_fix()
                end_phase(nxt_phase)

        def ffn_phase(i, nxt_phase=None):
            with ExitStack() as ph:
                psb = lambda name, shape, dt=F32: ph.enter_context(nc.sbuf_tensor(uniq(name), list(shape), dt))
                H = psb("f_h", [128, NCH, T], BF16); HB = [Buf() for _ in range(NT)]
                GS = psb("f_gs", [128, NHC, NS, 3]); GSB = [Buf() for _ in range(NHC)]
                gsall = Buf()

                def gs_dst(c0, ng):
                    return GS[:, c0:c0 + ng, :, 0:2], gsall
                with ExitStack() as ph2:
                    psb2 = lambda name, shape, dt=F32: ph2.enter_context(nc.sbuf_tensor(uniq(name), list(shape), dt))
                    SQ = [psb2(f"f_sq{k}", [128, 512], BF16) for k in range(4)]; SQB = [Buf() for _ in range(4)]
                    R = psb2("f_r", [128, 512]); RB = Buf()
                    for t in range(NT):
                        rms_tile(t, "norm_ffn", i * 8, H[:, :, TILES[t][0]:TILES[t][0] + TILES[t][1]], HB[t], SQ, SQB, R, RB)
                    load_fm(psb2, stffn_d[i], 2 * NS, FFN, gs_dst, "f_stg", split=(NS, 2))
                    S.barrier()
                for b_ in GSB:
                    b_.w = gsall.w
                G = [psb(f"f_g{k}", [128, 516]) for k in range(4)]; GB = [Buf() for _ in range(4)]
                A = [psb(f"f_a{k}", [128, 512]) for k in range(2)]; AB = [Buf() for _ in range(2)]
                SG = [psb(f"f_sg{k}", [128, 512]) for k in range(2)]; SGB = [Buf() for _ in range(2)]
                HID = [psb(f"f_hid{k}", [128, 4, 512], BF16) for k in range(2)]; HIDB = [Buf() for _ in range(2)]
                HAL = psb("f_hal", [128, NHC, 2]); HALB = [Buf() for _ in range(NHC)]
                GST = psb("f_gst", [128, NHC, 2 * NS + 2]); GSTB = [Buf() for _ in range(NHC)]
                groups = [(0, 4), (4, 4), (8, 4), (12, 4), (16, 4), (20, 2)]

                def load_group(h0, n):
                    if h0 == 0:
                        lds = loaders("ffn", i)
                        return tuple(take(("ffng", i, k), lds[("ffng", i, k)]) for k in range(3))
                    g = wload(wsrc(wup_d[i], h0 * 128, n * 128), 8, n * 128)
                    v = wload(wsrc(wup_d[i], FFN + h0 * 128, n * 128), 8, n * 128)
                    d = wload(wsrc(wdn_d[i], 0, D, r0=h0, kc=n), n, D)
                    return g, v, d
                wdw = lambda k, hc: cs("ffn_wdw", (i * 3 + k) * NHC + hc)
                bdw = lambda hc: cs("ffn_bdw", i * NHC + hc)
                loaded = {0: load_group(*groups[0])}
                cnt_box = [0]

                def ffn_up(gi, t):
                    nonlocal_cnt = cnt_box
                    h0, n = groups[gi]
                    (gw, gwB), (vw, vwB), (dw, dwB) = loaded[gi]
                    c0, w = TILES[t]
                    last = (t == NT - 1)
                    wp = 256 if last else w
                    hid = HID[(gi * NT + t) % 2]; hidB = HIDB[(gi * NT + t) % 2]
                    cnt = cnt_box[0]
                    for j in range(n):
                        hc = h0 + j
                        Gr = G[cnt % 4]; GrB = GB[cnt % 4]
                        A_ = A[cnt % 2]; A_B = AB[cnt % 2]
                        SG_ = SG[cnt % 2]; SG_B = SGB[cnt % 2]
                        cnt += 1; cnt_box[0] = cnt
                        bg = bank()
                        mm(PS[bg][:, :w], [(gw[:, k, j * 128:(j + 1) * 128], H[:, k, c0:c0 + w]) for k in range(NCH)],
                           reads=[gwB, HB[t]], wbuf=PSB[bg])
                        if t == 0:
                            S.op("pool", lambda e, Gr=Gr: e.memset(Gr[:, 0:2], 0.0), writes=[GrB])
                        else:
                            S.op("pool", lambda e, Gr=Gr, hc=hc: e.tensor_copy(out=Gr[:, 0:2], in_=HAL[:, hc, :]),
                                 reads=[HALB[hc]], writes=[GrB])
                        S.op("act", lambda e, Gr=Gr, bg=bg: e.activation(out=Gr[:, 2:2 + wp], in_=PS[bg][:, :wp], func=AF.Copy),
                             reads=[PSB[bg]], writes=[GrB])
                        if not last:
                            S.op("pool", lambda e, Gr=Gr, hc=hc: e.tensor_copy(out=HAL[:, hc, :], in_=Gr[:, wp:wp + 2]),
                                 reads=[GrB], writes=[HALB[hc]])
                        else:
                            S.op("pool", lambda e, Gr=Gr, hc=hc: e.tensor_copy(out=GST[:, hc, 2 * NS:2 * NS + 2], in_=Gr[:, wp:wp + 2]),
                                 reads=[GrB], writes=[GSTB[hc]])
                            S.op("act", lambda e, hc=hc, bg=bg: e.activation(out=GS[:, hc, :, 2], in_=PS[bg][:, 256:272], func=AF.Copy),
                                 reads=[PSB[bg]], writes=[GSB[hc]])
                        bv = bank()
                        mm(PS[bv][:, :w], [(vw[:, k, j * 128:(j + 1) * 128], H[:, k, c0:c0 + w]) for k in range(NCH)],
                           reads=[vwB, HB[t]], wbuf=PSB[bv])
                        S.op("act", lambda e, A_=A_, hc=hc, bg=bg: e.activation(
                            out=A_[:, :wp], in_=PS[bg][:, :wp], func=AF.Identity, bias=bdw(hc), scale=wdw(2, hc)),
                            reads=[PSB[bg], CB], writes=[A_B])
                        S.op("dve", lambda e, Gr=Gr, A_=A_, hc=hc: e.scalar_tensor_tensor(
                            out=A_[:, :wp], in0=Gr[:, 1:1 + wp], scalar=wdw(1, hc), in1=A_[:, :wp], op0=ALU.mult, op1=ALU.add),
                            reads=[GrB, CB, A_B], writes=[A_B])
                        S.op("dve", lambda e, Gr=Gr, A_=A_, hc=hc: e.scalar_tensor_tensor(
                            out=A_[:, :wp], in0=Gr[:, 0:wp], scalar=wdw(0, hc), in1=A_[:, :wp], op0=ALU.mult, op1=ALU.add),
                            reads=[GrB, CB, A_B], writes=[A_B])
                        if last:
                            S.op("dve", lambda e, A_=A_, hc=hc: e.tensor_scalar(
                                out=A_[:, 256:272], in0=GS[:, hc, :, 0], scalar1=wdw(0, hc), scalar2=bdw(hc), op0=ALU.mult, op1=ALU.add),
                                reads=[GSB[hc], CB], writes=[A_B])
                            for k in (1, 2):
                                S.op("dve", lambda e, A_=A_, hc=hc, k=k: e.scalar_tensor_tensor(
                                    out=A_[:, 256:272], in0=GS[:, hc, :, k], scalar=wdw(k, hc), in1=A_[:, 256:272], op0=ALU.mult, op1=ALU.add),
                                    reads=[GSB[hc], CB, A_B], writes=[A_B])
                        S.op("act", lambda e, A_=A_, SG_=SG_: e.activation(out=SG_[:, :w], in_=A_[:, :w], func=AF.Silu),
                             reads=[A_B], writes=[SG_B])
                        S.op("dve", lambda e, SG_=SG_, bv=bv, j=j, hid=hid: e.tensor_tensor(
                            out=hid[:, j, :w], in0=SG_[:, :w], in1=PS[bv][:, :w], op=ALU.mult), reads=[SG_B, PSB[bv]], writes=[hidB])

                def ffn_down(gi, t):
                    h0, n = groups[gi]
                    (gw, gwB), (vw, vwB), (dw, dwB) = loaded[gi]
                    c0, w = TILES[t]
                    hid = HID[(gi * NT + t) % 2]; hidB = HIDB[(gi * NT + t) % 2]
                    for co in range(NCH):
                        bd = bank()
                        mm(PS[bd][:, :w], [(dw[:, j, co * 128:(co + 1) * 128], hid[:, j, :w]) for j in range(n)],
                           reads=[dwB, hidB], wbuf=PSB[bd])
                        resid_add(t, co, bd)

                steps = [(gi, t) for gi in range(len(groups)) for t in range(NT)]
                for k, (gi, t) in enumerate(steps):
                    ffn_up(gi, t)
                    if k > 0:
                        ffn_down(*steps[k - 1])
                    if t == 0 and gi + 1 < len(groups):
                        loaded[gi + 1] = load_group(*groups[gi + 1])
                ffn_down(*steps[-1])
                halo_fix()
                for hc in range(NHC):
                    S.op("pool", lambda e, hc=hc: e.tensor_copy(out=GST[:, hc, 0:2 * NS].rearrange("p (s r) -> p s r", r=2),
                                                                in_=GS[:, hc, :, 1:3]), reads=[GSB[hc]], writes=[GSTB[hc]])
                stg = [psb(f"f_o{k}", [128, 512]) for k in range(2)]
                sbufs = [Buf() for _ in range(2)]
                R_ = 2 * NS + 2
                for gi2, g0 in enumerate(range(0, NHC, 4)):
                    ng = min(4, NHC - g0)
                    b = bank()
                    for jx in range(ng):
                        transpose(PS[b][:R_, jx * 128:(jx + 1) * 128], GST[:, g0 + jx, :], IDF, reads=[GSTB[g0 + jx], CB],
                                  wbuf=PSB[b], inc=(jx == ng - 1))
                    s = gi2 % 2
                    S.op("act", lambda e, s=s, b=b, ng=ng: e.activation(out=stg[s][:R_, :ng * 128], in_=PS[b][:R_, :ng * 128],
                                                                      func=AF.Copy), reads=[PSB[b]], writes=[sbufs[s]])
                    S.dma("sp", o_ffn_s[i][:, g0 * 128:(g0 + ng) * 128], stg[s][0:2 * NS, :ng * 128], reads=[sbufs[s]],
                          writes=[drb("o_ffn_s")])
                    S.dma("sp", o_ffn_p[i][:, g0 * 128:(g0 + ng) * 128], stg[s][2 * NS:R_, :ng * 128], reads=[sbufs[s]],
                          writes=[drb("o_ffn_p")])
                end_phase(nxt_phase)

        def swa_phase(i, nxt_phase=None):
            with ExitStack() as ph:
                psb = lambda name, shape, dt=F32: ph.enter_context(nc.sbuf_tensor(uniq(name), list(shape), dt))
                lds = loaders("swa", i)
                wqkv = [take(("wqkv", u), lds[("wqkv", u)]) for u in range(3)]
                wo = [take(("wo", u), lds[("wo", u)]) for u in range(2)]
                HT = [psb(f"a_ht{k}", [128, NCH, 512], BF16) for k in range(2)]; HTB = [Buf() for _ in range(2)]
                KT = psb("a_kt", [128, 2, T], BF16); KTB = [Buf() for _ in range(NT)]
                VT = psb("a_vt", [128, TP // 128, 256], BF16); VTB = [Buf() for _ in range(NT)]
                QR = psb("a_qr", [128, NCH, 512], BF16); QRB = Buf()
                SINKE = psb("a_sink", [128, 8]); SINKB = Buf()
                VNEW = psb("a_vnew", [NS, 256]); VNEWB = Buf()
                ph1 = ExitStack()
                psb1 = lambda name, shape, dt=F32: ph1.enter_context(nc.sbuf_tensor(uniq(name), list(shape), dt))
                SQ = [psb1(f"a_sq{k}", [128, 512], BF16) for k in range(4)]; SQB = [Buf() for _ in range(4)]
                R = psb1("a_r", [128, 512]); RB = Buf()
                SQC = [psb1(f"a_sqc{k}", [128, 512], BF16) for k in range(2)]; SQCB = [Buf() for _ in range(2)]
                RC = [psb1(f"a_rc{k}", [128, 512]) for k in range(1)]; RCB = [Buf() for _ in range(1)]
                QNC = [psb1(f"a_qnc{k}", [128, 512], BF16) for k in range(2)]; QNCB = [Buf() for _ in range(2)]
                T1 = [psb1(f"a_t1{k}", [128, 512]) for k in range(1)]; T1B = [Buf() for _ in range(1)]
                T2 = [psb1(f"a_t2{k}", [128, 512]) for k in range(1)] * 2; T2B = [Buf()] * 2
                CS_ = [psb1(f"a_cos{k}", [128, 512]) for k in range(1)] * 2; CSB = [Buf()] * 2
                SN_ = [psb1(f"a_sin{k}", [128, 512]) for k in range(1)] * 2; SNB = [Buf()] * 2
                EE = [psb1(f"a_e{k}", [128, 512], BF16) for k in range(8)]; EEB = [Buf() for _ in range(8)]
                DN = [psb1(f"a_dn{k}", [128, 512]) for k in range(1)]; DNB = [Buf() for _ in range(1)]
                S.op("act", lambda e: e.activation(out=SINKE[:], in_=cs("sinks", 0, 8), func=AF.Exp), reads=[CB], writes=[SINKB])
                plan = {}

                def at(step, prio, fn):
                    plan.setdefault(step, []).append((prio, len(plan.get(step, [])), fn))
                ecnt = [0]

                def mk_attn(t, bl, g2, Ht, HtB):
                    bq = TILES[t][0] // 128 + bl
                    kbs = ([bq - 1] if bq > 0 else []) + [bq]
                    u = {"ee": []}

                    def S1():
                        for half in range(2):
                            kb0 = half * 64
                            for ki, kb in enumerate(kbs):
                                tk = min(kb * 128 // 512, NT - 1)
                                bs = balloc()
                                if kb == bq:
                                    mname = "mb_cur"
                                else:
                                    mname = "mb_prev_h" if bq == HALO // 128 else "mb_prev"
                                out3 = PS[bs][:, :].rearrange("p (c q) -> p c q", c=4)
                                S.op("pe", lambda e, out3=out3, kb0=kb0, kb=kb: e.matmul(
                                    out3, lhsT=KT[kb0:kb0 + 64, g2, kb * 128:(kb + 1) * 128],
                                    rhs=QR[kb0:kb0 + 64, 4 * g2:4 * g2 + 4, bl * 128:(bl + 1) * 128], start=True, stop=False),
                                    reads=[KTB[tk], QRB], writes=[PSB[bs]], inc=False)
                                S.op("pe", lambda e, out3=out3, mname=mname: e.matmul(
                                    out3, lhsT=cb("ident"), rhs=cb(mname).unsqueeze(1).broadcast_to([128, 4, 128]), start=False, stop=True),
                                    reads=[CB], writes=[PSB[bs]], inc=True)
                                ee = EE[ecnt[0] % 8]; eeB = EEB[ecnt[0] % 8]
                                ecnt[0] += 1
                                S.op("act", lambda e, ee=ee, bs=bs: e.activation(out=ee[:, :], in_=PS[bs][:, :], func=AF.Exp, scale=0.125),
                                     reads=[PSB[bs]], writes=[eeB])
                                bfree(bs)
                                u["ee"].append((half, ki, kb, tk, ee, eeB))

                    def S2():
                        bo = balloc(); bdn = balloc()
                        nee = len(u["ee"])
                        for idx, (half, ki, kb, tk, ee, eeB) in enumerate(u["ee"]):
                            kh = 2 * g2 + half
                            kb0 = half * 64
                            lastmm = (idx == nee - 1)
                            S.op("pe", lambda e, ee=ee, kb=kb, kh=kh, kb0=kb0, ki=ki: e.matmul(
                                PS[bo][kb0:kb0 + 64, :], lhsT=VT[:, kb, kh * 64:(kh + 1) * 64], rhs=ee[:, :],
                                start=(ki == 0), stop=(ki == len(kbs) - 1)), reads=[VTB[tk], eeB], writes=[PSB[bo]], inc=lastmm)
                            S.op("pe", lambda e, ee=ee, kb0=kb0, ki=ki: e.matmul(
                                PS[bdn][kb0:kb0 + 64, :], lhsT=cb("ones1", j=0, n=64), rhs=ee[:, :],
                                start=(ki == 0), stop=(ki == len(kbs) - 1)), reads=[CB, eeB], writes=[PSB[bdn]], inc=lastmm)
                        dn = DN[0]; dnB = DNB[0]
                        S.op("dve", lambda e: e.tensor_tensor(
                            out=dn[:, :].rearrange("p (c q) -> p c q", c=4), in0=PS[bdn][:, :].rearrange("p (c q) -> p c q", c=4),
                            in1=SINKE[:, 4 * g2:4 * g2 + 4].unsqueeze(2).broadcast_to([128, 4, 128]), op=ALU.add),
                            reads=[PSB[bdn], SINKB], writes=[dnB])
                        recip_act(dn[:, :], dn[:, :], [dnB], dnB)
                        S.op("dve", lambda e: e.tensor_tensor(
                            out=Ht[:, 4 * g2:4 * g2 + 4, bl * 128:(bl + 1) * 128], in0=PS[bo][:, :].rearrange("p (c q) -> p c q", c=4),
                            in1=dn[:, :].rearrange("p (c q) -> p c q", c=4), op=ALU.mult), reads=[PSB[bo], dnB], writes=[HtB])
                        bfree(bo); bfree(bdn)
                    return S1, S2

                def mk_proj(t, qc, Ht, HtB):
                    c0, w = TILES[t]
                    if qc < 8:
                        uu, cu = divmod(qc, 4)
                        gname = "attn_qn"
                    else:
                        uu, cu = 2, qc - 8
                        gname = "attn_kn"
                    k_ = qc % 2
                    u = {}

                    def P1():
                        b = balloc()
                        u["b"] = b
                        mm(PS[b][:, :w], [(wqkv[uu][0][:, k, cu * 128:(cu + 1) * 128], Ht[:, k, :w]) for k in range(NCH)],
                           reads=[wqkv[uu][1], HtB], wbuf=PSB[b])
                        S.op("act", lambda e: e.activation(out=SQC[k_][:, :w], in_=PS[b][:, :w], func=AF.Square),
                             reads=[PSB[b]], writes=[SQCB[k_]])

                    def P2():
                        b = u["b"]
                        br = balloc()
                        mm(PS[br][:, :w], [(cb("bd64"), SQC[k_][:, :w])], reads=[SQCB[k_], CB], wbuf=PSB[br])
                        rstd_act(RC[0][:, :w], PS[br][:, :w], [PSB[br]], RCB[0])
                        bfree(br)
                        S.op("dve", lambda e: e.scalar_tensor_tensor(
                            out=QNC[k_][:, :w], in0=PS[b][:, :w], scalar=cs(gname), in1=RC[0][:, :w], op0=ALU.mult, op1=ALU.mult),
                            reads=[PSB[b], RCB[0], CB], writes=[QNCB[k_]])
                        bfree(b)

                    def P3():
                        bt = balloc()
                        mm(PS[bt][:, :w], [(cb("rot"), QNC[k_][:, :w])], reads=[QNCB[k_], CB], wbuf=PSB[bt])
                        S.op("dve", lambda e: e.tensor_tensor(out=T1[0][:, :w], in0=QNC[k_][:, :w], in1=CS_[0][:, :w], op=ALU.mult),
                             reads=[QNCB[k_], CSB[0]], writes=[T1B[0]])
                        S.op("dve", lambda e: e.tensor_tensor(out=T2[0][:, :w], in0=PS[bt][:, :w], in1=SN_[0][:, :w], op=ALU.mult),
                             reads=[PSB[bt], SNB[0]], writes=[T2B[0]])
                        bfree(bt)
                        if qc < 8:
                            S.op("dve", lambda e: e.tensor_tensor(out=QR[:, qc, :w], in0=T1[0][:, :w], in1=T2[0][:, :w], op=ALU.add),
                                 reads=[T1B[0], T2B[0]], writes=[QRB])
                        else:
                            S.op("dve", lambda e: e.tensor_tensor(out=KT[:, qc - 8, c0:c0 + w], in0=T1[0][:, :w], in1=T2[0][:, :w], op=ALU.add),
                                 reads=[T1B[0], T2B[0]], writes=[KTB[t]])
                    return P1, P2, P3

                def mk_tile(t):
                    c0, w = TILES[t]
                    last = (t == NT - 1)
                    wp = 256 if last else w
                    Ht = HT[t % 2]; HtB = HTB[t % 2]
                    base = 24 * t

                    def RMS():
                        rms_tile(t, "norm_mix", i * 8, Ht, HtB, SQ, SQB, R, RB)

                    def TAB():
                        S.dma("sp", CS_[0][:, :w], cos_d[:, c0:c0 + w], writes=[CSB[0]])
                        S.dma("sp", SN_[0][:, :w], sin_d[:, c0:c0 + w], writes=[SNB[0]])

                    def VP():
                        for bl in range(wp // 128):
                            b = balloc()
                            mm(PS[b][:, 0:256], [(Ht[:, k, bl * 128:(bl + 1) * 128], wqkv[2][0][:, k, 256:512]) for k in range(NCH)],
                               reads=[HtB, wqkv[2][1]], wbuf=PSB[b])
                            S.op("act", lambda e, b=b, bl=bl: e.activation(out=VT[:, c0 // 128 + bl, :], in_=PS[b][:, 0:256], func=AF.Copy),
                                 reads=[PSB[b]], writes=[VTB[t]])
                            bfree(b)
                        if last:
                            b = balloc()
                            mm(PS[b][:NS, 0:256], [(Ht[:, k, 256:272], wqkv[2][0][:, k, 256:512]) for k in range(NCH)],
                               reads=[HtB, wqkv[2][1]], wbuf=PSB[b])
                            S.op("act", lambda e, b=b: e.activation(out=VNEW[:, :], in_=PS[b][:NS, 0:256], func=AF.Copy),
                                 reads=[PSB[b]], writes=[VNEWB])
                            bfree(b)

                    def WO():
                        for co in range(NCH):
                            uu, cu = divmod(co, 4)
                            b = balloc()
                            mm(PS[b][:, :wp], [(wo[uu][0][:, k, cu * 128:(cu + 1) * 128], Ht[:, k, :wp]) for k in range(NCH)],
                               reads=[wo[uu][1], HtB], wbuf=PSB[b])
                            S.op("dve", lambda e, co=co, b=b: e.tensor_tensor(out=X[:, co, c0:c0 + wp], in0=X[:, co, c0:c0 + wp],
                                                                            in1=PS[b][:, :wp], op=ALU.add),
                                 reads=[PSB[b], XB[t]], writes=[XB[t]])
                            bfree(b)
                    at(base - 12 if t > 0 else -2, 9, RMS)
                    at(base - 1, 8, TAB)
                    for qc in range(10):
                        P1, P2, P3 = mk_proj(t, qc, Ht, HtB)
                        at(base + qc, 0, P1); at(base + qc + 1, 1, P2); at(base + qc + 2, 2, P3)
                    at(base + 10, 3, VP)
                    units = [(bl, g2) for bl in range(wp // 128) for g2 in range(2)]
                    for n, (bl, g2) in enumerate(units):
                        S1, S2 = mk_attn(t, bl, g2, Ht, HtB)
                        at(base + 12 + n, 4, S1); at(base + 13 + n, 5, S2)
                    at(base + 14 + len(units), 6, WO)

                for t in range(NT):
                    mk_tile(t)
                for step in sorted(plan):
                    for _, _, fn in sorted(plan[step], key=lambda x: (x[0], x[1])):
                        fn()
                halo_fix()
                S.barrier()
                ph1.close()
                Ht = HT[(NT - 1) % 2]; HtB = HTB[(NT - 1) % 2]
                swa_sample(psb, Ht, HtB, QR, QRB, KT, KTB, VT, VTB, VNEW, VNEWB, SINKE, SINKB)
                for co in range(NCH):
                    u, cu = divmod(co, 4)
                    b = bank()
                    mm(PS[b][:, :NS], [(wo[u][0][:, k, cu * 128:(cu + 1) * 128], Ht[:, k, 256:272]) for k in range(NCH)],
                       reads=[wo[u][1], HtB], wbuf=PSB[b])
                    S.op("dve", lambda e, co=co, b=b: e.tensor_tensor(out=X[:, co, TP:T], in0=X[:, co, TP:T],
                                                                    in1=PS[b][:, :NS], op=ALU.add),
                         reads=[PSB[b], XB[NT - 1]], writes=[XB[NT - 1]])
                end_phase(nxt_phase)

        def swa_sample(psb, Ht, HtB, QR, QRB, KT, KTB, VT, VTB, VNEW, VNEWB, SINKE, SINKB):
            IDB = cb("ident")
            KNEW = psb("as_knew", [NS, 256]); KNEWB = Buf()
            OKP = psb("as_okp", [128, 256]); OKPB = Buf()
            OVP = psb("as_ovp", [128, 256]); OVPB = Buf()
            QT = psb("as_qt", [NS, D], BF16); QTB = Buf()
            KC = psb("as_kc", [128, NS, 256], BF16); KCB = Buf()
            VC = psb("as_vc", [128, NS, 256], BF16); VCB = Buf()
            PR = psb("as_pr", [128, D]); PRB = Buf()
            SC = psb("as_sc", [128, NS, 16]); SCB = Buf()
            SE = psb("as_se", [128, NS, 16], BF16); SEB = Buf()
            DNs = psb("as_dn", [128, 8]); DNsB = Buf()
            b = bank()
            pb = PS[b][:].bitcast(BF16)
            for kc in range(2):
                transpose(pb[:NS, kc * 128:(kc + 1) * 128], KT[:, kc, TP:T], IDB, reads=[KTB[NT - 1], CB], wbuf=PSB[b], inc=(kc == 1))
            S.op("act", lambda e: e.activation(out=KNEW[:, :], in_=pb[:NS, 0:256], func=AF.Copy), reads=[PSB[b]], writes=[KNEWB])
            b = bank()
            pb2 = PS[b][:].bitcast(BF16)
            for kc in range(2):
                transpose(pb2[:, kc * 128:(kc + 1) * 128], KT[:, kc, TP - 128:TP], IDB, reads=[KTB[NT - 1], CB], wbuf=PSB[b], inc=(kc == 1))
            S.op("act", lambda e: e.activation(out=OKP[:, :], in_=pb2[:, 0:256], func=AF.Copy), reads=[PSB[b]], writes=[OKPB])
            S.op("act", lambda e: e.activation(out=OVP[:, :], in_=VT[:, TP // 128 - 1, :], func=AF.Copy), reads=[VTB[NT - 1]], writes=[OVPB])
            S.dma("sp", o_wk_p, OKP[:, :], reads=[OKPB], writes=[drb("o_wk_p")])
            S.dma("sp", o_wv_p, OVP[:, :], reads=[OVPB], writes=[drb("o_wv_p")])
            S.dma("sp", o_wk_s[:, 0:127, :], cwk_d[:, 1:128, :], writes=[drb("o_wk_s")])
            S.dma("sp", o_wv_s[:, 0:127, :], cwv_d[:, 1:128, :], writes=[drb("o_wv_s")])
            S.dma("sp", o_wk_s[:, 127, :], KNEW[:, :], reads=[KNEWB], writes=[drb("o_wk_s2")])
            S.dma("sp", o_wv_s[:, 127, :], VNEW[:, :], reads=[VNEWB], writes=[drb("o_wv_s2")])
            S.dma("pool", KC[:], o_wk_s.rearrange("s k d -> k s d"), reads=[drb("o_wk_s"), drb("o_wk_s2")], writes=[KCB])
            S.dma("pool", VC[:], o_wv_s.rearrange("s k d -> k s d"), reads=[drb("o_wv_s"), drb("o_wv_s2")], writes=[VCB])
            for half in range(2):
                b = bank()
                pb = PS[b][:].bitcast(BF16)
                for c4 in range(4):
                    c = half * 4 + c4
                    transpose(pb[:NS, c4 * 128:(c4 + 1) * 128], QR[:, c, 256:272], IDB, reads=[QRB, CB], wbuf=PSB[b], inc=(c4 == 3))
                S.op("act", lambda e, pb=pb, half=half: e.activation(out=QT[:, half * 512:(half + 1) * 512], in_=pb[:NS, 0:512],
                                                                    func=AF.Copy), reads=[PSB[b]], writes=[QTB])
            for s in range(NS):
                for g2 in range(2):
                    b = bank()
                    mm(PS[b][:, :], [(cb("e16", rows=NS, j=s * 128, n=128), QT[:, g2 * 512:(g2 + 1) * 512])], reads=[QTB, CB], wbuf=PSB[b])
                    S.op("dve", lambda e, s=s, g2=g2, b=b: e.tensor_tensor(
                        out=PR[:, g2 * 512:(g2 + 1) * 512].rearrange("p (c x) -> p c x", c=4),
                        in0=KC[:, s, g2 * 128:(g2 + 1) * 128].unsqueeze(1).broadcast_to([128, 4, 128]),
                        in1=PS[b][:, :].rearrange("p (c x) -> p c x", c=4), op=ALU.mult), reads=[KCB, PSB[b]], writes=[PRB])
                S.op("dve", lambda e, s=s: e.tensor_reduce(out=SC[:, s, :], in_=PR[:, :].rearrange("p (h d) -> p h d", d=64),
                                                          axis=AX.X, op=ALU.add), reads=[PRB], writes=[SCB])
            S.op("act", lambda e: e.activation(out=SE[:], in_=SC[:], func=AF.Exp, scale=0.125), reads=[SCB], writes=[SEB])
            for s in range(NS):
                bo = bank(); bdn = bank()
                for g2 in range(2):
                    for half in range(2):
                        kh = 2 * g2 + half
                        kb0 = half * 64
                        rhs = SE[:, s, :].rearrange("p (c h) -> p c h", h=2)[:, 4 * g2:4 * g2 + 4, half]
                        lastmm = (g2 == 1 and half == 1)
                        S.op("pe", lambda e, s=s, kh=kh, kb0=kb0, g2=g2, rhs=rhs: e.matmul(
                            PS[bo][kb0:kb0 + 64, 4 * g2:4 * g2 + 4], lhsT=VC[:, s, kh * 64:(kh + 1) * 64], rhs=rhs, start=True, stop=True),
                            reads=[VCB, SEB], writes=[PSB[bo]], inc=lastmm)
                        S.op("pe", lambda e, kb0=kb0, g2=g2, rhs=rhs: e.matmul(
                            PS[bdn][kb0:kb0 + 64, 4 * g2:4 * g2 + 4], lhsT=cb("ones1", j=0, n=64), rhs=rhs, start=True, stop=True),
                            reads=[CB, SEB], writes=[PSB[bdn]], inc=lastmm)
                S.op("dve", lambda e, bdn=bdn: e.tensor_tensor(out=DNs[:, :], in0=PS[bdn][:, 0:8], in1=SINKE[:, :], op=ALU.add),
                     reads=[PSB[bdn], SINKB], writes=[DNsB])
                S.op("dve", lambda e: e.reciprocal(out=DNs[:, :], in_=DNs[:, :]), reads=[DNsB], writes=[DNsB])
                S.op("dve", lambda e, s=s, bo=bo: e.tensor_tensor(out=Ht[:, :, 256 + s], in0=PS[bo][:, 0:8], in1=DNs[:, :], op=ALU.mult),
                     reads=[PSB[bo], DNsB], writes=[HtB])

        def conv_phase(i, nxt_phase=None):
            with ExitStack() as ph:
                psb = lambda name, shape, dt=F32: ph.enter_context(nc.sbuf_tensor(uniq(name), list(shape), dt))
                lds = loaders("conv", i)
                w1 = [take(("w1", u), lds[("w1", u)]) for u in range(4)]
                w2 = [take(("w2", u), lds[("w2", u)]) for u in range(2)]
                s0 = TILES[0][0]
                CT = [(k * 256, 256) for k in range(8)] + [(2048, 272)]
                CT[0] = (s0, 256 - s0)
                NCT = len(CT)
                W = 272
                GLUA = psb("c_glua", [128, NCH, 30 + T], BF16)
                GB = [[Buf() for _ in range(NCT)] for _ in range(NCH)]
                G0B = Buf()
                TL = psb("c_tl", [128, NCH, 30 + NS]); TLB = Buf()
                phG = ExitStack()
                GLS = phG.enter_context(nc.sbuf_tensor(uniq("c_gls"), [128, NCH, NS, 31], BF16)); GLSB = Buf()
                wo_, _ = coff["conv_wdw"]
                ho, _ = coff["hmask"]
                S.op("pool", lambda e: e.memset(GLUA[:, :, 0:30 + s0], 0.0), writes=[G0B])
                with ExitStack() as phA:
                    psa = lambda name, shape, dt=F32: phA.enter_context(nc.sbuf_tensor(uniq(name), list(shape), dt))
                    HT = [psa(f"c_ht{k}", [128, NCH, W], BF16) for k in range(2)]; HTB = [Buf() for _ in range(2)]
                    SQ = [psa(f"c_sq{k}", [128, W], BF16) for k in range(4)]; SQB = [Buf() for _ in range(4)]
                    R = psa("c_r", [128, W]); RB = Buf()
                    SGm = [psa(f"c_sg{k}", [128, W]) for k in range(2)]; SGmB = [Buf() for _ in range(2)]
                    cstg = (psa("c_stg", [128, 512]), Buf())
                    for q4 in range(4):
                        load_fm(cstg, stconv_d[q4 * 120:(q4 + 1) * 120, :], 120, D,
                                lambda c0, ng, q4=q4: (GLS[:, c0:c0 + ng, q4 * 4:(q4 + 1) * 4, 0:30], GLSB), f"c_stg{q4}", split=(4, 30))

                    def rmsA(ti):
                        c0, w = CT[ti]
                        rms_cols(c0, w, XB[min(c0 // 512, NT - 1)], "norm_mix", i * 8, HT[ti % 2], HTB[ti % 2], SQ, SQB, R, RB)
                    rmsA(0)
                    for ti, (c0, w) in enumerate(CT):
                        last = (ti == NCT - 1)
                        wp = 256 if last else w
                        Ht = HT[ti % 2]; HtB = HTB[ti % 2]
                        if ti + 1 < NCT:
                            rmsA(ti + 1)
                        for c in range(NCH):
                            u, cu = divmod(c, 4)
                            ba = balloc()
                            mm(PS[ba][:, :w], [(w1[u][0][:, k, cu * 128:(cu + 1) * 128], Ht[:, k, :w]) for k in range(NCH)],
                               reads=[w1[u][1], HtB], wbuf=PSB[ba])
                            bb = balloc()
                            mm(PS[bb][:, :w], [(w1[2 + u][0][:, k, cu * 128:(cu + 1) * 128], Ht[:, k, :w]) for k in range(NCH)],
                               reads=[w1[2 + u][1], HtB], wbuf=PSB[bb])
                            sg = SGm[c % 2]; sgB = SGmB[c % 2]
                            S.op("act", lambda e, sg=sg, bb=bb, c=c: e.activation(out=sg[:, :w], in_=PS[bb][:, :w], func=AF.Sigmoid,
                                                                                bias=cs("conv_b1", 8 + c)), reads=[PSB[bb], CB], writes=[sgB])
                            bfree(bb)
                            S.op("dve", lambda e, sg=sg, ba=ba, c=c: e.scalar_tensor_tensor(
                                out=GLUA[:, c, 30 + c0:30 + c0 + wp], in0=PS[ba][:, :wp], scalar=cs("conv_b1", c), in1=sg[:, :wp],
                                op0=ALU.add, op1=ALU.mult), reads=[PSB[ba], sgB, CB], writes=[GB[c][ti]])
                            if last:
                                S.op("dve", lambda e, sg=sg, ba=ba, c=c: e.scalar_tensor_tensor(
                                    out=TL[:, c, 30:30 + NS], in0=PS[ba][:, 256:272], scalar=cs("conv_b1", c), in1=sg[:, 256:272],
                                    op0=ALU.add, op1=ALU.mult), reads=[PSB[ba], sgB, CB], writes=[TLB])
                                S.op("dve", lambda e, c=c: e.tensor_copy(out=GLS[:, c, :, 30], in_=TL[:, c, 30:30 + NS]),
                                     reads=[TLB], writes=[GLSB])
                                S.op("dve", lambda e, c=c: e.tensor_copy(out=TL[:, c, 0:30], in_=GLUA[:, c, TP:TP + 30]),
                                     reads=[GB[c][ti]], writes=[TLB])
                            bfree(ba)
                        if ti == 0:
                            hm = CST[:, ho:ho + HALO].unsqueeze(1).broadcast_to([128, NCH, HALO])
                            S.op("dve", lambda e, hm=hm: e.tensor_tensor(out=GLUA[:, :, 30:30 + HALO], in0=GLUA[:, :, 30:30 + HALO],
                                                                        in1=hm, op=ALU.mult), reads=[GB[c][0] for c in range(NCH)] + [CB, G0B],
                                 writes=[GB[c][0] for c in range(NCH)] + [G0B])
                    S.barrier()
                with ExitStack() as phB:
                    psq = lambda name, shape, dt=F32: phB.enter_context(nc.sbuf_tensor(uniq(name), list(shape), dt))
                    DG = [psq(f"c_dg{k}", [128, 31, 128], BF16) for k in range(2)]; DGB = [Buf() for _ in range(2)]
                    PRs = psq("c_prs", [128, NS, 31]); PRsB = Buf()
                    DS = psq("c_ds", [128, NS]); DSB = Buf()

                    def build(c):
                        wv = CST[:, wo_ + c * 31:wo_ + (c + 1) * 31]
                        S.op("pool", lambda e, wv=wv, c=c: e.tensor_tensor(
                            out=DG[c % 2][:, :, :], in0=cb("ident").unsqueeze(1).broadcast_to([128, 31, 128]),
                            in1=wv.unsqueeze(2).broadcast_to([128, 31, 128]), op=ALU.mult), reads=[CB], writes=[DGB[c % 2]])
                    build(0)
                    for c in range(NCH):
                        if c + 1 < NCH:
                            build(c + 1)
                        wv = CST[:, wo_ + c * 31:wo_ + (c + 1) * 31]
                        S.op("dve", lambda e, c=c, wv=wv: e.tensor_tensor(
                            out=PRs[:, :, :], in0=GLS[:, c, :, :], in1=wv.unsqueeze(1).broadcast_to([128, NS, 31]), op=ALU.mult),
                            reads=[GLSB, CB], writes=[PRsB])
                        S.op("dve", lambda e: e.tensor_reduce(out=DS[:, :], in_=PRs[:, :, :], axis=AX.X, op=ALU.add),
                             reads=[PRsB], writes=[DSB])
                        for tj in range(NCT - 1, -1, -1):
                            c0 = CT[tj][0]
                            bd = balloc()
                            rb = [GB[c][tj]] + ([GB[c][tj - 1]] if tj > 0 else [G0B])
                            wj = 256 if tj == NCT - 1 else CT[tj][1]
                            mm(PS[bd][:, :wj], [(DG[c % 2][:, k, :], GLUA[:, c, c0 + k:c0 + k + wj]) for k in range(31)],
                               reads=[DGB[c % 2]] + rb, wbuf=PSB[bd])
                            S.op("act", lambda e, bd=bd, c=c, c0=c0, wj=wj: e.activation(out=GLUA[:, c, 30 + c0:30 + c0 + wj], in_=PS[bd][:, :wj],
                                                                                func=AF.Identity, bias=cs("conv_bdw", c)),
                                 reads=[PSB[bd], CB], writes=[GB[c][tj]])
                            bfree(bd)
                        S.op("dve", lambda e, c=c: e.tensor_scalar(out=GLUA[:, c, 30 + TP:30 + T], in0=DS[:, :], scalar1=cs("conv_bdw", c),
                                                                   scalar2=None, op0=ALU.add), reads=[DSB, CB], writes=[GB[c][NCT - 1]])
                    S.barrier()
                phG.close()
                with ExitStack() as phC:
                    psc = lambda name, shape, dt=F32: phC.enter_context(nc.sbuf_tensor(uniq(name), list(shape), dt))
                    HT = [psc(f"c_yt{k}", [128, NCH, 512], BF16) for k in range(2)]; HTB = [Buf() for _ in range(2)]
                    SQ2 = psc("c_sq2", [128, NCH, 512], BF16); SQ2B = Buf()
                    MU = psc("c_mu", [128, 512]); MUB = Buf()
                    RS = psc("c_rs", [128, 512]); RSB = Buf()
                    TA = [psc(f"c_ta{k}", [128, 512]) for k in range(2)]; TAB_ = [Buf() for _ in range(2)]
                    gall = [GB[c][tj] for c in range(NCH) for tj in range(NCT)]

                    def lnorm(t):
                        c0, w = TILES[t]
                        Ht = HT[t % 2]; HtB = HTB[t % 2]
                        Dv = GLUA[:, :, 30 + c0:30 + c0 + w]
                        S.op("act", lambda e: e.activation(out=SQ2[:, :, :w], in_=Dv, func=AF.Square), reads=gall, writes=[SQ2B])
                        bmu = balloc()
                        mm(PS[bmu][:, :w], [(cb("ones1024"), GLUA[:, c, 30 + c0:30 + c0 + w]) for c in range(NCH)], reads=gall + [CB], wbuf=PSB[bmu])
                        bms = balloc()
                        mm(PS[bms][:, :w], [(cb("ones1024"), SQ2[:, c, :w]) for c in range(NCH)], reads=[SQ2B, CB], wbuf=PSB[bms])
                        S.op("act", lambda e: e.activation(out=MU[:, :w], in_=PS[bmu][:, :w], func=AF.Copy), reads=[PSB[bmu]], writes=[MUB])
                        bfree(bmu)
                        S.op("dve", lambda e: e.tensor_tensor(out=RS[:, :w], in0=MU[:, :w], in1=MU[:, :w], op=ALU.mult), reads=[MUB], writes=[RSB])
                        S.op("dve", lambda e: e.tensor_tensor(out=RS[:, :w], in0=PS[bms][:, :w], in1=RS[:, :w], op=ALU.subtract),
                             reads=[PSB[bms], RSB], writes=[RSB])
                        bfree(bms)
                        S.op("dve", lambda e: e.tensor_scalar(out=RS[:, :w], in0=RS[:, :w], scalar1=0.0, scalar2=None, op0=ALU.max),
                             reads=[RSB], writes=[RSB])
                        rstd_act(RS[:, :w], RS[:, :w], [RSB], RSB)
                        for c in range(NCH):
                            ta = TA[c % 2]; taB = TAB_[c % 2]
                            S.op("dve", lambda e, ta=ta, c=c: e.tensor_tensor(out=ta[:, :w], in0=GLUA[:, c, 30 + c0:30 + c0 + w], in1=MU[:, :w],
                                                                            op=ALU.subtract), reads=gall + [MUB], writes=[taB])
                            S.op("dve", lambda e, ta=ta: e.tensor_tensor(out=ta[:, :w], in0=ta[:, :w], in1=RS[:, :w], op=ALU.mult),
                                 reads=[taB, RSB], writes=[taB])
                            S.op("act", lambda e, ta=ta, c=c: e.activation(out=Ht[:, c, :w], in_=ta[:, :w], func=AF.Silu,
                                                                          bias=cs("conv_lnb", c), scale=cs("conv_lng", c)),
                                 reads=[taB, CB], writes=[HtB])

                    def pw2(t):
                        c0, w = TILES[t]
                        Ht = HT[t % 2]; HtB = HTB[t % 2]
                        for co in range(NCH):
                            u, cu = divmod(co, 4)
                            b = balloc()
                            mm(PS[b][:, :w], [(w2[u][0][:, k, cu * 128:(cu + 1) * 128], Ht[:, k, :w]) for k in range(NCH)],
                               reads=[w2[u][1], HtB], wbuf=PSB[b])
                            S.op("dve", lambda e, co=co, b=b: e.scalar_tensor_tensor(
                                out=X[:, co, c0:c0 + w], in0=PS[b][:, :w], scalar=cs("conv_b2", co), in1=X[:, co, c0:c0 + w],
                                op0=ALU.add, op1=ALU.add), reads=[PSB[b], XB[t], CB], writes=[XB[t]])
                            bfree(b)
                    lnorm(0)
                    for t in range(NT):
                        if t + 1 < NT:
                            lnorm(t + 1)
                        pw2(t)
                    halo_fix()
                    stg = [psc(f"c_o{k}", [128, 512]) for k in range(2)]
                    sbufs = [Buf() for _ in range(2)]
                    for half in range(2):
                        b = bank()
                        for c4 in range(4):
                            transpose(PS[b][:30 + NS, c4 * 128:(c4 + 1) * 128], TL[:, half * 4 + c4, :], IDF, reads=[TLB, CB], wbuf=PSB[b], inc=(c4 == 3))
                        S.op("act", lambda e, half=half, b=b: e.activation(out=stg[half][:30 + NS, :], in_=PS[b][:30 + NS, :], func=AF.Copy),
                             reads=[PSB[b]], writes=[sbufs[half]])
                        S.dma("sp", o_conv_p[:, half * 512:(half + 1) * 512], stg[half][0:30, :], reads=[sbufs[half]], writes=[drb("o_conv_p")])
                        S.dma("sp", o_conv_s[:, 29, half * 512:(half + 1) * 512], stg[half][30:30 + NS, :], reads=[sbufs[half]],
                              writes=[drb("o_conv_s")])
                    S.dma("sp", o_conv_s[:, 0:29, :], stconv_d.rearrange("(s r) d -> s r d", r=30)[:, 1:30, :], writes=[drb("o_conv_s")])
                    end_phase(nxt_phase)

        S0 = [56, 0, 192, 224]
        S1 = [56, 184, 224, 248]
        for i in range(n_layers):
            kind, j = i % 3, i // 3
            TILES[0] = (S0[i], 512 - S0[i])
            if kind == 0:
                pool_phase(i, j, nxt_phase=("mem", i))
            elif kind == 1:
                swa_phase(i, nxt_phase=("mem", i))
            else:
                conv_phase(i, nxt_phase=("mem", i))
            dump_x()
            if stop_after == (i, 0):
                break
            TILES[0] = (S1[i], 512 - S1[i])
            mem_phase(i, nxt_phase=("ffn", i))
            dump_x()
            if stop_after == (i, 1):
                break
            ffn_phase(i, nxt_phase=(({0: "pool", 1: "swa", 2: "conv"}[(i + 1) % 3], i + 1) if i + 1 < n_layers else None))
            dump_x()
            if stop_after == (i, 2):
                break

        TILES[0] = (0, 512)
        with ExitStack() as ph:
            psb = lambda name, shape, dt=F32: ph.enter_context(nc.sbuf_tensor(uniq(name), list(shape), dt))
            OST = [psb(f"o_st{k}", [128, D]) for k in range(2)]; OSTB = [Buf() for _ in range(2)]
            nblk = MAIN // 128
            for blk in range(nblk + 1):
                s = blk % 2
                rows = 128 if blk < nblk else NS
                col0 = HALO + blk * 128
                t = min(col0 // 512, NT - 1)
                for half in range(2):
                    b = bank()
                    for c4 in range(4):
                        c = half * 4 + c4
                        transpose(PS[b][:rows, c4 * 128:(c4 + 1) * 128], X[:, c, col0:col0 + rows], IDF,
                                  reads=[XB[t], CB], wbuf=PSB[b], inc=(c4 == 3))
                    if half == 0:
                        S.op("act", lambda e, s=s, b=b, rows=rows: e.activation(out=OST[s][:rows, 0:512], in_=PS[b][:rows, :],
                                                                              func=AF.Copy), reads=[PSB[b]], writes=[OSTB[s]])
                    else:
                        S.op("dve", lambda e, s=s, b=b, rows=rows: e.tensor_copy(out=OST[s][:rows, 512:1024], in_=PS[b][:rows, :]),
                             reads=[PSB[b]], writes=[OSTB[s]])
                dst = y_p[blk * 128:(blk + 1) * 128, :] if blk < nblk else y_s
                S.dma("sp", dst, OST[s][:rows, :], reads=[OSTB[s]], writes=[drb("y")])
        S.finish()
        nc._n_ins = S.n_ins
        nc._cnts = {k: (v.cnt, list(v.dcnt)) for k, v in S.E.items()}
    return nc


def make_in_maps(inp, cores=range(8)):
    f32 = lambda a: np.ascontiguousarray(np.asarray(a, dtype=np.float32))
    maps = []
    meta = None
    perm = []
    for (ha, hb) in _qperm():
        perm += list(range(ha * 64, ha * 64 + 64)) + list(range(hb * 64, hb * 64 + 64))
    perm = np.array(perm)
    wqkv = f32(inp["attn_w_qkv"])[0]
    wqkv_p = np.ascontiguousarray(np.concatenate([wqkv[:, :1024][:, perm], wqkv[:, 1024:]], axis=1))
    wo_p = np.ascontiguousarray(f32(inp["attn_w_o"])[0][perm, :])
    shared = {
        "pool_w": f32(inp["pool_w"]),
        "attn_w_qkv": wqkv_p,
        "attn_w_o": wo_p,
        "conv_w_pw1": f32(inp["conv_w_pw1"])[0],
        "conv_w_pw2": f32(inp["conv_w_pw2"])[0],
        "mem_w_q": f32(inp["mem_w_q"]),
        "mem_w_kv": f32(inp["mem_w_kv"]),
        "mem_w_o": f32(inp["mem_w_o"]),
        "mem_k_norm_b": np.ascontiguousarray(np.tile(f32(inp["mem_k_norm"]).reshape(1, -1), (128, 1))),
        "norm_src_b": np.ascontiguousarray(np.tile(f32(inp["norm_src"]).reshape(1, -1), (128, 1))),
        "ffn_w_up": f32(inp["ffn_w_up"]),
        "ffn_w_down": f32(inp["ffn_w_down"]),
    }
    xp_all = f32(inp["x_prompt"])
    for core in cores:
        n, q = divmod(core, 4)
        coff, cst, boff, cbf, cos, sin = build_consts(inp, core)
        meta = (coff, cst.shape[1], boff, cbf.shape[1])
        xp = np.zeros((TP, D), np.float32)
        s0 = q * MAIN - HALO
        if q == 0:
            xp[HALO:] = xp_all[n, 0:MAIN]
        else:
            xp[:] = xp_all[n, s0:s0 + TP]
        sl = slice(core * NS, (core + 1) * NS)
        m = dict(shared)
        m.update({
            "xp": xp,
            "xs": f32(inp["x_sample"])[sl, 0, :],
            "memp": f32(inp["mem_prompt"])[n],
            "st_pool": np.ascontiguousarray(f32(inp["state_pool"])[:, sl].reshape(2, NS * 15, D)),
            "st_conv": np.ascontiguousarray(f32(inp["state_conv"])[0, sl].reshape(NS * 30, D)),
            "st_ffn": np.ascontiguousarray(f32(inp["state_ffn"])[:, sl].reshape(DEPTH, NS * 2, FFN)),
            "cwk": np.ascontiguousarray(f32(inp["cache_win_k"])[0, sl].reshape(NS, 128, 256)),
            "cwv": np.ascontiguousarray(f32(inp["cache_win_v"])[0, sl].reshape(NS, 128, 256)),
            "cmk": np.ascontiguousarray(f32(inp["cache_mem_k"])[:, sl].reshape(DEPTH, NS, 256, D)),
            "cmv": np.ascontiguousarray(f32(inp["cache_mem_v"])[:, sl].reshape(DEPTH, NS, 256, D)),
            "cst": cst, "cbf": cbf, "cos": cos, "sin": sin,
        })
        maps.append(m)
    return maps, meta


def assemble(results):
    R = results
    y_p = np.stack([np.concatenate([R[n * 4 + q]["y_p"] for q in range(4)], 0) for n in range(2)])
    y_s = np.concatenate([R[c]["y_s"] for c in range(8)], 0)[:, None, :]
    pool_p = np.stack([R[n * 4 + 3]["o_pool_p"] for n in range(2)], 1)
    pool_s = np.concatenate([R[c]["o_pool_s"] for c in range(8)], 1)
    wk_p = np.stack([R[n * 4 + 3]["o_wk_p"].reshape(128, 4, 64) for n in range(2)])[None]
    wv_p = np.stack([R[n * 4 + 3]["o_wv_p"].reshape(128, 4, 64) for n in range(2)])[None]
    wk_s = np.concatenate([R[c]["o_wk_s"].reshape(NS, 128, 4, 64) for c in range(8)], 0)[None]
    wv_s = np.concatenate([R[c]["o_wv_s"].reshape(NS, 128, 4, 64) for c in range(8)], 0)[None]
    conv_p = np.stack([R[n * 4 + 3]["o_conv_p"] for n in range(2)])[None]
    conv_s = np.concatenate([R[c]["o_conv_s"] for c in range(8)], 0)[None]
    ffn_p = np.stack([R[n * 4 + 3]["o_ffn_p"] for n in range(2)], 1)
    ffn_s = np.concatenate([R[c]["o_ffn_s"].reshape(DEPTH, NS, 2, FFN) for c in range(8)], 1)
    mk = np.stack([R[n * 4]["o_mk"].reshape(DEPTH, 256, 4, 256) for n in range(2)], 1)
    mv = np.stack([R[n * 4]["o_mv"].reshape(DEPTH, 256, 4, 256) for n in range(2)], 1)
    outs = (y_p, y_s, pool_p, pool_s, wk_p, wv_p, wk_s, wv_s, conv_p, conv_s, ffn_p, ffn_s, mk, mv)
    return tuple(np.ascontiguousarray(o, dtype=np.float32) for o in outs)


def kernel(**inputs):
    maps, (coff, ncst, boff, nbf) = make_in_maps(inputs)
    nc = build_program(coff, ncst, boff, nbf)
    res = run_bass_kernel_spmd(nc, maps, core_ids=list(range(8)))
    return assemble(res.results)
```
